# Optimizing a Trainium2 kernel written in Bass

```python
import math
import jax, jax.numpy as jnp
from jax import lax
import numpy as np

D_MODEL = 1024
BATCH = 2
SEQ = 8192
DEPTH = 4

N_MIXERS = 3
N_CONV_LAYERS = (DEPTH + 2) // 3
N_DIL_LAYERS = (DEPTH + 1) // 3
N_RWKV_LAYERS = DEPTH // 3

CONV_WIDTH = 31
DIL_GROUPS = ((128, 1), (512, 4), (2048, 16))
DIL_HEADS = 16
DIL_HEAD_DIM = D_MODEL // DIL_HEADS
REL_BUCKETS = 32
REL_MAX_DIST = 2048
RWKV_HEAD = 64
RWKV_HEADS = D_MODEL // RWKV_HEAD
DECAY_LORA = 64
ICLR_LORA = 64
GATE_LORA = 128
RWKV_GN_EPS = 64e-5
MEM_LEN = 256
XATTN_HEADS = 4
XATTN_HEAD_DIM = D_MODEL // XATTN_HEADS
D_FF = 4 * D_MODEL
DEEPNORM_ALPHA = (2 * DEPTH) ** 0.25
DEEPNORM_BETA = (8 * DEPTH) ** -0.25
LN_EPS = 1e-5

kernel_name = "hybrid_conv_dilattn_rwkv7_deepnorm"

F32 = jnp.float32


def layer_norm(x, g, b, eps=LN_EPS):
    xf = x.astype(F32)
    mu = jnp.mean(xf, axis=-1, keepdims=True)
    var = jnp.mean(jnp.square(xf - mu), axis=-1, keepdims=True)
    return ((xf - mu) * lax.rsqrt(var + eps) * g + b).astype(x.dtype)


def conv_module(x, w_in, b_in, dw, dw_b, ln_g, ln_b, w_out, b_out):
    h = x @ w_in + b_in
    val, gate = jnp.split(h, 2, axis=-1)
    h = val * jax.nn.sigmoid(gate)
    h = lax.conv_general_dilated(
        h, dw[:, None, :].astype(h.dtype), window_strides=(1,),
        padding=[(CONV_WIDTH - 1, 0)],
        dimension_numbers=('NWC', 'WIO', 'NWC'),
        feature_group_count=D_MODEL) + dw_b
    h = jax.nn.silu(layer_norm(h, ln_g, ln_b))
    return h @ w_out + b_out


def t5_causal_bucket(dist):
    n = jnp.maximum(dist, 0)
    max_exact = REL_BUCKETS // 2
    nf = jnp.maximum(n, 1).astype(F32)
    large = max_exact + (jnp.log(nf / max_exact) / math.log(REL_MAX_DIST / max_exact)
                         * (REL_BUCKETS - max_exact)).astype(jnp.int32)
    large = jnp.minimum(large, REL_BUCKETS - 1)
    return jnp.where(n < max_exact, n, large)


def dilated_group_attention(q, k, v, window, dilation, rel_bias):
    b, s, h, e = q.shape
    blk = window // dilation
    sub_len = s // dilation
    nb = -(-sub_len // blk)
    lp = nb * blk

    def to_blocks(t):
        t = t.reshape(b, sub_len, dilation, h, e).transpose(0, 2, 1, 3, 4)
        t = jnp.pad(t, ((0, 0), (0, 0), (0, lp - sub_len), (0, 0), (0, 0)))
        return t.reshape(b, dilation, nb, blk, h, e)

    def with_prev(t):
        prev = jnp.pad(t[:, :, :-1], ((0, 0), (0, 0), (1, 0), (0, 0), (0, 0), (0, 0)))
        return jnp.concatenate([prev, t], axis=3)

    qb = to_blocks(q)
    kw = with_prev(to_blocks(k))
    vw = with_prev(to_blocks(v))
    logits = jnp.einsum('bdnqhe,bdnkhe->bdnhqk', qb, kw,
                        preferred_element_type=F32) * (e ** -0.5)
    qi = jnp.arange(blk)[:, None]
    kj = jnp.arange(2 * blk)[None, :]
    rel = qi + blk - kj
    key_pos = jnp.arange(nb)[:, None, None] * blk + kj[None] - blk
    valid = ((rel >= 0) & (rel <= blk))[None] & (key_pos >= 0)
    bias = jnp.transpose(rel_bias[t5_causal_bucket(rel * dilation)], (2, 0, 1)).astype(F32)
    logits = jnp.where(valid[None, None, :, None], logits + bias[None, None, None], -jnp.inf)
    m = jnp.max(logits, axis=-1, keepdims=True)
    p = jnp.exp(logits - m)
    den = jnp.sum(p, axis=-1, keepdims=True)
    o = jnp.einsum('bdnhqk,bdnkhe->bdnqhe', p / den, vw.astype(F32))
    lse = (m + jnp.log(den))[..., 0]
    o = o.reshape(b, dilation, lp, h, e)[:, :, :sub_len].transpose(0, 2, 1, 3, 4).reshape(b, s, h, e)
    lse = jnp.swapaxes(lse, -1, -2).reshape(b, dilation, lp, h)[:, :, :sub_len]
    lse = lse.transpose(0, 2, 1, 3).reshape(b, s, h)
    return o, lse


def dilated_attention(x, w_qkv, w_out, rel_bias):
    b, s, _ = x.shape
    n_groups = len(DIL_GROUPS)
    qkv = (x @ w_qkv).reshape(b, s, n_groups, 3, DIL_HEADS, DIL_HEAD_DIM)
    outs, lses = [], []
    for g, (win, dil) in enumerate(DIL_GROUPS):
        o, lse = dilated_group_attention(qkv[:, :, g, 0], qkv[:, :, g, 1], qkv[:, :, g, 2],
                                         win, dil, rel_bias)
        outs.append(o)
        lses.append(lse)
    wts = jax.nn.softmax(jnp.stack(lses), axis=0)
    o = jnp.einsum('gbsh,gbshe->bshe', wts, jnp.stack(outs))
    return o.reshape(b, s, DIL_HEADS * DIL_HEAD_DIM).astype(x.dtype) @ w_out


def rwkv7_time_mix(x, mu, w_rkv, w0, w1, w2, a0, a1, a2, g1, g2, k_k, k_a, r_k,
                   lnx_g, lnx_b, w_out):
    b, t, c = x.shape
    h, n = RWKV_HEADS, RWKV_HEAD
    xx = jnp.pad(x, ((0, 0), (1, 0), (0, 0)))[:, :-1] - x
    xs = x[None] + xx[None] * mu[:, None, None, :]
    rkv = jnp.einsum('sbtc,scd->sbtd', xs[:3], w_rkv)
    r, k, v = rkv[0], rkv[1], rkv[2]
    xw, xa, xg = xs[3], xs[4], xs[5]
    w_log = -jax.nn.softplus(-(w0 + jnp.tanh(xw @ w1) @ w2)) - 0.5
    decay = jnp.exp(-jnp.exp(w_log.astype(F32)))
    a = jax.nn.sigmoid(a0 + (xa @ a1) @ a2)
    g = jax.nn.sigmoid(xg @ g1) @ g2

    def heads(z):
        return z.astype(F32).reshape(b, t, h, n)

    kk = heads(k * k_k)
    kk = kk / jnp.maximum(jnp.sqrt(jnp.sum(jnp.square(kk), axis=-1, keepdims=True)), 1e-12)
    k = k * (1 + (a - 1) * k_a)
    r_h, k_h, v_h, a_h = heads(r), heads(k), heads(v), heads(a)
    w_h = decay.reshape(b, t, h, n)

    def step(state, inp):
        r_t, w_t, k_t, v_t, kk_t, a_t = inp
        sa = jnp.einsum('bhvk,bhk->bhv', state, -kk_t)
        state = (state * w_t[:, :, None, :]
                 + sa[..., None] * (kk_t * a_t)[:, :, None, :]
                 + v_t[..., None] * k_t[:, :, None, :])
        return state, jnp.einsum('bhvk,bhk->bhv', state, r_t)

    def seq_first(z):
        return jnp.swapaxes(z, 0, 1)

    state0 = jnp.zeros((b, h, n, n), F32)
    _, y = lax.scan(step, state0, (seq_first(r_h), seq_first(w_h), seq_first(k_h),
                                   seq_first(v_h), seq_first(kk), seq_first(a_h)))
    y = jnp.swapaxes(y, 0, 1)
    y_mu = jnp.mean(y, axis=-1, keepdims=True)
    y_var = jnp.mean(jnp.square(y - y_mu), axis=-1, keepdims=True)
    y = ((y - y_mu) * lax.rsqrt(y_var + RWKV_GN_EPS)).reshape(b, t, c) * lnx_g + lnx_b
    bonus = jnp.sum(r_h * k_h * r_k.astype(F32), axis=-1, keepdims=True) * v_h
    y = y + bonus.reshape(b, t, c)
    return (y * g).astype(x.dtype) @ w_out


def memory_cross_attention(x, mem, w_q, w_kv, w_out):
    b, s, _ = x.shape
    q = (x @ w_q).reshape(b, s, XATTN_HEADS, XATTN_HEAD_DIM)
    kv = (mem @ w_kv).reshape(b, mem.shape[1], 2, XATTN_HEADS, XATTN_HEAD_DIM)
    logits = jnp.einsum('bshe,bmhe->bhsm', q, kv[:, :, 0],
                        preferred_element_type=F32) * (XATTN_HEAD_DIM ** -0.5)
    p = jax.nn.softmax(logits, axis=-1)
    o = jnp.einsum('bhsm,bmhe->bshe', p, kv[:, :, 1].astype(F32))
    return o.reshape(b, s, D_MODEL).astype(x.dtype) @ w_out


def sq_relu_mlp(x, w1, w2):
    return jnp.square(jax.nn.relu(x @ w1)) @ w2


def setup_inputs(seed: int = 0) -> dict:
    key = jax.random.key(seed)
    ks = iter(jax.random.split(key, 40))

    def normal(shape, scale):
        return jax.random.normal(next(ks), shape, F32) * scale

    D = D_MODEL
    nA, nB, nC = N_CONV_LAYERS, N_DIL_LAYERS, N_RWKV_LAYERS
    qkv_cols = len(DIL_GROUPS) * 3 * DIL_HEADS * DIL_HEAD_DIM
    inp = {}
    inp["x"] = normal((BATCH, SEQ, D), 1.0)
    inp["mem"] = normal((BATCH, MEM_LEN, D), 1.0)
    inp["rel_bias"] = normal((REL_BUCKETS, DIL_HEADS), 0.5)
    inp["a_w_in"] = normal((nA, D, 2 * D), D ** -0.5)
    inp["a_b_in"] = normal((nA, 2 * D), 0.02)
    inp["a_dw"] = normal((nA, CONV_WIDTH, D), CONV_WIDTH ** -0.5)
    inp["a_dw_b"] = normal((nA, D), 0.02)
    inp["a_ln_g"] = 1.0 + normal((nA, D), 0.02)
    inp["a_ln_b"] = normal((nA, D), 0.02)
    inp["a_w_out"] = normal((nA, D, D), D ** -0.5 * DEEPNORM_BETA)
    inp["a_b_out"] = normal((nA, D), 0.02)
    inp["b_w_qkv"] = normal((nB, D, qkv_cols), D ** -0.5)
    inp["b_w_out"] = normal((nB, DIL_HEADS * DIL_HEAD_DIM, D),
                            (DIL_HEADS * DIL_HEAD_DIM) ** -0.5 * DEEPNORM_BETA)
    inp["c_mu"] = jax.random.uniform(next(ks), (nC, 6, D), F32)
    inp["c_w_rkv"] = normal((nC, 3, D, D), D ** -0.5)
    inp["c_w0"] = jnp.linspace(-6.5, -1.5, D, dtype=F32)[None] + normal((nC, D), 0.1)
    inp["c_w1"] = normal((nC, D, DECAY_LORA), D ** -0.5)
    inp["c_w2"] = normal((nC, DECAY_LORA, D), DECAY_LORA ** -0.5 * 0.1)
    inp["c_a0"] = normal((nC, D), 0.1)
    inp["c_a1"] = normal((nC, D, ICLR_LORA), D ** -0.5)
    inp["c_a2"] = normal((nC, ICLR_LORA, D), ICLR_LORA ** -0.5 * 0.1)
    inp["c_g1"] = normal((nC, D, GATE_LORA), D ** -0.5)
    inp["c_g2"] = normal((nC, GATE_LORA, D), GATE_LORA ** -0.5)
    inp["c_k_k"] = 0.85 + normal((nC, D), 0.02)
    inp["c_k_a"] = 1.0 + normal((nC, D), 0.02)
    inp["c_r_k"] = normal((nC, RWKV_HEADS, RWKV_HEAD), 0.1)
    inp["c_lnx_g"] = 1.0 + normal((nC, D), 0.02)
    inp["c_lnx_b"] = normal((nC, D), 0.02)
    inp["c_w_out"] = normal((nC, D, D), D ** -0.5 * DEEPNORM_BETA)
    inp["x_w_q"] = normal((DEPTH, D, D), D ** -0.5)
    inp["x_w_kv"] = normal((DEPTH, D, 2 * D), D ** -0.5)
    inp["x_w_out"] = normal((DEPTH, D, D), D ** -0.5 * DEEPNORM_BETA)
    inp["m_w1"] = normal((DEPTH, D, D_FF), D ** -0.5)
    inp["m_w2"] = normal((DEPTH, D_FF, D), D_FF ** -0.5 * DEEPNORM_BETA)
    inp["ln_g"] = 1.0 + normal((DEPTH, 3, D), 0.02)
    inp["ln_b"] = normal((DEPTH, 3, D), 0.02)
    return inp


def reference(x, mem, rel_bias,
              a_w_in, a_b_in, a_dw, a_dw_b, a_ln_g, a_ln_b, a_w_out, a_b_out,
              b_w_qkv, b_w_out,
              c_mu, c_w_rkv, c_w0, c_w1, c_w2, c_a0, c_a1, c_a2, c_g1, c_g2,
              c_k_k, c_k_a, c_r_k, c_lnx_g, c_lnx_b, c_w_out,
              x_w_q, x_w_kv, x_w_out, m_w1, m_w2, ln_g, ln_b):
    for i in range(DEPTH):
        kind, j = i % N_MIXERS, i // N_MIXERS
        if kind == 0:
            h = conv_module(x, a_w_in[j], a_b_in[j], a_dw[j], a_dw_b[j], a_ln_g[j], a_ln_b[j],
                            a_w_out[j], a_b_out[j])
        elif kind == 1:
            h = dilated_attention(x, b_w_qkv[j], b_w_out[j], rel_bias)
        else:
            h = rwkv7_time_mix(x, c_mu[j], c_w_rkv[j], c_w0[j], c_w1[j], c_w2[j], c_a0[j],
                               c_a1[j], c_a2[j], c_g1[j], c_g2[j], c_k_k[j], c_k_a[j], c_r_k[j],
                               c_lnx_g[j], c_lnx_b[j], c_w_out[j])
        x = layer_norm(DEEPNORM_ALPHA * x + h, ln_g[i, 0], ln_b[i, 0])
        h = memory_cross_attention(x, mem, x_w_q[i], x_w_kv[i], x_w_out[i])
        x = layer_norm(DEEPNORM_ALPHA * x + h, ln_g[i, 1], ln_b[i, 1])
        h = sq_relu_mlp(x, m_w1[i], m_w2[i])
        x = layer_norm(DEEPNORM_ALPHA * x + h, ln_g[i, 2], ln_b[i, 2])
    return x
```

```python
import concourse.bass as bass
import concourse.mybir as mybir

ENG = ("pe", "act", "dve", "pool", "sp")


class Buf:
    __slots__ = ("name", "w", "r")

    def __init__(self, name):
        self.name = name
        self.w = None
        self.r = []


class Sched:
    def __init__(self, nc):
        self.nc = nc
        self.streams = {e: [] for e in ENG}
        self.cnt = {e: 0 for e in ENG}
        self.known = {e: {} for e in ENG}
        self.pools = {"hw": ["dmah%d" % i for i in range(24)], "sw": ["dmas%d" % i for i in range(12)], "cc": ["dmac%d" % i for i in range(4)]}
        self.pool_rr = {"hw": 0, "sw": 0, "cc": 0}
        self.dma_use = {k: 0 for p in self.pools.values() for k in p}
        self.sems = {}
        self.out_tokens = []
        self.same_engine_sync = True

    def _need(self, eng, tok):
        if tok is None:
            return
        key, val, src = tok
        if src == eng and not (self.same_engine_sync and eng != "pe"):
            return
        if self.known[eng].get(key, 0) >= val:
            return
        self.known[eng][key] = val
        self.streams[eng].append(("wait", key, val))

    def _deps(self, eng, reads, writes):
        for b in reads:
            self._need(eng, b.w)
        for b in writes:
            self._need(eng, b.w)
            for t in b.r:
                self._need(eng, t)

    def _mark(self, tok, reads, writes):
        for b in reads:
            b.r.append(tok)
        for b in writes:
            b.w = tok
            b.r = []

    def op(self, eng, fn, reads=(), writes=()):
        self._deps(eng, reads, writes)
        self.cnt[eng] += 1
        tok = (eng, self.cnt[eng], eng)
        self.streams[eng].append(("op", fn, (eng, 1)))
        self._mark(tok, reads, writes)
        return tok

    def dma(self, eng, fn, reads=(), writes=(), is_output=False, inc=16):
        self._deps(eng, reads, writes)
        kind = "cc" if inc == 1 else ("sw" if eng == "pool" else "hw")
        pool = self.pools[kind]
        key = pool[self.pool_rr[kind]]
        self.pool_rr[kind] = (self.pool_rr[kind] + 1) % len(pool)
        i = key
        prev = self.dma_use[i]
        if prev:
            if self.known[eng].get(key, 0) < prev:
                self.known[eng][key] = prev
                self.streams[eng].append(("wait", key, prev))
        self.dma_use[i] = prev + inc
        tok = (key, prev + inc, None)
        self.streams[eng].append(("op", fn, (key, inc)))
        self._mark(tok, reads, writes)
        if is_output:
            self.out_tokens.append(tok)
        return tok

    def barrier(self):
        toks = [(e, self.cnt[e], e) for e in ENG if self.cnt[e]]
        toks += [(k, v, None) for k, v in self.dma_use.items() if v]
        for e in ENG:
            for t in toks:
                self._need(e, t)

    def finish(self, eng="sp"):
        for tok in self.out_tokens:
            self._need(eng, tok)

    def emit(self, stack):
        nc = self.nc
        keys = list(ENG) + [k for p in self.pools.values() for k in p]
        for k in keys:
            self.sems[k] = stack.enter_context(nc.semaphore("s_" + k))
        block = stack.enter_context(nc.Block())
        sems = self.sems

        def replay(stream):
            def run(e):
                for item in stream:
                    if item[0] == "wait":
                        e.wait_ge(sems[item[1]], item[2])
                    else:
                        ins = item[1](e)
                        k, amt = item[2]
                        ins.then_inc(sems[k], amt)
            return run

        if self.streams["sp"]:
            block.sync(replay(self.streams["sp"]))
        if self.streams["pe"]:
            block.tensor(replay(self.streams["pe"]))
        if self.streams["act"]:
            block.scalar(replay(self.streams["act"]))
        if self.streams["dve"]:
            block.vector(replay(self.streams["dve"]))
        if self.streams["pool"]:
            block.gpsimd(replay(self.streams["pool"]))


from contextlib import ExitStack
import numpy as np
import concourse.bass as bass
import concourse.mybir as mybir

F32 = mybir.dt.float32
BF16 = mybir.dt.bfloat16
AF = mybir.ActivationFunctionType
ALU = mybir.AluOpType

NT = 2048
TT = 512
NTT = NT // TT
KC = 8
HALO = 32
ALPHA = float((2 * 4) ** 0.25)
LN_EPS = 1e-5
WB = 256
CONVW = 31


class Ten:
    def __init__(self, ap, name, C, N, tw=TT):
        self.ap = ap
        self.C, self.N, self.tw = C, N, tw
        self.nt = (N + tw - 1) // tw
        self.bufs = [[Buf("%s_%d_%d" % (name, c, t)) for t in range(self.nt)] for c in range(C)]

    def b(self, c=None, t0=0, t1=None):
        t1 = self.N if t1 is None else t1
        cs = range(self.C) if c is None else ([c] if isinstance(c, int) else c)
        out = []
        for cc in cs:
            for t in range(t0 // self.tw, (t1 - 1) // self.tw + 1):
                out.append(self.bufs[cc][t])
        return out


class Ctx:
    def __init__(self, nc, stack, arena_words):
        self.nc = nc
        self.S = Sched(nc)
        self.stack = stack
        self.arena = stack.enter_context(nc.sbuf_tensor("arena", [128, arena_words], F32))
        self.off = 0
        self.ps = []
        for i in range(8):
            t = stack.enter_context(nc.psum_tensor("ps%d" % i, [128, 512], F32))
            self.ps.append((t, Buf("ps%d" % i)))
        self.ps_rr = 0
        self.rot = {}

    def alloc(self, words):
        o = self.off
        self.off += words
        assert self.off <= self.arena.shape[1], (self.off, self.arena.shape)
        return o

    def f32(self, off, C, N):
        return self.arena[:, off:off + C * N].rearrange("p (c t) -> p c t", c=C)

    def bf(self, off, C, N):
        assert (C * N) % 2 == 0
        return self.arena[:, off:off + C * N // 2].bitcast(BF16).rearrange("p (c t) -> p c t", c=C)

    def psum(self):
        t, b = self.ps[self.ps_rr]
        self.ps_rr = (self.ps_rr + 1) % 8
        return t, b

    def make_rot(self, name, n, words, mk):
        lst = []
        for i in range(n):
            o = self.alloc(words)
            lst.append((mk(o), Buf("%s%d" % (name, i))))
        self.rot[name] = [lst, 0]

    def get(self, name):
        r = self.rot[name]
        x = r[0][r[1]]
        r[1] = (r[1] + 1) % len(r[0])
        return x


def mm_group(S, out_ap, pairs, reads, writes):
    def fn(e):
        n = len(pairs)
        ins = None
        for i, (l, r) in enumerate(pairs):
            ins = e.matmul(out_ap, lhsT=l, rhs=r, start=(i == 0), stop=(i == n - 1))
        return ins
    return S.op("pe", fn, reads, writes)


def linear(cx, xin, W2d, row0, col0, ncols, evac, tiles=None, kc=KC):
    S = cx.S
    if tiles is None:
        tiles = [(t * TT, TT) for t in range(xin.N // TT)]
    nblk = (ncols + WB - 1) // WB

    def load(bi):
        c0 = col0 + bi * WB
        nc_ = min(WB, col0 + ncols - c0)
        stg, sgb = cx.get("wst")
        wt, wb = cx.get("w")
        src = W2d[row0:row0 + kc * 128, c0:c0 + nc_].rearrange("(c p) m -> p c m", p=128)
        S.dma("sp", lambda e, stg=stg, src=src, nc_=nc_: e.dma_start(out=stg[:, 0:kc, 0:nc_], in_=src), writes=[sgb])
        S.op("pool", lambda e, stg=stg, wt=wt, nc_=nc_: e.tensor_copy(out=wt[:, 0:kc, 0:nc_], in_=stg[:, 0:kc, 0:nc_]), reads=[sgb], writes=[wb])
        return wt, wb, nc_

    nxt = load(0)
    for bi in range(nblk):
        wt, wb, nc_ = nxt
        if bi + 1 < nblk:
            nxt = load(bi + 1)
        for mi in range(nc_ // 128):
            mglob = (bi * WB) // 128 + mi
            for (t0, tn) in tiles:
                pt, pb = cx.psum()
                pairs = [(wt[:, k, mi * 128:(mi + 1) * 128], xin.ap[:, k, t0:t0 + tn]) for k in range(kc)]
                mm_group(S, pt[:, 0:tn], pairs, reads=[wb] + xin.b(None, t0, t0 + tn), writes=[pb])
                evac(mglob, (t0, tn), pt[:, 0:tn], pb)


def layernorm(cx, z, gcol, bcol, outs, func=None, tok0=0, ntok=None):
    S = cx.S
    func = AF.Identity if func is None else func
    ntok = z.N if ntok is None else ntok
    ones = cx.ones
    def do_tile(tt):
        t0 = tok0 + tt * TT
        p1, pb1 = cx.psum()
        p2, pb2 = cx.psum()
        zbs, zss = [], []
        for c in range(KC):
            zb, zbb = cx.get("lnb")
            zs, zsb = cx.get("lnb")
            S.op("pool", lambda e, zb=zb, c=c: e.tensor_copy(out=zb, in_=z.ap[:, c, t0:t0 + TT]), reads=z.b(c, t0, t0 + TT), writes=[zbb])
            S.op("act", lambda e, zs=zs, c=c: e.activation(out=zs, in_=z.ap[:, c, t0:t0 + TT], func=AF.Square), reads=z.b(c, t0, t0 + TT), writes=[zsb])
            S.op("pe", lambda e, zb=zb, c=c: e.matmul(p1[:], lhsT=ones, rhs=zb, start=(c == 0), stop=(c == KC - 1)), reads=[zbb, cx.onesb], writes=[pb1])
            S.op("pe", lambda e, zs=zs, c=c: e.matmul(p2[:], lhsT=ones, rhs=zs, start=(c == 0), stop=(c == KC - 1)), reads=[zsb, cx.onesb], writes=[pb2])
        nmean, mb = cx.get("st")
        msq, qb = cx.get("st")
        rstd, rb = msq, qb
        S.op("act", lambda e: e.activation(out=nmean, in_=p1[:], func=AF.Identity, scale=-1.0 / 1024), reads=[pb1], writes=[mb])
        S.op("dve", lambda e: e.scalar_tensor_tensor(out=msq, in0=nmean, scalar=-1.0, in1=nmean, op0=ALU.mult, op1=ALU.mult), reads=[mb], writes=[qb])
        S.op("dve", lambda e: e.scalar_tensor_tensor(out=msq, in0=p2[:], scalar=1.0 / 1024, in1=msq, op0=ALU.mult, op1=ALU.add), reads=[pb2, qb], writes=[qb])
        S.op("act", lambda e: e.activation(out=rstd, in_=msq, func=AF.Sqrt, bias=cx.epsc, scale=1.0), reads=[qb], writes=[rb])
        S.op("dve", lambda e: e.reciprocal(out=rstd, in_=rstd), reads=[rb], writes=[rb])
        for c in range(KC):
            t1, tb1 = cx.get("s")
            S.op("dve", lambda e, c=c, t1=t1: e.tensor_tensor(out=t1, in0=z.ap[:, c, t0:t0 + TT], in1=nmean, op=ALU.add), reads=z.b(c, t0, t0 + TT) + [mb], writes=[tb1])
            S.op("pool", lambda e, t1=t1: e.tensor_tensor(out=t1, in0=t1, in1=rstd, op=ALU.mult), reads=[tb1, rb], writes=[tb1])
            for (o, ooff) in outs:
                oo = t0 - tok0 + ooff
                S.op("act", lambda e, c=c, t1=t1, o=o, oo=oo: e.activation(out=o.ap[:, c, oo:oo + TT], in_=t1, func=func, scale=gcol(c), bias=bcol(c)),
                     reads=[tb1, cx.vb], writes=o.b(c, oo, oo + TT))

    for tt in range(ntok // TT):
        do_tile(tt)


def emit_tok(cx, kind, D, final=False):
    conv = kind in ("conv0", "conv3")
    stop = None
    nc = cx.nc
    d_x, d_mem, d_vec = D["x"], D["memT"], D["vecs"]
    d_win = D.get("w_in")
    d_wout, d_wq, d_wkv, d_wo, d_w1, d_w2, d_y = D["w_out"], D["w_q"], D["w_kv"], D["w_o"], D["w1"], D["w2"], D["y"]
    if True:
        cx.off = 0
        cx.rot = {}
        S = cx.S
        vt = cx.vt
        vb = Buf("vecs")
        cx.vb = vb
        S.dma("sp", lambda e: e.dma_start(out=vt[:, 0:NVEC], in_=d_vec), writes=[vb])

        def V(name, c=0):
            i = VIDX[name] + c
            return vt[:, i:i + 1]

        o_x32 = cx.alloc(KC * NT)
        o_xbf = cx.alloc(KC * (NT + HALO) // 2)
        o_r1 = cx.alloc(KC * NT // 2)
        o_a = cx.alloc(KC * NT // 2)
        o_g = cx.alloc(2 * (NT // 2 + HALO))
        x32 = Ten(cx.f32(o_x32, KC, NT), "x32", KC, NT)
        xbf_all = cx.bf(o_xbf, KC, NT + HALO)
        xbf = Ten(xbf_all[:, :, HALO:HALO + NT], "xbf", KC, NT)
        A = Ten(cx.bf(o_a, KC, NT), "A", KC, NT)
        Q = Ten(cx.bf(o_r1, KC, NT), "Q", KC, NT)
        cx.make_rot("w", 2, KC * WB // 2, lambda o: cx.bf(o, KC, WB))
        cx.make_rot("wst", 1, KC * WB, lambda o: cx.f32(o, KC, WB))
        cx.make_rot("lnb", 2, TT // 2, lambda o: cx.arena[:, o:o + TT // 2].bitcast(BF16))
        cx.make_rot("s", 3, TT, lambda o: cx.arena[:, o:o + TT])
        cx.make_rot("st", 2, TT, lambda o: cx.arena[:, o:o + TT])
        o_ones = cx.alloc(64)
        cx.ones = cx.arena[:, o_ones:o_ones + 64].bitcast(BF16)
        onesb = Buf("ones")
        cx.onesb = onesb
        S.op("dve", lambda e: e.memset(cx.ones, 1.0), writes=[onesb])
        o_eps = cx.alloc(1)
        cx.epsc = cx.arena[:, o_eps:o_eps + 1]
        S.op("dve", lambda e: e.memset(cx.epsc, LN_EPS), writes=[onesb])
        o_kt = cx.alloc(KC * 256 // 2)
        o_v = cx.alloc(2 * 1024 // 2)
        KT = Ten(cx.bf(o_kt, KC, 256), "KT", KC, 256, tw=256)
        Vt = Ten(cx.bf(o_v, 2, 1024), "V", 2, 1024, tw=1024)
        memT = Ten(cx.bf(o_g, KC, 256), "memT", KC, 256, tw=256)
        PT = [(cx.arena[:, o_g + 1024 + i * 512: o_g + 1024 + (i + 1) * 512].bitcast(BF16).rearrange("p (c t) -> p c t", c=2), Buf("PT%d" % i)) for i in range(2)]

        xoff = HALO if kind == "conv0" else 0
        for c in range(KC):
            S.dma("sp", lambda e, c=c: e.dma_start(out=x32.ap[:, c, :], in_=d_x[c * 128:(c + 1) * 128, xoff:xoff + NT]), reads=D.get("xB", []), writes=x32.b(c))

        ydst = [d_y] + list(D.get("y_extra", []))
        yB = D.get("yB", [Buf("ydram")])

        def finish_out():
            for c in range(KC):
                for yd in ydst:
                    S.dma("sp", lambda e, c=c, yd=yd: e.dma_start(out=yd[c * 128:(c + 1) * 128, :], in_=x32.ap[:, c, :]), reads=x32.b(c), writes=yB, is_output=final)
            if "tail_out" in D:
                S.dma("sp", lambda e: e.dma_start(out=D["tail_out"].rearrange("(c p) t -> p c t", p=128), in_=x32.ap[:, :, NT - HALO:NT]), reads=x32.b(None, NT - HALO, NT), writes=yB)

        if stop == "load":
            finish_out()
            return nc

        def resid_prep(bias_name):
            for c in range(KC):
                if bias_name is None:
                    S.op("act", lambda e, c=c: e.activation(out=x32.ap[:, c, :], in_=x32.ap[:, c, :], func=AF.Identity, scale=ALPHA), reads=x32.b(c), writes=x32.b(c))
                else:
                    S.op("act", lambda e, c=c: e.activation(out=x32.ap[:, c, :], in_=x32.ap[:, c, :], func=AF.Identity, scale=ALPHA, bias=V(bias_name, c)), reads=x32.b(c) + [vb], writes=x32.b(c))

        def evac_acc(m, tl, p, pb):
            t0, tn = tl
            S.op("dve", lambda e: e.tensor_tensor(out=x32.ap[:, m, t0:t0 + tn], in0=x32.ap[:, m, t0:t0 + tn], in1=p, op=ALU.add), reads=[pb] + x32.b(m, t0, t0 + tn), writes=x32.b(m, t0, t0 + tn))

        flip = [0]

        def evac_copy_to(dst):
            def ev(m, tl, p, pb):
                t0, tn = tl
                flip[0] ^= 1
                if flip[0]:
                    S.op("act", lambda e: e.activation(out=dst.ap[:, m, t0:t0 + tn], in_=p, func=AF.Identity), reads=[pb], writes=dst.b(m, t0, t0 + tn))
                else:
                    S.op("dve", lambda e: e.tensor_copy(out=dst.ap[:, m, t0:t0 + tn], in_=p), reads=[pb], writes=dst.b(m, t0, t0 + tn))
            return ev

        if conv:
            if kind == "conv0":
                S.dma("pool", lambda e: e.dma_start(out=xbf_all, in_=d_x.rearrange("(c p) t -> p c t", p=128)), reads=D.get("xB", []), writes=xbf.b())
            else:
                S.dma("pool", lambda e: e.dma_start(out=xbf.ap, in_=d_x.rearrange("(c p) t -> p c t", p=128)), reads=D.get("xB", []), writes=xbf.b())
                hg = D["halo_g"]
                hacc, haB = cx.get("st")
                hv = hacc[:, 0:KC * HALO].rearrange("p (c t) -> p c t", c=KC)
                for j in range(4):
                    ht, htB = cx.get("s")
                    htv = ht[:, 0:KC * HALO].rearrange("p (c t) -> p c t", c=KC)
                    S.dma("sp", lambda e, htv=htv, j=j: e.dma_start(out=htv, in_=hg[j * 1024:(j + 1) * 1024, :].rearrange("(c p) t -> p c t", p=128)), reads=D["halo_gB"], writes=[htB])
                    if j == 0:
                        S.op("dve", lambda e, htv=htv: e.tensor_scalar(out=hv, in0=htv, scalar1=V("hsel", 0), scalar2=None, op0=ALU.mult), reads=[htB, vb], writes=[haB])
                    else:
                        S.op("dve", lambda e, htv=htv, j=j: e.scalar_tensor_tensor(out=hv, in0=htv, scalar=V("hsel", j), in1=hv, op0=ALU.mult, op1=ALU.add), reads=[htB, vb, haB], writes=[haB])
                S.op("dve", lambda e: e.tensor_copy(out=xbf_all[:, :, 0:HALO], in_=hv), reads=[haB], writes=xbf.b(None, 0, 1))
            NH = NT // 2
            CO = Ten(cx.f32(o_r1, KC, NH), "CO", KC, NH)
            glus = [(cx.arena[:, o_g + i * (NH + HALO): o_g + (i + 1) * (NH + HALO)], Buf("glu%d" % i)) for i in range(2)]
            gi = 0
            for half in range(2):
                base = half * NH
                tiles = [(base, HALO), (base + HALO, TT), (base + HALO + TT, TT)]
                for m in range(KC):
                    glu, gb = glus[gi]
                    gi ^= 1
                    pend = {}

                    def ev(mg, tl, p, pb, glu=glu, gb=gb, m=m, pend=pend, base=base, half=half):
                        t0, tn = tl
                        which = mg % 2
                        if which == 0:
                            pend[t0] = (p, pb)
                            return
                        pv, pvb = pend.pop(t0)
                        sg, sb = cx.get("s")
                        S.op("act", lambda e: e.activation(out=sg[:, 0:tn], in_=p, func=AF.Sigmoid, bias=V("b_gate", m)), reads=[pb, vb], writes=[sb])
                        g0 = t0 - base
                        S.op("dve", lambda e: e.scalar_tensor_tensor(out=glu[:, g0:g0 + tn], in0=pv, scalar=V("b_val", m), in1=sg[:, 0:tn], op0=ALU.add, op1=ALU.mult), reads=[pvb, sb, vb], writes=[gb])
                        if g0 == 0 and half == 0:
                            S.op("dve", lambda e: e.tensor_scalar(out=glu[:, 0:HALO], in0=glu[:, 0:HALO], scalar1=V("halo_mask"), scalar2=None, op0=ALU.mult), reads=[gb, vb], writes=[gb])

                    class XH:
                        ap = xbf_all
                        N = NT + HALO

                        @staticmethod
                        def b(c, t0, t1):
                            return xbf.b(None, max(0, t0 - HALO), max(1, t1 - HALO))
                    linear(cx, XH, d_win, 0, m * 256, 256, ev, tiles=tiles)
                    if m % 2 == 0:
                        for j in range(CONVW):
                            src = glu[:, 2 + j: 2 + j + NH]
                            if j == 0:
                                S.op("dve", lambda e, src=src, m=m: e.tensor_scalar(out=CO.ap[:, m, :], in0=src, scalar1=V("dw", m), scalar2=V("dw_b", m), op0=ALU.mult, op1=ALU.add), reads=[gb, vb], writes=CO.b(m))
                            else:
                                S.op("dve", lambda e, src=src, m=m, j=j: e.scalar_tensor_tensor(out=CO.ap[:, m, :], in0=src, scalar=V("dw", j * KC + m), in1=CO.ap[:, m, :], op0=ALU.mult, op1=ALU.add), reads=[gb, vb] + CO.b(m), writes=CO.b(m))
                    else:
                        for sub in range(NH // TT):
                            s0 = sub * TT
                            for j in range(CONVW):
                                src = glu[:, 2 + j + s0: 2 + j + s0 + TT]
                                dst = CO.ap[:, m, s0:s0 + TT]
                                if j == 0:
                                    S.op("act", lambda e, src=src, dst=dst, m=m: e.activation(out=dst, in_=src, func=AF.Identity, scale=V("dw", m), bias=V("dw_b", m)), reads=[gb, vb], writes=CO.b(m, s0, s0 + TT))
                                else:
                                    tmp, tmb = cx.get("s")
                                    S.op("act", lambda e, src=src, tmp=tmp, m=m, j=j: e.activation(out=tmp, in_=src, func=AF.Identity, scale=V("dw", j * KC + m)), reads=[gb, vb], writes=[tmb])
                                    S.op("pool", lambda e, dst=dst, tmp=tmp: e.tensor_tensor(out=dst, in0=dst, in1=tmp, op=ALU.add), reads=[tmb] + CO.b(m, s0, s0 + TT), writes=CO.b(m, s0, s0 + TT))
                layernorm(cx, CO, lambda c: V("a_ln_g", c), lambda c: V("a_ln_b", c), [(A, half * NH)], func=AF.Silu)
            S.barrier()
            if stop == "conv":
                finish_out()
                return nc
            resid_prep("b_out")
        else:
            S.dma("pool", lambda e: e.dma_start(out=xbf.ap, in_=d_x.rearrange("(c p) t -> p c t", p=128)), reads=D.get("xB", []), writes=xbf.b())
            og = D["h_g"]
            for tt in range(NTT):
                for c in range(KC):
                    acc, accB = cx.get("st")
                    for j in range(4):
                        ht, htB = cx.get("s")
                        row0 = (2 * j + tt // 2) * 1024 + (c // 2) * 256 + (c % 2) * 128
                        off = (tt % 2) * TT
                        S.dma("sp", lambda e, ht=ht, row0=row0, off=off: e.dma_start(out=ht, in_=og[row0:row0 + 128, off:off + TT]), reads=D["h_gB"], writes=[htB])
                        if j == 0:
                            S.op("dve", lambda e, ht=ht, acc=acc: e.tensor_scalar(out=acc, in0=ht, scalar1=V("hsel", 0), scalar2=None, op0=ALU.mult), reads=[htB, vb], writes=[accB])
                        elif j < 3:
                            S.op("dve", lambda e, ht=ht, acc=acc, j=j: e.scalar_tensor_tensor(out=acc, in0=ht, scalar=V("hsel", j), in1=acc, op0=ALU.mult, op1=ALU.add), reads=[htB, vb, accB], writes=[accB])
                        else:
                            S.op("dve", lambda e, ht=ht, acc=acc, c=c, tt=tt: e.scalar_tensor_tensor(out=A.ap[:, c, tt * TT:(tt + 1) * TT], in0=ht, scalar=V("hsel", 3), in1=acc, op0=ALU.mult, op1=ALU.add),
                                 reads=[htB, vb, accB], writes=A.b(c, tt * TT, (tt + 1) * TT))
            resid_prep(None)
        if stop == "prep":
            finish_out()
            return nc
        linear(cx, A, d_wout, 0, 0, 1024, evac_acc)
        if stop == "wout":
            finish_out()
            return nc
        if stop == "ln0a":
            layernorm(cx, x32, lambda c: V("ln_g0", c), lambda c: V("ln_b0", c), [(x32, 0)])
        elif stop == "ln0b":
            layernorm(cx, x32, lambda c: V("ln_g0", c), lambda c: V("ln_b0", c), [(xbf, 0)])
        elif stop == "ln0c":
            layernorm(cx, x32, lambda c: V("ln_g0", c), lambda c: V("ln_b0", c), [(x32, 0)], ntok=512)
        else:
            layernorm(cx, x32, lambda c: V("ln_g0", c), lambda c: V("ln_b0", c), [(x32, 0), (xbf, 0)])
        S.barrier()
        if stop in ("ln0", "ln0a", "ln0b", "ln0c"):
            finish_out()
            return nc

        S.dma("pool", lambda e: e.dma_start(out=memT.ap, in_=d_mem.rearrange("(c p) t -> p c t", p=128)), writes=memT.b())
        linear(cx, memT, d_wkv, 0, 0, 1024, evac_copy_to(KT), tiles=[(0, 256)])
        for eb in range(4):
            wt, wb = cx.get("w")
            src = d_wkv[:, 1024 + eb * WB: 1024 + (eb + 1) * WB].rearrange("(c p) m -> p c m", p=128)
            S.dma("pool", lambda e, wt=wt, src=src: e.dma_start(out=wt, in_=src), writes=[wb])
            for mch in range(2):
                pt, pb = cx.psum()
                pairs = [(memT.ap[:, k, mch * 128:(mch + 1) * 128], wt[:, k, :]) for k in range(KC)]
                mm_group(S, pt[:, 0:WB], pairs, reads=[wb] + memT.b(), writes=[pb])
                S.op("act", lambda e, pt=pt, mch=mch, eb=eb: e.activation(out=Vt.ap[:, mch, eb * WB:(eb + 1) * WB], in_=pt[:, 0:WB], func=AF.Identity), reads=[pb], writes=Vt.b(mch))
        linear(cx, xbf, d_wq, 0, 0, 1024, evac_copy_to(Q))
        resid_prep(None)
        O = A
        pi = 0
        for tt in range(NTT):
            t0 = tt * TT
            for h in range(4):
                P, Pb = PT[pi]
                pi ^= 1
                for mch in range(2):
                    pl, plb = cx.psum()
                    pairs = [(KT.ap[:, 2 * h + ec, mch * 128:(mch + 1) * 128], Q.ap[:, 2 * h + ec, t0:t0 + TT]) for ec in range(2)]
                    mm_group(S, pl[:], pairs, reads=KT.b([2 * h, 2 * h + 1]) + Q.b([2 * h, 2 * h + 1], t0, t0 + TT), writes=[plb])
                    S.op("act", lambda e, pl=pl, P=P, mch=mch: e.activation(out=P[:, mch, :], in_=pl[:], func=AF.Exp, scale=1.0 / 16), reads=[plb], writes=[Pb])
                pd, pdb = cx.psum()
                mm_group(S, pd[:], [(cx.ones, P[:, 0, :]), (cx.ones, P[:, 1, :])], reads=[Pb, onesb], writes=[pdb])
                rd, rdb = cx.get("s")
                S.op("dve", lambda e, rd=rd, pd=pd: e.reciprocal(out=rd, in_=pd[:]), reads=[pdb], writes=[rdb])
                for ec in range(2):
                    po, pob = cx.psum()
                    ch = 2 * h + ec
                    pairs = [(Vt.ap[:, mch, ch * 128:(ch + 1) * 128], P[:, mch, :]) for mch in range(2)]
                    mm_group(S, po[:], pairs, reads=Vt.b() + [Pb], writes=[pob])
                    S.op("dve", lambda e, po=po, rd=rd, ch=ch, t0=t0: e.tensor_tensor(out=O.ap[:, ch, t0:t0 + TT], in0=po[:], in1=rd, op=ALU.mult), reads=[pob, rdb], writes=O.b(ch, t0, t0 + TT))
        linear(cx, O, d_wo, 0, 0, 1024, evac_acc)
        layernorm(cx, x32, lambda c: V("ln_g1", c), lambda c: V("ln_b1", c), [(x32, 0), (xbf, 0)])
        S.barrier()
        if stop == "xattn":
            finish_out()
            return nc

        resid_prep(None)
        H = A

        def evac_relu2(m, tl, p, pb):
            t0, tn = tl
            r, rb_ = cx.get("s")
            S.op("act", lambda e: e.activation(out=r[:, 0:tn], in_=p, func=AF.Relu), reads=[pb], writes=[rb_])
            S.op("pool", lambda e: e.tensor_tensor(out=H.ap[:, m, t0:t0 + tn], in0=r[:, 0:tn], in1=r[:, 0:tn], op=ALU.mult), reads=[rb_], writes=H.b(m, t0, t0 + tn))

        for fq in range(4):
            linear(cx, xbf, d_w1, 0, fq * 1024, 1024, evac_relu2)
            linear(cx, H, d_w2, fq * 1024, 0, 1024, evac_acc)
        layernorm(cx, x32, lambda c: V("ln_g2", c), lambda c: V("ln_b2", c), [(x32, 0)])
        finish_out()
        return yB


VNAMES = [("ln_g0", 8), ("ln_b0", 8), ("ln_g1", 8), ("ln_b1", 8), ("ln_g2", 8), ("ln_b2", 8),
          ("b_val", 8), ("b_gate", 8), ("dw", 31 * 8), ("dw_b", 8), ("a_ln_g", 8), ("a_ln_b", 8), ("b_out", 8), ("halo_mask", 1), ("hsel", 4)]
VIDX = {}
_o = 0
for _n, _k in VNAMES:
    VIDX[_n] = _o
    _o += _k
NVEC = _o


def fcols(v):
    return np.ascontiguousarray(np.asarray(v, np.float32).reshape(-1, 128).T)


def make_vecs(d):
    out = np.zeros((128, NVEC), np.float32)
    for n, k in VNAMES:
        if n in d:
            out[:, VIDX[n]:VIDX[n] + k] = d[n]
    return out


from contextlib import ExitStack
import math
import numpy as np
import concourse.bass as bass
import concourse.mybir as mybir

SEQ = 8192
SB = 2048
NSB = SEQ // SB
GROUPS = ((128, 1), (512, 4), (2048, 16))
DILS = (1, 4, 16)


def t5_bucket(n):
    n = max(int(n), 0)
    if n < 16:
        return n
    v = np.float32(16) + (np.log(np.float32(n) / np.float32(16)) / np.float32(math.log(2048 / 16)) * np.float32(16)).astype(np.int32)
    return int(min(int(v), 31))


def oh_tables():
    keys, mats = [], []
    for g, d in enumerate(DILS):
        for chunk in range(2):
            q = np.arange(128)[None, :]
            k = np.arange(128)[:, None]
            rel = q + 128 - k if chunk == 0 else q - k
            valid = (rel >= 0) & (rel <= 128)
            bk = np.vectorize(t5_bucket)(np.maximum(rel, 0) * d)
            for b in range(32):
                m = (valid & (bk == b)).astype(np.float32)
                if m.any():
                    keys.append((g, chunk, b))
                    mats.append(m)
    return keys, np.stack(mats)


OH_KEYS, OH_MATS = oh_tables()


def emit_attn(cx, D):
    nc = cx.nc
    st = cx.stack
    d_x, d_w, d_rb, d_oh, d_o = D["x_g"], D["wqkv"], D["relb"], D["oh"], D["o"]
    if True:
        cx.off = 0
        cx.rot = {}
        S = cx.S
        A = cx.arena
        cx.D = D

        def bf2(off, n):
            return A[:, off:off + n // 2].bitcast(BF16)

        o_erb = cx.alloc(512)
        erb = A[:, o_erb:o_erb + 512]
        erbB = Buf("erb")
        S.dma("sp", lambda e: e.dma_start(out=erb, in_=d_rb), writes=[erbB])
        S.op("act", lambda e: e.activation(out=erb, in_=erb, func=AF.Exp), reads=[erbB], writes=[erbB])
        o_tab = cx.alloc(12 * 256 + 128)
        tabB = Buf("tab")
        S.op("dve", lambda e: e.memset(A[:, o_tab:o_tab + 12 * 256 + 128], 0.0), writes=[tabB])
        zero_tab = A[:, o_tab + 12 * 256: o_tab + 12 * 256 + 128]

        def tab(g, hh, chunk):
            o = o_tab + (g * 4 + hh) * 256 + chunk * 128
            return A[:, o:o + 128]
        cx.make_rot("oh", 3, 128, lambda o: A[:, o:o + 128])
        cx.hbase = None
        return_nc = nc
        cx.tab = tab
        cx.zero_tab = zero_tab
        cx.erb = erb
        cx.erbB = erbB
        cx.tabB = tabB
        return _attn_body(cx, nc, st, d_x, d_w, d_oh, d_o, bf2)


def _attn_body(cx, nc, st, d_x, d_w, d_oh, d_o, bf2):
    S = cx.S
    A = cx.arena
    tab, zero_tab, erb, erbB, tabB = cx.tab, cx.zero_tab, cx.erb, cx.erbB, cx.tabB
    for i, (g, chunk, b) in enumerate(OH_KEYS):
        oh, ohB = cx.get("oh")
        S.dma("sp", lambda e, oh=oh, i=i: e.dma_start(out=oh, in_=d_oh[i]), writes=[ohB])
        for hh in range(4):
            t = tab(g, hh, chunk)
            S.op("dve", lambda e, t=t, oh=oh, b=b, hh=hh: e.scalar_tensor_tensor(out=t, in0=oh, scalar=erb[:, b * 4 + hh: b * 4 + hh + 1], in1=t, op0=ALU.mult, op1=ALU.add),
                 reads=[ohB, erbB, tabB], writes=[tabB])

    o_x = cx.alloc(KC * SB // 2)
    xbf = bf2(o_x, KC * SB).rearrange("p (c t) -> p c t", c=KC)
    xB = Buf("x")
    o_w = cx.alloc(KC * 1152 // 2)
    wt = bf2(o_w, KC * 1152).rearrange("p (c m) -> p c m", c=KC)
    wB = Buf("w")
    o_q = cx.alloc(3 * SB // 2)
    QT = bf2(o_q, 3 * SB).rearrange("p (g t) -> p g t", g=3)
    qB = [Buf("q%d" % g) for g in range(3)]
    o_k = cx.alloc(2 * 3 * SB // 2)
    KT = bf2(o_k, 2 * 3 * SB).rearrange("p (s g t) -> p s g t", s=2, g=3)
    kB = [[Buf("k%d_%d" % (s, g)) for g in range(3)] for s in range(2)]
    o_v = cx.alloc(2 * 3 * 16 * 128 // 2)
    VT = bf2(o_v, 2 * 3 * 16 * 128).rearrange("p (s g j e) -> p s g j e", s=2, g=3, j=16)
    vB = [[Buf("v%d_%d" % (s, g)) for g in range(3)] for s in range(2)]
    o_ao = cx.alloc(2 * SB)
    accO = A[:, o_ao:o_ao + 2 * SB].rearrange("p (h t) -> p h t", h=2)
    o_ad = cx.alloc(2 * SB)
    accD = A[:, o_ad:o_ad + 2 * SB].rearrange("p (h t) -> p h t", h=2)
    aB = [Buf("acc%d" % h) for h in range(2)]
    cx.make_rot("E", 2, 512, lambda o: A[:, o:o + 512])
    cx.make_rot("P", 2, 256, lambda o: A[:, o:o + 256].bitcast(BF16))
    o_ones = cx.alloc(32)
    ones64 = A[:, o_ones:o_ones + 32].bitcast(BF16)
    onesB = Buf("ones")
    S.op("dve", lambda e: e.memset(ones64, 1.0), writes=[onesB])

    def tok_view(ap2d, d):
        return ap2d.rearrange("p (n r) -> p r n", r=d)

    flip = [0]
    oB = [Buf("o_dram")]
    for ps_ in range(2):
        S.dma("pool", lambda e, ps_=ps_: e.dma_start(out=wt, in_=d_w[:, ps_ * 1152:(ps_ + 1) * 1152].rearrange("(c p) m -> p c m", p=128)), writes=[wB])
        for sb in range(NSB):
            slot = sb % 2
            S.dma("pool", lambda e, sb=sb: e.dma_start(out=xbf, in_=d_x.rearrange("(c r p) t -> r p c t", c=8, r=4, p=128)[sb]), reads=cx.D["xgB"], writes=[xB])
            for g in range(3):
                d = DILS[g]
                for t in range(2):
                    for tt in range(SB // 512):
                        pt, pb = cx.psum()
                        blk = g * 3 + t
                        pairs = [(wt[:, k, blk * 128:(blk + 1) * 128], xbf[:, k, tt * 512:(tt + 1) * 512]) for k in range(KC)]
                        mm_group(S, pt[:], pairs, reads=[wB, xB], writes=[pb])
                        dst = QT[:, g, tt * 512:(tt + 1) * 512] if t == 0 else KT[:, slot, g, tt * 512:(tt + 1) * 512]
                        dB = qB[g] if t == 0 else kB[slot][g]
                        flip[0] ^= 1
                        if flip[0]:
                            S.op("act", lambda e, dst=dst, pt=pt: e.activation(out=dst, in_=pt[:], func=AF.Identity), reads=[pb], writes=[dB])
                        else:
                            S.op("dve", lambda e, dst=dst, pt=pt: e.tensor_copy(out=dst, in_=pt[:]), reads=[pb], writes=[dB])
                nper = 16 // d
                for j4 in range(4):
                    pt, pb = cx.psum()
                    for jj in range(4):
                        j = j4 * 4 + jj
                        r, n_ = j // nper, j % nper
                        blk = g * 3 + 2
                        pairs = []
                        for k in range(KC):
                            xv = xbf[:, k, :].rearrange("p (n r) -> p r n", r=d)[:, r, n_ * 128:(n_ + 1) * 128]
                            pairs.append((xv, wt[:, k, blk * 128:(blk + 1) * 128]))
                        mm_group(S, pt[:, jj * 128:(jj + 1) * 128], pairs, reads=[wB, xB], writes=[pb])
                    dst = VT[:, slot, g, j4 * 4:(j4 + 1) * 4, :]
                    S.op("act", lambda e, dst=dst, pt=pt: e.activation(out=dst, in_=pt[:].rearrange("p (j e) -> p j e", j=4), func=AF.Identity), reads=[pb], writes=[vB[slot][g]])
            for g in range(3):
                d = DILS[g]
                nper = 16 // d
                for hh in range(2):
                    hp = slice(hh * 64, (hh + 1) * 64)
                    habs = ps_ * 2 + hh
                    qv = QT[hp, g, :].rearrange("p (n r) -> p r n", r=d)
                    kcur = KT[hp, slot, g, :].rearrange("p (n r) -> p r n", r=d)
                    kprv = KT[hp, 1 - slot, g, :].rearrange("p (n r) -> p r n", r=d)
                    aOv = accO[0:64, hh, :].rearrange("p (n r) -> p r n", r=d)
                    aDv = accD[0:64, hh, :].rearrange("p (n r) -> p r n", r=d)
                    for u2 in range(8):
                        pl, plb = cx.psum()
                        units = []
                        for uu in range(2):
                            j = u2 * 2 + uu
                            r, n_ = j // nper, j % nper
                            qs = qv[:, r, n_ * 128:(n_ + 1) * 128]
                            first = (sb == 0 and n_ == 0)
                            if n_ > 0:
                                kp = kcur[:, r, (n_ - 1) * 128:n_ * 128]
                                vp = VT[:, slot, g, j - 1, hp]
                                rp = [kB[slot][g], vB[slot][g]]
                            elif first:
                                kp = kcur[:, r, 0:128]
                                vp = VT[:, slot, g, j, hp]
                                rp = [kB[slot][g], vB[slot][g]]
                            else:
                                kp = kprv[:, r, (nper - 1) * 128:nper * 128]
                                vp = VT[:, 1 - slot, g, r * nper + nper - 1, hp]
                                rp = [kB[1 - slot][g], vB[1 - slot][g]]
                            kc_ = kcur[:, r, n_ * 128:(n_ + 1) * 128]
                            vc = VT[:, slot, g, j, hp]
                            units.append((r, n_, first, vp, vc, rp))
                            mm_group(S, pl[:, uu * 256:uu * 256 + 128], [(kp, qs)], reads=[qB[g], kB[slot][g]] + rp, writes=[plb])
                            mm_group(S, pl[:, uu * 256 + 128:uu * 256 + 256], [(kc_, qs)], reads=[qB[g], kB[slot][g]], writes=[plb])
                        E, EB = cx.get("E")
                        S.op("act", lambda e, E=E, pl=pl: e.activation(out=E, in_=pl[:], func=AF.Exp, scale=0.125), reads=[plb], writes=[EB])
                        po, pob = cx.psum()
                        for uu in range(2):
                            r, n_, first, vp, vc, rp = units[uu]
                            P, PB = cx.get("P")
                            t0_ = zero_tab if first else cx.tab(g, habs, 0)
                            S.op("dve", lambda e, P=P, E=E, uu=uu, t0_=t0_: e.tensor_tensor(out=P[:, 0:128], in0=E[:, uu * 256:uu * 256 + 128], in1=t0_, op=ALU.mult), reads=[EB, tabB], writes=[PB])
                            t1_ = cx.tab(g, habs, 1)
                            S.op("dve", lambda e, P=P, E=E, uu=uu, t1_=t1_: e.tensor_tensor(out=P[:, 128:256], in0=E[:, uu * 256 + 128:uu * 256 + 256], in1=t1_, op=ALU.mult), reads=[EB, tabB], writes=[PB])
                            mm_group(S, po[0:64, uu * 128:(uu + 1) * 128], [(vp, P[:, 0:128]), (vc, P[:, 128:256])], reads=[PB, vB[slot][g]] + rp, writes=[pob])
                            mm_group(S, po[0:64, 256 + uu * 128:256 + (uu + 1) * 128], [(ones64, P[:, 0:128]), (ones64, P[:, 128:256])], reads=[PB, onesB], writes=[pob])
                        for uu in range(2):
                            r, n_ = units[uu][0], units[uu][1]
                            oO = aOv[:, r, n_ * 128:(n_ + 1) * 128]
                            oD = aDv[:, r, n_ * 128:(n_ + 1) * 128]
                            so = po[0:64, uu * 128:(uu + 1) * 128]
                            sd = po[0:64, 256 + uu * 128:256 + (uu + 1) * 128]
                            if g == 0:
                                S.op("act", lambda e, oO=oO, so=so: e.activation(out=oO, in_=so, func=AF.Identity), reads=[pob], writes=[aB[hh]])
                                S.op("act", lambda e, oD=oD, sd=sd: e.activation(out=oD, in_=sd, func=AF.Identity), reads=[pob], writes=[aB[hh]])
                            else:
                                S.op("dve", lambda e, oO=oO, so=so: e.tensor_tensor(out=oO, in0=oO, in1=so, op=ALU.add), reads=[pob, aB[hh]], writes=[aB[hh]])
                                S.op("dve", lambda e, oD=oD, sd=sd: e.tensor_tensor(out=oD, in0=oD, in1=sd, op=ALU.add), reads=[pob, aB[hh]], writes=[aB[hh]])
            for hh in range(2):
                S.op("dve", lambda e, hh=hh: e.reciprocal(out=accD[0:64, hh, :], in_=accD[0:64, hh, :]), reads=[aB[hh]], writes=[aB[hh]])
                S.op("dve", lambda e, hh=hh: e.tensor_tensor(out=accO[0:64, hh, :], in0=accO[0:64, hh, :], in1=accD[0:64, hh, :], op=ALU.mult), reads=[aB[hh]], writes=[aB[hh]])
                row = (ps_ * 2 + hh) * 64
                for kk in range(2):
                    r0 = (2 * sb + kk) * 256 + row
                    S.dma("sp", lambda e, hh=hh, r0=r0, kk=kk: e.dma_start(out=d_o[r0:r0 + 64, :], in_=accO[0:64, hh, kk * 1024:(kk + 1) * 1024]), reads=[aB[hh]], writes=oB)
    return oB


from contextlib import ExitStack
import math
import numpy as np
import concourse.bass as bass
import concourse.mybir as mybir

T = 8192
TW = 256
NTW = T // TW
SBLK = 128
NA = 6
NCONST = 1408
STOPB = [99]
GN_EPS = 64e-5
RV = {"mu": 0, "k_a": 48, "a0": 50, "lnx_g": 52, "lnx_b": 54, "r_k": 56}
NRV = 58


def emit_rwkv(cx, D):
    nc = cx.nc
    st = cx.stack
    d_xg, d_wrkv, d_wl1, d_wl2, d_vec, d_bc, d_blk, d_o, d_scr = D["x_g"], D["wrkv"], D["wl1"], D["wl2"], D["vecs"], D["bc"], D["blk"], D["o"], D["scr"]
    xgB = D["xgB"]
    NTC = 2048
    if True:
        cx.off = 0
        cx.rot = {}
        S = cx.S
        S.same_engine_sync = True
        A = cx.arena
        oB = [Buf("o_dram")]

        def f32v(off, n):
            return A[:, off:off + n]

        def bfv(off, n):
            return A[:, off:off + n // 2].bitcast(BF16)

        vt = cx.vt
        vb = Buf("vecs")
        S.dma("sp", lambda e: e.dma_start(out=vt[:, 0:NRV], in_=d_vec), writes=[vb])

        def V(name, c=0):
            i = RV[name] + c
            return vt[:, i:i + 1]

        o_y = cx.alloc(2 * T)
        YT = f32v(o_y, 2 * T).rearrange("p (s t) -> p s t", s=2)
        yB = [Buf("Y%d" % i) for i in range(NTW)]
        o_wrkv = cx.alloc(KC * 768 // 2)
        wrkv = bfv(o_wrkv, KC * 768).rearrange("p (c m) -> p c m", c=KC)
        o_wl1 = cx.alloc(KC * 256 // 2)
        wl1 = bfv(o_wl1, KC * 256).rearrange("p (c m) -> p c m", c=KC)
        o_wrkv2 = cx.alloc(KC * 768 // 2)
        wrkv2 = bfv(o_wrkv2, KC * 768).rearrange("p (c m) -> p c m", c=KC)
        o_wl12 = cx.alloc(KC * 256 // 2)
        wl12 = bfv(o_wl12, KC * 256).rearrange("p (c m) -> p c m", c=KC)
        o_wl2 = cx.alloc(768 // 2)
        wl2 = bfv(o_wl2, 768)
        o_blk = cx.alloc(64)
        blk = bfv(o_blk, 128)
        o_bc = cx.alloc(4 * 256)
        bc = f32v(o_bc, 1024).rearrange("p (a m) -> p a m", a=4)
        wB = Buf("weights")
        S.dma("pool", lambda e: e.dma_start(out=wrkv, in_=d_wrkv.rearrange("(c p) m -> p c m", p=128)), writes=[wB])
        S.dma("pool", lambda e: e.dma_start(out=wl1, in_=d_wl1.rearrange("(c p) m -> p c m", p=128)), writes=[wB])
        S.dma("pool", lambda e: e.dma_start(out=wl2, in_=d_wl2), writes=[wB])
        S.dma("pool", lambda e: e.dma_start(out=blk, in_=d_blk), writes=[wB])
        S.dma("sp", lambda e: e.dma_start(out=bc, in_=d_bc), writes=[wB])
        w2B = Buf("weights2")
        for k in range(KC):
            for (wt_, wt2_, c0, c1, s_) in ((wrkv, wrkv2, 0, 256, 0), (wrkv, wrkv2, 256, 512, 1), (wrkv, wrkv2, 512, 768, 2), (wl1, wl12, 0, 64, 3), (wl1, wl12, 64, 128, 4), (wl1, wl12, 128, 256, 5)):
                S.op("act", lambda e, k=k, wt_=wt_, wt2_=wt2_, c0=c0, c1=c1, s_=s_: e.activation(out=wt2_[:, k, c0:c1], in_=wt_[:, k, c0:c1], func=AF.Identity, scale=V("mu", s_ * 8 + k)), reads=[wB, vb], writes=[w2B])
        o_eps = cx.alloc(2)
        epsc = f32v(o_eps, 1)
        S.op("dve", lambda e: e.memset(epsc, GN_EPS), writes=[wB])
        REG = 25800
        o_reg = cx.alloc(REG)
        ro = [o_reg]

        def ralloc(n):
            o = ro[0]
            ro[0] += n
            assert ro[0] <= o_reg + REG, (ro[0] - o_reg)
            return o
        xs = f32v(ralloc(KC * (TW + 1)), KC * (TW + 1)).rearrange("p (c t) -> p c t", c=KC)
        xsB = Buf("xs")
        xxb = bfv(ralloc(KC * TW // 2), KC * TW).rearrange("p (c t) -> p c t", c=KC)
        xb = bfv(ralloc(KC * TW // 2), KC * TW).rearrange("p (c t) -> p c t", c=KC)
        xxB = Buf("xx")
        _tmo = [ralloc(4 * NA * 64) for _ in range(2)]
        TMt = [f32v(o, 4 * NA * 64).rearrange("p (h a e) -> p h a e", h=4, a=NA) for o in _tmo]
        TMf = [f32v(o, 4 * NA * 64) for o in _tmo]
        tmB = [Buf("TM%d" % i) for i in range(2)]
        tq = [(f32v(ralloc(256), 256), Buf("tq%d" % i)) for i in range(8)]
        fm = [(f32v(ralloc(TW), TW), Buf("fm%d" % i)) for i in range(6)]
        tqi = [0]
        fmi = [0]

        def gtq():
            x = tq[tqi[0]]
            tqi[0] = (tqi[0] + 1) % len(tq)
            return x

        def gfm():
            x = fm[fmi[0]]
            fmi[0] = (fmi[0] + 1) % len(fm)
            return x

        def load_x_and_mix(tt, which):
            t0 = tt * TW
            rk, l0 = t0 // NTC, t0 % NTC

            def xsrc(rk_, a, b_):
                return d_xg.rearrange("(c r p) t -> r p c t", c=8, r=4, p=128)[rk_][:, :, a:b_]
            if l0 > 0:
                S.dma("sp", lambda e: e.dma_start(out=xs[:, :, :], in_=xsrc(rk, l0 - 1, l0 + TW)), reads=xgB, writes=[xsB])
            else:
                if rk == 0:
                    S.op("dve", lambda e: e.memset(xs[:, :, 0:1], 0.0), writes=[xsB])
                else:
                    S.dma("sp", lambda e: e.dma_start(out=xs[:, :, 0:1], in_=xsrc(rk - 1, NTC - 1, NTC), allow_slow_non_contiguous=True), reads=xgB, writes=[xsB])
                S.dma("sp", lambda e: e.dma_start(out=xs[:, :, 1:TW + 1], in_=xsrc(rk, 0, TW)), reads=xgB, writes=[xsB])
            S.op("dve", lambda e: e.scalar_tensor_tensor(out=xxb, in0=xs[:, :, 1:TW + 1], scalar=-1.0, in1=xs[:, :, 0:TW], op0=ALU.mult, op1=ALU.add), reads=[xsB], writes=[xxB])
            S.op("act", lambda e: e.activation(out=xb, in_=xs[:, :, 1:TW + 1], func=AF.Identity), reads=[xsB], writes=[xxB])

        W2 = {id(wrkv): wrkv2, id(wl1): wl12}

        def fm_proj(wtile, col0, ncol):
            pt, pb = cx.psum()
            w2 = W2[id(wtile)]
            pairs = [(wtile[:, k, col0:col0 + ncol], xb[:, k, :]) for k in range(KC)] + [(w2[:, k, col0:col0 + ncol], xxb[:, k, :]) for k in range(KC)]
            mm_group(S, pt[0:ncol, 0:TW], pairs, reads=[wB, w2B, xxB], writes=[pb])
            return pt, pb

        def tm_pairs(tb0, c0, c1):
            return [(xb[:, k, tb0:tb0 + 128], wrkv[:, k, c0:c1]) for k in range(KC)] + [(xxb[:, k, tb0:tb0 + 128], wrkv2[:, k, c0:c1]) for k in range(KC)]

        def phaseA(tt):
            t0 = tt * TW
            load_x_and_mix(tt, None)
            pw, pwb = fm_proj(wl1, 0, 64)
            tw_, twB = gfm()
            tw_b = tw_[0:64, 0:TW // 2].bitcast(BF16)
            S.op("act", lambda e: e.activation(out=tw_b, in_=pw[0:64, 0:TW], func=AF.Tanh), reads=[pwb], writes=[twB])
            pa, pab = fm_proj(wl1, 64, 64)
            ta_, taB = gfm()
            ta_b = ta_[0:64, 0:TW // 2].bitcast(BF16)
            S.op("act", lambda e: e.activation(out=ta_b, in_=pa[0:64, 0:TW], func=AF.Identity), reads=[pab], writes=[taB])
            for bi in range(TW // 128):
                tb0 = bi * 128
                tm = TMt[(tt * (TW // 128) + bi) % 2]
                tmf = TMf[(tt * (TW // 128) + bi) % 2]
                tmb = tmB[(tt * (TW // 128) + bi) % 2]
                pr, prb = cx.psum()
                mm_group(S, pr[:, 0:256], tm_pairs(tb0, 0, 256), reads=[wB, w2B, xxB], writes=[prb])
                pk, pkb = cx.psum()
                mm_group(S, pk[:, 0:256], tm_pairs(tb0, 256, 512), reads=[wB, w2B, xxB], writes=[pkb])
                p2w, p2wb = cx.psum()
                mm_group(S, p2w[:, 0:256], [(tw_b[:, tb0:tb0 + 128], wl2[0:64, 0:256])], reads=[wB, twB], writes=[p2wb])
                p2a, p2ab = cx.psum()
                mm_group(S, p2a[:, 0:256], [(ta_b[:, tb0:tb0 + 128], wl2[0:64, 256:512])], reads=[wB, taB], writes=[p2ab])

                def h3(ap):
                    return ap.rearrange("p (h e) -> p h e", h=4)
                S.op("act", lambda e, tm=tm, pr=pr: e.activation(out=tm[:, :, 4, :], in_=h3(pr[:, 0:256]), func=AF.Identity), reads=[prb], writes=[tmb])
                ksb, ksB = gtq()
                S.op("act", lambda e, ksb=ksb, pk=pk: e.activation(out=ksb, in_=pk[:, 0:256], func=AF.Identity), reads=[pkb], writes=[ksB])
                u, uB = gtq()
                S.op("dve", lambda e, u=u, p2w=p2w: e.tensor_tensor(out=u, in0=p2w[:, 0:256], in1=bc[:, 0, :], op=ALU.add), reads=[p2wb, wB], writes=[uB])
                S.op("act", lambda e, u=u: e.activation(out=u, in_=u, func=AF.Sigmoid), reads=[uB], writes=[uB])
                S.op("act", lambda e, u=u, tm=tm: e.activation(out=tm[:, :, 1, :], in_=h3(u), func=AF.Identity, scale=-math.exp(-0.5)), reads=[uB], writes=[tmb])
                pvt, pvtb = cx.psum()
                mm_group(S, pvt[:, 0:256], tm_pairs(tb0, 512, 768), reads=[wB, w2B, xxB], writes=[pvtb])
                S.op("act", lambda e, tm=tm, pvt=pvt: e.activation(out=tm[:, :, 5, :], in_=h3(pvt[:, 0:256]), func=AF.Identity), reads=[pvtb], writes=[tmb])
                a_, aB = gtq()
                S.op("dve", lambda e, a_=a_, p2a=p2a: e.tensor_tensor(out=a_, in0=p2a[:, 0:256], in1=bc[:, 1, :], op=ALU.add), reads=[p2ab, wB], writes=[aB])
                S.op("act", lambda e, a_=a_: e.activation(out=a_, in_=a_, func=AF.Sigmoid), reads=[aB], writes=[aB])
                kr, krB = gtq()
                S.op("dve", lambda e, kr=kr, ksb=ksb: e.tensor_tensor(out=kr, in0=ksb, in1=bc[:, 2, :], op=ALU.mult), reads=[ksB, wB], writes=[krB])
                sq, sqB = gtq()
                S.op("dve", lambda e, sq=sq, kr=kr: e.tensor_tensor(out=sq, in0=kr, in1=kr, op=ALU.mult), reads=[krB], writes=[sqB])
                ss, ssB = gtq()
                S.op("dve", lambda e, ss=ss, sq=sq: e.tensor_reduce(out=ss[:, 0:4], in_=h3(sq), axis=mybir.AxisListType.X, op=ALU.add), reads=[sqB], writes=[ssB])
                S.op("act", lambda e, ss=ss: e.activation(out=ss[:, 0:4], in_=ss[:, 0:4], func=AF.Sqrt), reads=[ssB], writes=[ssB])
                S.op("dve", lambda e, ss=ss: e.tensor_scalar_max(out=ss[:, 0:4], in0=ss[:, 0:4], scalar1=1e-12), reads=[ssB], writes=[ssB])
                S.op("dve", lambda e, ss=ss: e.reciprocal(out=ss[:, 0:4], in_=ss[:, 0:4]), reads=[ssB], writes=[ssB])
                for h in range(4):
                    S.op("dve", lambda e, h=h, tm=tm, kr=kr, ss=ss: e.tensor_scalar(out=tm[:, h, 0, :], in0=kr[:, h * 64:(h + 1) * 64], scalar1=ss[:, h:h + 1], scalar2=None, op0=ALU.mult), reads=[krB, ssB], writes=[tmb])
                tk, tkB = gtq()
                S.op("dve", lambda e, tk=tk, a_=a_: e.scalar_tensor_tensor(out=tk, in0=a_, scalar=-1.0, in1=bc[:, 3, :], op0=ALU.add, op1=ALU.mult), reads=[aB, wB], writes=[tkB])
                S.op("dve", lambda e, tk=tk, ksb=ksb, tm=tm: e.scalar_tensor_tensor(out=tm[:, :, 3, :], in0=h3(tk), scalar=1.0, in1=h3(ksb), op0=ALU.add, op1=ALU.mult), reads=[tkB, ksB], writes=[tmb])
                S.op("dve", lambda e, tm=tm, a_=a_: e.tensor_tensor(out=tm[:, :, 2, :], in0=tm[:, :, 0, :], in1=h3(a_), op=ALU.mult), reads=[tmb, aB], writes=[tmb])
                S.op("dve", lambda e, tm=tm: e.tensor_scalar(out=tm[:, :, 0, :], in0=tm[:, :, 0, :], scalar1=-1.0, scalar2=None, op0=ALU.mult), reads=[tmb], writes=[tmb])
                S.dma("sp", lambda e, tmf=tmf, tb0=tb0: e.dma_start(out=d_scr[t0 + tb0:t0 + tb0 + 128].rearrange("t h m -> t (h m)"), in_=tmf), reads=[tmb], writes=[scrB[(t0 + tb0) // SBLK + i] for i in range(128 // SBLK)])

        scrB = [Buf("scr%d" % i) for i in range(T // SBLK)]
        for tt in range(NTW):
            phaseA(tt)
        S.barrier()
        ro[0] = o_reg

        d_rc = D["rc"]
        cB = Buf("rconst")
        c_bf = bfv(ralloc(192), 384)
        triU, triL, ident = c_bf[:, 0:128], c_bf[:, 128:256], c_bf[:, 256:384]
        if STOPB[0] >= 0 or STOPB[0] == -2:
            S.dma("pool", lambda e: e.dma_start(out=c_bf, in_=d_rc[:, 0:384]), writes=[cB])
        M4 = f32v(ralloc(512), 512)
        SL4 = f32v(ralloc(512), 512)
        if STOPB[0] >= 0 or STOPB[0] == -2:
            S.dma("sp", lambda e: e.dma_start(out=M4, in_=d_rc[:, 384:896]), writes=[cB])
            S.dma("sp", lambda e: e.dma_start(out=SL4, in_=d_rc[:, 896:1408]), writes=[cB])
        onec = bfv(ralloc(4), 8)
        S.op("dve", lambda e: e.memset(onec, 1.0), writes=[cB])
        def mkbufs():
            TMi1 = (f32v(ralloc(4 * NA * 64), 4 * NA * 64), Buf("tmi"))
            LW = bfv(ralloc(256), 512).rearrange("p (s m) -> p s m", s=2)
            lwB = Buf("LW")
            ft = [f32v(ralloc(256), 256) for _ in range(5)]
            ftB = [Buf("ft%d" % i) for i in range(5)]
            Rt, Bt, Kt, Bh, Kh = [bfv(ralloc(128), 256) for _ in range(5)]
            tmB_ = Buf("tmprod")
            _o = ralloc(256)
            Vpf = f32v(_o, 256)
            Vp = bfv(_o, 512).rearrange("p (h m) -> p h m", h=4)
            vpB = Buf("Vp")
            FMt = bfv(ralloc(1024), 2048).rearrange("p (h a t) -> p h a t", h=4, a=4)
            fmB = Buf("FMt")
            AM = bfv(ralloc(1024), 2048).rearrange("p (h a t) -> p h a t", h=4, a=4)
            amB = Buf("AM")
            NPt = [[bfv(ralloc(256), 512).rearrange("p (h t) -> p h t", h=4) for _ in range(2)] for _ in range(2)]
            npB = [[Buf("NP%d%d" % (i, j)) for j in range(2)] for i in range(2)]
            _o = ralloc(768)
            X32f = f32v(_o, 768)
            X32 = f32v(_o, 768).rearrange("p (h m) -> p h m", h=4)
            _o = ralloc(384)
            Xbff = f32v(_o, 384)
            Xbf = bfv(_o, 768).rearrange("p (h m) -> p h m", h=4)
            x32B, xbB = Buf("X32"), Buf("Xbf")
            RhT = bfv(ralloc(256), 512).rearrange("p (h t) -> p h t", h=4)
            rhB = Buf("RhT")
            GT = bfv(ralloc(128), 256).rearrange("p (h t) -> p h t", h=4)
            gtB = Buf("GT")
            Hh = f32v(ralloc(256), 256).rearrange("p (h t) -> p h t", h=4)
            hhB = Buf("H")
            gl = f32v(ralloc(32), 32)
            glB = Buf("gl")


            return dict(locals())
        BUFS = [mkbufs(), mkbufs()]
        _o = ralloc(256)
        S32f = f32v(_o, 256)
        S32 = f32v(_o, 256).rearrange("p (h t) -> p h t", h=4)
        s32B = Buf("S32")
        _o = ralloc(256)
        Spadf = f32v(_o, 256)
        Spad = bfv(_o, 512).rearrange("p (h m) -> p h m", h=4)
        spB = Buf("Spad")

        def q4(ap3):
            return ap3.rearrange("p (q f) m -> p q f m", q=2)

        def h3(ap):
            return ap.rearrange("p (h e) -> p h e", h=4)

        def acts(h):
            return slice(0, 128) if h % 2 == 0 else slice(64, 192)

        def ahc(h):
            return slice(0, 64) if h % 2 == 0 else slice(128, 192)

        def upad(h):
            return slice(64, 192) if h % 2 == 0 else slice(0, 128)

        def vd(h):
            return slice(0, 64) if h % 2 == 0 else slice(64, 128)

        def run_chunks():
            for B_ in BUFS:
                S.op("dve", lambda e, B_=B_: e.memset(B_["X32f"], 0.0), writes=[B_["x32B"]])
                S.op("dve", lambda e, B_=B_: e.memset(B_["Xbff"], 0.0), writes=[B_["xbB"]])
                S.op("dve", lambda e, B_=B_: e.memset(B_["Vpf"], 0.0), writes=[B_["vpB"]])
            S.op("dve", lambda e: e.memset(Spadf, 0.0), writes=[spB])
            S.op("dve", lambda e: e.memset(S32f, 0.0), writes=[s32B])
            for c0 in range(0, T // 128, 2):
                g0, g1 = chunk(c0), chunk(c0 + 1)
                d0 = d1 = False
                while not (d0 and d1):
                    if not d0:
                        d0 = next(g0) == "split"
                    if not d1:
                        d1 = next(g1) == "split"
                for _ in g0:
                    pass
                for _ in g1:
                    pass

        def chunk(c):
            B_ = BUFS[c % 2]
            tmf, tib = B_["TMi1"]
            LW, lwB, ft, ftB, Rt, Bt, Kt, Bh, Kh, tmB_, Vp, vpB, FMt, fmB, AM, amB, NPt, npB, X32, Xbf, x32B, xbB, RhT, rhB, GT, gtB, Hh, hhB, gl, glB = [B_[k] for k in (
                "LW", "lwB", "ft", "ftB", "Rt", "Bt", "Kt", "Bh", "Kh", "tmB_", "Vp", "vpB", "FMt", "fmB", "AM", "amB", "NPt", "npB", "X32", "Xbf", "x32B", "xbB", "RhT", "rhB", "GT", "gtB", "Hh", "hhB", "gl", "glB")]
            tm = tmf.rearrange("p (h a e) -> p h a e", h=4, a=NA)
            tmq = tmf.rearrange("p (q f a e) -> p q f a e", q=2, f=2, a=NA)
            S.dma("sp", lambda e: e.dma_start(out=tmf, in_=d_scr[c * 128:(c + 1) * 128].rearrange("t h m -> t (h m)")), reads=[scrB[c]], writes=[tib])

            def arr(i):
                return tm[:, :, i, :]
            S.op("dve", lambda e: e.tensor_copy(out=h3(LW[:, 0, :]), in_=arr(1)), reads=[tib], writes=[lwB])
            S.op("dve", lambda e: e.scalar_tensor_tensor(out=h3(LW[:, 1, :]), in0=h3(LW[:, 0, :]), scalar=-1.0, in1=arr(1), op0=ALU.mult, op1=ALU.add), reads=[tib, lwB], writes=[lwB])
            pCI, pCIb = cx.psum()
            mm_group(S, pCI[:, 0:256], [(triU, LW[:, 0, :]), (triU, LW[:, 1, :])], reads=[cB, lwB], writes=[pCIb])
            pCR, pCRb = cx.psum()
            mm_group(S, pCR[:, 0:256], [(triL, LW[:, 0, :]), (triL, LW[:, 1, :])], reads=[cB, lwB], writes=[pCRb])
            pGL, pGLb = cx.psum()
            for h in range(4):
                mm_group(S, pGL[0:64, h * 8:(h + 1) * 8], [(LW[:, 0, h * 64:(h + 1) * 64], onec), (LW[:, 1, h * 64:(h + 1) * 64], onec)], reads=[cB, lwB], writes=[pGLb])
            e_ex, e_in, e_ninv, e_rev, tmpx = ft
            S.op("act", lambda e: e.activation(out=e_in, in_=pCI[:, 0:256], func=AF.Exp), reads=[pCIb], writes=[ftB[1]])
            S.op("act", lambda e: e.activation(out=e_ninv, in_=pCI[:, 0:256], func=AF.Exp, scale=-1.0), reads=[pCIb], writes=[ftB[2]])
            S.op("act", lambda e: e.activation(out=e_rev, in_=pCR[:, 0:256], func=AF.Exp), reads=[pCRb], writes=[ftB[3]])
            S.op("act", lambda e: e.activation(out=h3(tmpx), in_=arr(1), func=AF.Exp, scale=-1.0), reads=[tib], writes=[ftB[4]])
            S.op("dve", lambda e: e.tensor_tensor(out=e_ex, in0=tmpx, in1=e_in, op=ALU.mult), reads=[ftB[4], ftB[1]], writes=[ftB[0]])
            S.op("act", lambda e: e.activation(out=gl[0:64, 0:32], in_=pGL[0:64, 0:32], func=AF.Exp), reads=[pGLb], writes=[glB])
            yield 1
            for hf in range(2):
                ac = slice(0, 64) if hf == 0 else slice(128, 192)
                S.op("dve", lambda e, hf=hf, ac=ac: e.tensor_tensor(out=q4(X32)[:, :, hf, ac], in0=tmq[:, :, hf, 0, :], in1=e_ex.rearrange("p (q f e) -> p q f e", q=2, f=2)[:, :, hf, :], op=ALU.mult),
                     reads=[tib, ftB[0]], writes=[x32B])
                S.op("act", lambda e, hf=hf, ac=ac: e.activation(out=q4(Xbf)[:, :, hf, ac], in_=q4(X32)[:, :, hf, ac], func=AF.Identity), reads=[x32B], writes=[xbB])
                S.op("act", lambda e, hf=hf: e.activation(out=q4(Vp)[:, :, hf, hf * 64:(hf + 1) * 64], in_=tmq[:, :, hf, 5, :], func=AF.Identity), reads=[tib], writes=[vpB])
            for (dst, ai, ee, eb) in ((Rt, 4, e_in, 1), (Bt, 2, e_ninv, 2), (Kt, 3, e_ninv, 2), (Bh, 2, e_rev, 3), (Kh, 3, e_rev, 3)):
                S.op("dve", lambda e, dst=dst, ai=ai, ee=ee: e.tensor_tensor(out=h3(dst), in0=arr(ai), in1=h3(ee), op=ALU.mult), reads=[tib, ftB[eb]], writes=[tmB_])
            yield 2
            for q in range(2):
                pT, pTb = cx.psum()
                pT16 = pT[:, :].bitcast(BF16)

                def tr(e, q=q, pT16=pT16):
                    ins = None
                    for hh in range(2):
                        h = 2 * q + hh
                        srcs = (Xbf[:, h, ahc(h)], Rt[:, h * 64:(h + 1) * 64], Bt[:, h * 64:(h + 1) * 64], Kt[:, h * 64:(h + 1) * 64])
                        for ai, src in enumerate(srcs):
                            o0 = (hh * 4 + ai) * 128
                            ins = e.transpose(pT16[0:64, o0:o0 + 128], src, ident)
                    return ins
                S.op("pe", tr, reads=[xbB, tmB_, cB], writes=[pTb])
                S.op("act", lambda e, q=q, pT16=pT16: e.activation(out=FMt[0:64, 2 * q:2 * q + 2, :, :], in_=pT16[0:64, 0:1024].rearrange("p (h a t) -> p h a t", h=2, a=4), func=AF.Identity), reads=[pTb], writes=[fmB])
            yield 3
            pP, pPb = cx.psum()
            for h in range(4):
                pA, pAb = cx.psum()
                rhs = FMt[0:64, h, 0:2, :].rearrange("p a t -> p (a t)")
                mm_group(S, pA[:, 0:256], [(FMt[0:64, h, 2, :], rhs)], reads=[fmB], writes=[pAb])
                mm_group(S, pA[:, 256:512], [(FMt[0:64, h, 3, :], rhs)], reads=[fmB], writes=[pAb])
                S.op("dve", lambda e, h=h, pA=pA: e.tensor_tensor(out=AM[:, h, :, :].rearrange("p a t -> p (a t)"), in0=pA[:, :], in1=M4, op=ALU.mult), reads=[pAb, cB], writes=[amB])
                mm_group(S, pP[:, h * 128:(h + 1) * 128], [(FMt[0:64, h, 0, :], FMt[0:64, h, 2, :])], reads=[fmB], writes=[pPb])
            S.op("dve", lambda e: e.tensor_tensor(out=NPt[0][1].rearrange("p h t -> p (h t)"), in0=pP[:, :], in1=SL4, op=ALU.mult), reads=[pPb, cB], writes=[npB[0][1]])
            yield 4
            pW, pWb = cx.psum()
            for h in range(4):
                mm_group(S, pW[:, h * 64:(h + 1) * 64], [(AM[:, h, 2, :], Vp[:, h, vd(h)])], reads=[amB, vpB], writes=[pWb])
            S.op("act", lambda e: e.activation(out=X32[:, :, 64:128], in_=h3(pW[:, 0:256]), func=AF.Identity), reads=[pWb], writes=[x32B])
            S.op("dve", lambda e: e.tensor_copy(out=Xbf[:, :, 64:128], in_=X32[:, :, 64:128]), reads=[x32B], writes=[xbB])
            yield 5
            for lev in range(7):
                if lev == 0:
                    Nc = [AM[:, h, 0, :] for h in range(4)]
                    Pc = [NPt[0][1][:, h, :] for h in range(4)]
                    nB, pB_ = amB, npB[0][1]
                    wset = 1
                else:
                    cs_ = lev % 2
                    Nc = [NPt[cs_][0][:, h, :] for h in range(4)]
                    Pc = [NPt[cs_][1][:, h, :] for h in range(4)]
                    nB, pB_ = npB[cs_][0], npB[cs_][1]
                    wset = (lev + 1) % 2
                pX, pXb = cx.psum()
                for h in range(4):
                    mm_group(S, pX[:, h * 128:(h + 1) * 128], [(Nc[h], Xbf[:, h, acts(h)])], reads=[nB, xbB], writes=[pXb])
                pXq = pX[:, :].rearrange("p (q f t) -> p q f t", q=2, f=2)
                for hf in range(2):
                    ac = slice(0, 128) if hf == 0 else slice(64, 192)
                    S.op("dve", lambda e, hf=hf, ac=ac, pXq=pXq: e.tensor_tensor(out=q4(X32)[:, :, hf, ac], in0=q4(X32)[:, :, hf, ac], in1=pXq[:, :, hf, :], op=ALU.add), reads=[pXb, x32B], writes=[x32B])
                    S.op("act", lambda e, hf=hf, ac=ac: e.activation(out=q4(Xbf)[:, :, hf, ac], in_=q4(X32)[:, :, hf, ac], func=AF.Identity), reads=[x32B], writes=[xbB])
                if lev < 6:
                    pN, pNb = cx.psum()
                    pQ, pQb = cx.psum()
                    for h in range(4):
                        mm_group(S, pN[:, h * 128:(h + 1) * 128], [(Pc[h], Nc[h])], reads=[nB, pB_], writes=[pNb])
                        mm_group(S, pQ[:, h * 128:(h + 1) * 128], [(Nc[h], Pc[h])], reads=[nB, pB_], writes=[pQb])
                    S.op("act", lambda e, pN=pN, wset=wset: e.activation(out=NPt[wset][0].rearrange("p h t -> p (h t)"), in_=pN[:, :], func=AF.Identity), reads=[pNb], writes=[npB[wset][0]])
                    S.op("dve", lambda e, pQ=pQ, wset=wset: e.tensor_copy(out=NPt[wset][1].rearrange("p h t -> p (h t)"), in_=pQ[:, :]), reads=[pQb], writes=[npB[wset][1]])
                yield 50 + lev
            yield 6
            pR, pRb = cx.psum()
            for h in range(4):
                mm_group(S, pR[0:64, h * 128:(h + 1) * 128], [(Xbf[:, h, ahc(h)], AM[:, h, 1, :])], reads=[xbB, amB], writes=[pRb])
            S.op("dve", lambda e: e.tensor_tensor(out=RhT[0:64, :, :], in0=pR[0:64, :].rearrange("p (h t) -> p h t", h=4), in1=FMt[0:64, :, 1, :], op=ALU.add), reads=[pRb, fmB], writes=[rhB])
            pG, pGb = cx.psum()
            for h in range(4):
                mm_group(S, pG[0:64, h * 64:(h + 1) * 64], [(Xbf[:, h, ahc(h)], Bh[:, h * 64:(h + 1) * 64])], reads=[xbB, tmB_], writes=[pGb])
                mm_group(S, pG[0:64, 256 + h * 64:256 + (h + 1) * 64], [(Bh[:, h * 64:(h + 1) * 64], Xbf[:, h, 64:128]), (Kh[:, h * 64:(h + 1) * 64], Vp[:, h, vd(h)])], reads=[xbB, tmB_, vpB], writes=[pGb])
            S.op("act", lambda e: e.activation(out=GT[0:64, :, :], in_=pG[0:64, 0:256].rearrange("p (h t) -> p h t", h=4), func=AF.Identity), reads=[pGb], writes=[gtB])
            S.op("act", lambda e: e.activation(out=Hh[0:64, :, :], in_=pG[0:64, 256:512].rearrange("p (h t) -> p h t", h=4), func=AF.Identity), reads=[pGb], writes=[hhB])
            yield "split"
            for q in range(2):
                pY, pYb = cx.psum()
                pairs = []
                for hh in range(2):
                    h = 2 * q + hh
                    pairs += [(Spad[0:64, h, :], RhT[0:64, h, :]), (Xbf[:, h, upad(h)], AM[:, h, 1, :]), (Vp[:, h, :], AM[:, h, 3, :])]
                mm_group(S, pY[:, 0:128], pairs, reads=[spB, rhB, xbB, amB, vpB], writes=[pYb])
                S.op("act", lambda e, q=q, pY=pY: e.activation(out=YT[:, q, c * 128:(c + 1) * 128], in_=pY[:, 0:128], func=AF.Identity), reads=[pYb], writes=[yB[(c * 128) // TW]])
            pS, pSb = cx.psum()
            for h in range(4):
                mm_group(S, pS[0:64, h * 64:(h + 1) * 64], [(GT[0:64, h, :], Spad[0:64, h, vd(h)])], reads=[gtB, spB], writes=[pSb])
            for h in range(4):
                S.op("dve", lambda e, h=h, pS=pS: e.scalar_tensor_tensor(out=S32[0:64, h, :], in0=S32[0:64, h, :], scalar=gl[0:64, h * 8:h * 8 + 1], in1=pS[0:64, h * 64:(h + 1) * 64], op0=ALU.mult, op1=ALU.add),
                     reads=[pSb, glB, s32B], writes=[s32B])
            S.op("dve", lambda e: e.tensor_tensor(out=S32[0:64, :, :], in0=S32[0:64, :, :], in1=Hh[0:64, :, :], op=ALU.add), reads=[s32B, hhB], writes=[s32B])
            for hf in range(2):
                S.op("act", lambda e, hf=hf: e.activation(out=q4(Spad)[0:64, :, hf, hf * 64:(hf + 1) * 64], in_=q4(S32)[0:64, :, hf, :], func=AF.Identity), reads=[s32B], writes=[spB])

        def phaseC(s, tt):
            t0 = tt * TW
            if s == 0:
                load_x_and_mix(tt, (0, 1, 4, 5))
            pr, prb = fm_proj(wrkv, s * 128, 128)
            pk, pkb = fm_proj(wrkv, 256 + s * 128, 128)
            pa, pab = fm_proj(wl1, 64, 64)
            ta_, taB = gfm()
            ta_b = ta_[0:64, 0:TW // 2].bitcast(BF16)
            S.op("act", lambda e: e.activation(out=ta_b, in_=pa[0:64, 0:TW], func=AF.Identity), reads=[pab], writes=[taB])
            pa2, pa2b = cx.psum()
            mm_group(S, pa2[:, 0:TW], [(wl2[0:64, 256 + s * 128:256 + (s + 1) * 128], ta_b)], reads=[wB, taB], writes=[pa2b])
            aT, aTB = gfm()
            S.op("act", lambda e: e.activation(out=aT, in_=pa2[:, 0:TW], func=AF.Sigmoid, bias=V("a0", s)), reads=[pa2b, vb], writes=[aTB])
            S.op("dve", lambda e: e.tensor_scalar(out=aT, in0=aT, scalar1=-1.0, scalar2=V("k_a", s), op0=ALU.add, op1=ALU.mult), reads=[aTB, vb], writes=[aTB])
            S.op("dve", lambda e: e.scalar_tensor_tensor(out=aT, in0=aT, scalar=1.0, in1=pk[:, 0:TW], op0=ALU.add, op1=ALU.mult), reads=[aTB, pkb], writes=[aTB])
            rk_, rkB = gfm()
            rkb16 = rk_[:, 0:TW // 2].bitcast(BF16)
            S.op("dve", lambda e: e.scalar_tensor_tensor(out=rkb16, in0=pr[:, 0:TW], scalar=V("r_k", s), in1=aT, op0=ALU.mult, op1=ALU.mult), reads=[prb, aTB, vb], writes=[rkB])
            pbs, pbsb = cx.psum()
            mm_group(S, pbs[:, 0:TW], [(blk, rkb16)], reads=[wB, rkB], writes=[pbsb])
            bon, bonB = gfm()
            pv, pvb = fm_proj(wrkv, 512 + s * 128, 128)
            S.op("act", lambda e: e.activation(out=bon, in_=pv[:, 0:TW], func=AF.Identity), reads=[pvb], writes=[bonB])
            S.op("dve", lambda e: e.tensor_tensor(out=bon, in0=pbs[:, 0:TW], in1=bon, op=ALU.mult), reads=[pbsb, bonB], writes=[bonB])
            pg, pgb = fm_proj(wl1, 128, 128)
            gs_, gsB = gfm()
            gs16 = gs_[:, 0:TW // 2].bitcast(BF16)
            S.op("act", lambda e: e.activation(out=gs16, in_=pg[:, 0:TW], func=AF.Sigmoid), reads=[pgb], writes=[gsB])
            pg2, pg2b = cx.psum()
            mm_group(S, pg2[:, 0:TW], [(wl2[:, 512 + s * 128:512 + (s + 1) * 128], gs16)], reads=[wB, gsB], writes=[pg2b])
            y = YT[:, s, t0:t0 + TW]
            yb16_, y16B = gfm()
            yb16 = yb16_[:, 0:TW // 2].bitcast(BF16)
            ysq16 = yb16_[:, TW // 2:TW].bitcast(BF16)
            S.op("pool", lambda e: e.tensor_copy(out=yb16, in_=y), reads=[yB[tt]], writes=[y16B])
            S.op("act", lambda e: e.activation(out=ysq16, in_=y, func=AF.Square), reads=[yB[tt]], writes=[y16B])
            p1, p1b = cx.psum()
            mm_group(S, p1[:, 0:TW], [(blk, yb16)], reads=[wB, y16B], writes=[p1b])
            p2, p2b = cx.psum()
            mm_group(S, p2[:, 0:TW], [(blk, ysq16)], reads=[wB, y16B], writes=[p2b])
            nm, nmB = gfm()
            S.op("act", lambda e: e.activation(out=nm, in_=p1[:, 0:TW], func=AF.Identity, scale=-1.0 / 64), reads=[p1b], writes=[nmB])
            S.op("dve", lambda e: e.scalar_tensor_tensor(out=gs_, in0=nm, scalar=-1.0, in1=nm, op0=ALU.mult, op1=ALU.mult), reads=[nmB, gsB, pg2b], writes=[gsB])
            S.op("dve", lambda e: e.scalar_tensor_tensor(out=gs_, in0=p2[:, 0:TW], scalar=1.0 / 64, in1=gs_, op0=ALU.mult, op1=ALU.add), reads=[p2b, gsB], writes=[gsB])
            S.op("act", lambda e: e.activation(out=gs_, in_=gs_, func=AF.Sqrt, bias=epsc, scale=1.0), reads=[gsB, wB], writes=[gsB])
            S.op("dve", lambda e: e.reciprocal(out=gs_, in_=gs_), reads=[gsB], writes=[gsB])
            S.op("dve", lambda e: e.tensor_tensor(out=nm, in0=y, in1=nm, op=ALU.add), reads=[yB[tt], nmB], writes=[nmB])
            S.op("dve", lambda e: e.tensor_tensor(out=nm, in0=nm, in1=gs_, op=ALU.mult), reads=[nmB, gsB], writes=[nmB])
            S.op("act", lambda e: e.activation(out=nm, in_=nm, func=AF.Identity, scale=V("lnx_g", s), bias=V("lnx_b", s)), reads=[nmB, vb], writes=[nmB])
            S.op("dve", lambda e: e.tensor_tensor(out=nm, in0=nm, in1=bon, op=ALU.add), reads=[nmB, bonB], writes=[nmB])
            S.op("dve", lambda e: e.tensor_tensor(out=nm, in0=nm, in1=pg2[:, 0:TW], op=ALU.mult), reads=[nmB, pg2b], writes=[nmB])
            S.dma("sp", lambda e: e.dma_start(out=d_o[(t0 // 1024) * 256 + s * 128:(t0 // 1024) * 256 + (s + 1) * 128, t0 % 1024:t0 % 1024 + TW], in_=nm), reads=[nmB], writes=oB)

        run_chunks()
        S.barrier()
        for tt in range(NTW if STOPB[0] >= 99 else 1):
            for s in range(2):
                phaseC(s, tt)
        S.barrier()
        return oB


import numpy as np
from contextlib import ExitStack
import concourse.bass as bass
import concourse.mybir as mybir
from concourse.bass_utils import run_bass_kernel_spmd

GROUPS4 = [[0, 1, 2, 3], [4, 5, 6, 7]]
TOKW = ("w_out", [1024, 1024]), ("w_q", [1024, 1024]), ("w_kv", [1024, 2048]), ("w_o", [1024, 1024]), ("w1", [1024, 4096]), ("w2", [4096, 1024])


def build_fused(upto=9):
    nc = bass.Bass("TRN2", target_bir_lowering=False)

    def din(name, shape):
        return nc.dram_tensor(name, shape, F32, kind="ExternalInput").ap()

    def dint(name, shape):
        return nc.dram_tensor(name, shape, F32).ap()

    x0 = din("xT", [1024, NT + HALO])
    memT = din("memT", [1024, 256])
    L = []
    for i in range(4):
        d = {"memT": memT, "vecs": din("vecs%d" % i, [128, NVEC])}
        for n, shp in TOKW:
            d[n] = din("%s%d" % (n, i), shp)
        if i in (0, 3):
            d["w_in"] = din("w_in%d" % i, [1024, 2048])
        L.append(d)
    at = {"wqkv": din("wqkv", [1024, 2304]), "relb": din("relb", [128, 512]), "oh": din("oh", [len(OH_KEYS), 128, 128])}
    rw = {"wrkv": din("wrkv", [1024, 768]), "wl1": din("wl1", [1024, 256]), "wl2": din("wl2", [128, 768]), "vecs": din("rvecs", [128, NRV]),
          "bc": din("bc", [128, 4, 256]), "blk": din("blk", [128, 128]), "rc": din("rc", [128, NCONST])}
    yT = nc.dram_tensor("yT", [1024, NT], F32, kind="ExternalOutput").ap()
    xs = [dint("xs%d" % i, [1024, NT]) for i in range(1, 4)]
    xg = [dint("xg%d" % i, [4096, NT]) for i in range(1, 3)]
    osrc = [dint("os%d" % i, [8 * 256, 1024]) for i in range(1, 3)]
    og = [dint("og%d" % i, [8 * 4 * 256, 1024]) for i in range(1, 3)]
    hs = dint("hs", [1024, HALO])
    hg = dint("hg", [4096, HALO])
    rw["scr"] = dint("scr", [T, 4, NA * 64])

    with ExitStack() as st:
        cx = Ctx(nc, st, 52520)
        cx.vt = st.enter_context(nc.sbuf_tensor("vt", [128, NVEC], F32))
        S = cx.S

        def gather(src, dst, srcB, name, nchunk=8):
            S.barrier()
            rows = src.shape[0] // nchunk
            bs = []
            for k in range(nchunk):
                b = Buf("%s_%d" % (name, k))
                S.dma("pool", lambda e, k=k: e.collective_compute("AllGather", ALU.bypass, replica_groups=GROUPS4, ins=[src[k * rows:(k + 1) * rows, :].opt()],
                                                                    outs=[dst[k * 4 * rows:(k + 1) * 4 * rows, :].opt()]), reads=srcB, writes=[b], inc=1)
                bs.append(b)
            return bs

        d = dict(L[0]); d.update({"x": x0, "y": xs[0]})
        yB1 = emit_tok(cx, "conv0", d)
        g1 = gather(xs[0], xg[0], yB1, "xg1")

        def early(src_ap, srcB):
            S.dma("sp", lambda e: e.dma_start(out=yT.rearrange("(c p) t -> c p t", p=128) if len(src_ap.shape) == 3 else yT, in_=src_ap), reads=srcB, is_output=True)
            S.finish("sp")
            S.emit(st)
            return nc
        def early_o(ogt, srcB):
            for k in range(2):
                S.dma("sp", lambda e, k=k: e.dma_start(out=yT[:, k * 1024:(k + 1) * 1024], in_=ogt[k * 1024:(k + 1) * 1024, :]), reads=srcB, is_output=True)
            S.finish("sp")
            S.emit(st)
            return nc
        if upto == 1:
            return early(xg[0].rearrange("(c r p) t -> r c p t", c=8, r=4, p=128)[0], g1)
        d = dict(at); d.update({"x_g": xg[0], "xgB": g1, "o": osrc[0]})
        oB1 = emit_attn(cx, d)
        gB1 = gather(osrc[0], og[0], oB1, "og1")
        if upto == 2:
            return early_o(og[0], gB1)
        d = dict(L[1]); d.update({"x": xs[0], "xB": yB1, "h_g": og[0], "h_gB": gB1, "y": xs[1]})
        yB2 = emit_tok(cx, "tail", d)
        if upto == 3:
            return early(xs[1], yB2)
        g2 = gather(xs[1], xg[1], yB2, "xg2")
        d = dict(rw); d.update({"x_g": xg[1], "xgB": g2, "o": osrc[1]})
        oB2 = emit_rwkv(cx, d)
        gB2 = gather(osrc[1], og[1], oB2, "og2")
        if upto == 4:
            return early_o(og[1], gB2)
        d = dict(L[2]); d.update({"x": xs[1], "xB": yB2, "h_g": og[1], "h_gB": gB2, "y": xs[2], "tail_out": hs})
        yB3 = emit_tok(cx, "tail", d)
        if upto == 5:
            return early(xs[2], yB3)
        gh = gather(hs, hg, yB3, "hg", nchunk=1)
        d = dict(L[3]); d.update({"x": xs[2], "xB": yB3, "halo_g": hg, "halo_gB": gh, "y": yT})
        emit_tok(cx, "conv3", d, final=True)
        S.finish("sp")
        S.emit(st)
    return nc


_prog = {}


def rwkv_consts():
    i = np.arange(128)
    ui = (i[:, None] <= i[None, :]).astype(np.float32)
    su = (i[:, None] < i[None, :]).astype(np.float32)
    sl = (i[:, None] > i[None, :]).astype(np.float32)
    return np.ascontiguousarray(np.concatenate([ui, sl, np.eye(128, dtype=np.float32), su, ui, su, ui, sl, sl, sl, sl], axis=1))

UPTO = [9]


def fcols_(v):
    return fcols(v)


def kernel(**inputs):
    inp = {k: np.asarray(v) for k, v in inputs.items()}
    x = np.ascontiguousarray(inp["x"], dtype=np.float32)
    if "fused" not in _prog:
        _prog["fused"] = build_fused(UPTO[0])
    nc = _prog["fused"]
    blk = np.zeros((128, 128), np.float32); blk[:64, :64] = 1; blk[64:, 64:] = 1

    def interleave(win):
        return np.ascontiguousarray(np.concatenate([np.concatenate([win[:, m * 128:(m + 1) * 128], win[:, 1024 + m * 128:1024 + (m + 1) * 128]], axis=1) for m in range(8)], axis=1))
    win = {0: interleave(inp["a_w_in"][0]), 3: interleave(inp["a_w_in"][1])}
    mix_out = {0: inp["a_w_out"][0], 1: inp["b_w_out"][0], 2: inp["c_w_out"][0], 3: inp["a_w_out"][1]}
    Wqkv = inp["b_w_qkv"][0]
    Wr = inp["c_w_rkv"][0]
    wl1 = np.ascontiguousarray(np.concatenate([inp["c_w1"][0], inp["c_a1"][0], inp["c_g1"][0]], axis=1))
    in_maps = []
    for c in range(8):
        b, q = c // 4, c % 4
        h0 = 4 * q
        cs = slice(h0 * 64, h0 * 64 + 256)
        m = {}
        xt = np.zeros((1024, NT + HALO), np.float32)
        if q == 0:
            xt[:, HALO:] = x[b, 0:NT].T
        else:
            xt[:, :] = x[b, q * NT - HALO:(q + 1) * NT].T
        m["xT"] = xt
        m["memT"] = np.ascontiguousarray(inp["mem"][b].T)
        for i in range(4):
            d = {}
            for s in range(3):
                d["ln_g%d" % s] = fcols(inp["ln_g"][i, s]); d["ln_b%d" % s] = fcols(inp["ln_b"][i, s])
            if i in (0, 3):
                j = i // 3
                d["b_val"] = fcols(inp["a_b_in"][j][:1024]); d["b_gate"] = fcols(inp["a_b_in"][j][1024:])
                d["dw"] = np.concatenate([fcols(inp["a_dw"][j][t]) for t in range(31)], axis=1)
                d["dw_b"] = fcols(inp["a_dw_b"][j]); d["a_ln_g"] = fcols(inp["a_ln_g"][j]); d["a_ln_b"] = fcols(inp["a_ln_b"][j])
                d["b_out"] = fcols(inp["a_b_out"][j])
                d["halo_mask"] = np.full((128, 1), 0.0 if q == 0 else 1.0, np.float32)
                m["w_in%d" % i] = win[i]
            hsel = np.zeros((128, 4), np.float32)
            if i in (1, 2):
                hsel[:, q] = 1.0
            elif i == 3 and q > 0:
                hsel[:, q - 1] = 1.0
            d["hsel"] = hsel
            m["vecs%d" % i] = make_vecs(d)
            m["w_out%d" % i] = mix_out[i]
            m["w_q%d" % i] = inp["x_w_q"][i]; m["w_kv%d" % i] = inp["x_w_kv"][i]; m["w_o%d" % i] = inp["x_w_out"][i]
            m["w1%d" % i] = inp["m_w1"][i]; m["w2%d" % i] = inp["m_w2"][i]
        cols = []
        for p in range(2):
            for g in range(3):
                for t in range(3):
                    c0 = g * 3072 + t * 1024 + (h0 + 2 * p) * 64
                    cols.append(Wqkv[:, c0:c0 + 128])
        m["wqkv"] = np.ascontiguousarray(np.concatenate(cols, axis=1))
        relb = np.zeros((128, 512), np.float32)
        for bb in range(32):
            for hh in range(4):
                relb[:, bb * 4 + hh] = inp["rel_bias"][bb, h0 + hh]
        m["relb"] = relb
        m["oh"] = OH_MATS
        vec = np.zeros((128, NRV), np.float32)
        for s in range(6):
            vec[:, s * 8:(s + 1) * 8] = fcols(inp["c_mu"][0][s])
        vec[:, RV["k_a"]:RV["k_a"] + 2] = fcols(inp["c_k_a"][0][cs])
        vec[:, RV["a0"]:RV["a0"] + 2] = fcols(inp["c_a0"][0][cs])
        vec[:, RV["lnx_g"]:RV["lnx_g"] + 2] = fcols(inp["c_lnx_g"][0][cs])
        vec[:, RV["lnx_b"]:RV["lnx_b"] + 2] = fcols(inp["c_lnx_b"][0][cs])
        vec[:, RV["r_k"]:RV["r_k"] + 2] = fcols(inp["c_r_k"][0].reshape(-1)[cs])
        m["rvecs"] = vec
        m["bc"] = np.ascontiguousarray(np.stack([np.tile(inp[n][0][cs][None, :], (128, 1)) for n in ("c_w0", "c_a0", "c_k_k", "c_k_a")], axis=1).astype(np.float32))
        wl2 = np.zeros((128, 768), np.float32)
        wl2[:64, 0:256] = inp["c_w2"][0][:, cs]; wl2[:64, 256:512] = inp["c_a2"][0][:, cs]; wl2[:, 512:768] = inp["c_g2"][0][:, cs]
        m["wl2"] = wl2
        m["wrkv"] = np.ascontiguousarray(np.concatenate([Wr[0][:, cs], Wr[1][:, cs], Wr[2][:, cs]], axis=1))
        m["wl1"] = wl1
        m["blk"] = blk
        m["rc"] = rwkv_consts()
        in_maps.append(m)
    res = run_bass_kernel_spmd(nc, in_maps, core_ids=list(range(8)))
    out = np.empty_like(x)
    for c in range(8):
        b, q = c // 4, c % 4
        out[b, q * NT:(q + 1) * NT] = res.results[c]["yT"].T
    return np.ascontiguousarray(out, dtype=np.float32)
```

```python
import concourse.bass as bass
import concourse.mybir as mybir

ENG = ("pe", "act", "dve", "pool", "sp")


class Buf:
    __slots__ = ("name", "w", "r")

    def __init__(self, name):
        self.name = name
        self.w = None
        self.r = []


class Sched:
    def __init__(self, nc):
        self.nc = nc
        self.streams = {e: [] for e in ENG}
        self.cnt = {e: 0 for e in ENG}
        self.known = {e: {} for e in ENG}
        self.pools = {"hw": ["dmah%d" % i for i in range(24)], "sw": ["dmas%d" % i for i in range(12)], "cc": ["dmac%d" % i for i in range(4)]}
        self.pool_rr = {"hw": 0, "sw": 0, "cc": 0}
        self.dma_use = {k: 0 for p in self.pools.values() for k in p}
        self.sems = {}
        self.out_tokens = []
        self.same_engine_sync = True

    def _need(self, eng, tok):
        if tok is None:
            return
        key, val, src = tok
        if src == eng and not (self.same_engine_sync and eng != "pe"):
            return
        if self.known[eng].get(key, 0) >= val:
            return
        self.known[eng][key] = val
        self.streams[eng].append(("wait", key, val))

    def _deps(self, eng, reads, writes):
        for b in reads:
            self._need(eng, b.w)
        for b in writes:
            self._need(eng, b.w)
            for t in b.r:
                self._need(eng, t)

    def _mark(self, tok, reads, writes):
        for b in reads:
            b.r.append(tok)
        for b in writes:
            b.w = tok
            b.r = []

    def op(self, eng, fn, reads=(), writes=()):
        self._deps(eng, reads, writes)
        self.cnt[eng] += 1
        tok = (eng, self.cnt[eng], eng)
        self.streams[eng].append(("op", fn, (eng, 1)))
        self._mark(tok, reads, writes)
        return tok

    def dma(self, eng, fn, reads=(), writes=(), is_output=False, inc=16):
        self._deps(eng, reads, writes)
        kind = "cc" if inc == 1 else ("sw" if eng == "pool" else "hw")
        pool = self.pools[kind]
        key = pool[self.pool_rr[kind]]
        self.pool_rr[kind] = (self.pool_rr[kind] + 1) % len(pool)
        i = key
        prev = self.dma_use[i]
        if prev:
            if self.known[eng].get(key, 0) < prev:
                self.known[eng][key] = prev
                self.streams[eng].append(("wait", key, prev))
        self.dma_use[i] = prev + inc
        tok = (key, prev + inc, None)
        self.streams[eng].append(("op", fn, (key, inc)))
        self._mark(tok, reads, writes)
        if is_output:
            self.out_tokens.append(tok)
        return tok

    def barrier(self):
        toks = [(e, self.cnt[e], e) for e in ENG if self.cnt[e]]
        toks += [(k, v, None) for k, v in self.dma_use.items() if v]
        for e in ENG:
            for t in toks:
                self._need(e, t)

    def finish(self, eng="sp"):
        for tok in self.out_tokens:
            self._need(eng, tok)

    def emit(self, stack):
        nc = self.nc
        keys = list(ENG) + [k for p in self.pools.values() for k in p]
        for k in keys:
            self.sems[k] = stack.enter_context(nc.semaphore("s_" + k))
        block = stack.enter_context(nc.Block())
        sems = self.sems

        def replay(stream):
            def run(e):
                for item in stream:
                    if item[0] == "wait":
                        e.wait_ge(sems[item[1]], item[2])
                    else:
                        ins = item[1](e)
                        k, amt = item[2]
                        ins.then_inc(sems[k], amt)
            return run

        if self.streams["sp"]:
            block.sync(replay(self.streams["sp"]))
        if self.streams["pe"]:
            block.tensor(replay(self.streams["pe"]))
        if self.streams["act"]:
            block.scalar(replay(self.streams["act"]))
        if self.streams["dve"]:
            block.vector(replay(self.streams["dve"]))
        if self.streams["pool"]:
            block.gpsimd(replay(self.streams["pool"]))


from contextlib import ExitStack
import numpy as np
import concourse.bass as bass
import concourse.mybir as mybir

F32 = mybir.dt.float32
BF16 = mybir.dt.bfloat16
AF = mybir.ActivationFunctionType
ALU = mybir.AluOpType

NT = 2048
TT = 512
NTT = NT // TT
KC = 8
HALO = 32
ALPHA = float((2 * 4) ** 0.25)
LN_EPS = 1e-5
WB = 256
CONVW = 31


class Ten:
    def __init__(self, ap, name, C, N, tw=TT):
        self.ap = ap
        self.C, self.N, self.tw = C, N, tw
        self.nt = (N + tw - 1) // tw
        self.bufs = [[Buf("%s_%d_%d" % (name, c, t)) for t in range(self.nt)] for c in range(C)]

    def b(self, c=None, t0=0, t1=None):
        t1 = self.N if t1 is None else t1
        cs = range(self.C) if c is None else ([c] if isinstance(c, int) else c)
        out = []
        for cc in cs:
            for t in range(t0 // self.tw, (t1 - 1) // self.tw + 1):
                out.append(self.bufs[cc][t])
        return out


class Ctx:
    def __init__(self, nc, stack, arena_words):
        self.nc = nc
        self.S = Sched(nc)
        self.stack = stack
        self.arena = stack.enter_context(nc.sbuf_tensor("arena", [128, arena_words], F32))
        self.off = 0
        self.ps = []
        for i in range(8):
            t = stack.enter_context(nc.psum_tensor("ps%d" % i, [128, 512], F32))
            self.ps.append((t, Buf("ps%d" % i)))
        self.ps_rr = 0
        self.rot = {}

    def alloc(self, words):
        o = self.off
        self.off += words
        assert self.off <= self.arena.shape[1], (self.off, self.arena.shape)
        return o

    def f32(self, off, C, N):
        return self.arena[:, off:off + C * N].rearrange("p (c t) -> p c t", c=C)

    def bf(self, off, C, N):
        assert (C * N) % 2 == 0
        return self.arena[:, off:off + C * N // 2].bitcast(BF16).rearrange("p (c t) -> p c t", c=C)

    def psum(self):
        t, b = self.ps[self.ps_rr]
        self.ps_rr = (self.ps_rr + 1) % 8
        return t, b

    def make_rot(self, name, n, words, mk):
        lst = []
        for i in range(n):
            o = self.alloc(words)
            lst.append((mk(o), Buf("%s%d" % (name, i))))
        self.rot[name] = [lst, 0]

    def get(self, name):
        r = self.rot[name]
        x = r[0][r[1]]
        r[1] = (r[1] + 1) % len(r[0])
        return x


def mm_group(S, out_ap, pairs, reads, writes):
    def fn(e):
        n = len(pairs)
        ins = None
        for i, (l, r) in enumerate(pairs):
            ins = e.matmul(out_ap, lhsT=l, rhs=r, start=(i == 0), stop=(i == n - 1))
        return ins
    return S.op("pe", fn, reads, writes)


def _wload(cx, W2d, row0, c0, nc_, kc):
    S = cx.S
    wt, wb = cx.get("w")
    src = W2d[row0:row0 + kc * 128, c0:c0 + nc_].rearrange("(c p) m -> p c m", p=128)
    S.dma("pool", lambda e, wt=wt, src=src, nc_=nc_: e.dma_start(out=wt[:, 0:kc, 0:nc_], in_=src), writes=[wb])
    return wt, wb, nc_


def linear(cx, xin, W2d, row0, col0, ncols, evac, tiles=None, kc=KC, first=None, nextw=None):
    S = cx.S
    if tiles is None:
        tiles = [(t * TT, TT) for t in range(xin.N // TT)]
    nblk = (ncols + WB - 1) // WB

    def load(bi):
        c0 = col0 + bi * WB
        nc_ = min(WB, col0 + ncols - c0)
        return _wload(cx, W2d, row0, c0, nc_, kc)

    nxt = first if first is not None else load(0)
    pre = None
    for bi in range(nblk):
        wt, wb, nc_ = nxt
        if bi + 1 < nblk:
            nxt = load(bi + 1)
        elif nextw is not None:
            pre = _wload(cx, *nextw)
        for mi in range(nc_ // 128):
            mglob = (bi * WB) // 128 + mi
            for (t0, tn) in tiles:
                pt, pb = cx.psum()
                pairs = [(wt[:, k, mi * 128:(mi + 1) * 128], xin.ap[:, k, t0:t0 + tn]) for k in range(kc)]
                mm_group(S, pt[:, 0:tn], pairs, reads=[wb] + xin.b(None, t0, t0 + tn), writes=[pb])
                evac(mglob, (t0, tn), pt[:, 0:tn], pb)
    return pre


def layernorm(cx, z, gcol, bcol, outs, func=None, tok0=0, ntok=None):
    S = cx.S
    func = AF.Identity if func is None else func
    ntok = z.N if ntok is None else ntok
    ones = cx.ones
    def do_tile(tt):
        t0 = tok0 + tt * TT
        p1, pb1 = cx.psum()
        p2, pb2 = cx.psum()
        zbs, zss = [], []
        for c in range(KC):
            zb, zbb = cx.get("lnb")
            zs, zsb = cx.get("lnb")
            S.op("pool", lambda e, zb=zb, c=c: e.tensor_copy(out=zb, in_=z.ap[:, c, t0:t0 + TT]), reads=z.b(c, t0, t0 + TT), writes=[zbb])
            S.op("act", lambda e, zs=zs, c=c: e.activation(out=zs, in_=z.ap[:, c, t0:t0 + TT], func=AF.Square), reads=z.b(c, t0, t0 + TT), writes=[zsb])
            S.op("pe", lambda e, zb=zb, c=c: e.matmul(p1[:], lhsT=ones, rhs=zb, start=(c == 0), stop=(c == KC - 1)), reads=[zbb, cx.onesb], writes=[pb1])
            S.op("pe", lambda e, zs=zs, c=c: e.matmul(p2[:], lhsT=ones, rhs=zs, start=(c == 0), stop=(c == KC - 1)), reads=[zsb, cx.onesb], writes=[pb2])
        nmean, mb = cx.get("st")
        msq, qb = cx.get("st")
        rstd, rb = cx.get("st")
        S.op("act", lambda e: e.activation(out=nmean, in_=p1[:], func=AF.Identity, scale=-1.0 / 1024), reads=[pb1], writes=[mb])
        S.op("dve", lambda e: e.scalar_tensor_tensor(out=msq, in0=nmean, scalar=-1.0, in1=nmean, op0=ALU.mult, op1=ALU.mult), reads=[mb], writes=[qb])
        S.op("dve", lambda e: e.scalar_tensor_tensor(out=msq, in0=p2[:], scalar=1.0 / 1024, in1=msq, op0=ALU.mult, op1=ALU.add), reads=[pb2, qb], writes=[qb])
        S.op("act", lambda e: e.activation(out=rstd, in_=msq, func=AF.Sqrt, bias=cx.epsc, scale=1.0), reads=[qb], writes=[rb])
        S.op("dve", lambda e: e.reciprocal(out=rstd, in_=rstd), reads=[rb], writes=[rb])
        for c in range(KC):
            t1, tb1 = cx.get("s")
            S.op("dve", lambda e, c=c, t1=t1: e.tensor_tensor(out=t1, in0=z.ap[:, c, t0:t0 + TT], in1=nmean, op=ALU.add), reads=z.b(c, t0, t0 + TT) + [mb], writes=[tb1])
            S.op("pool", lambda e, t1=t1: e.tensor_tensor(out=t1, in0=t1, in1=rstd, op=ALU.mult), reads=[tb1, rb], writes=[tb1])
            for (o, ooff) in outs:
                oo = t0 - tok0 + ooff
                S.op("act", lambda e, c=c, t1=t1, o=o, oo=oo: e.activation(out=o.ap[:, c, oo:oo + TT], in_=t1, func=func, scale=gcol(c), bias=bcol(c)),
                     reads=[tb1, cx.vb], writes=o.b(c, oo, oo + TT))

    for tt in range(ntok // TT):
        do_tile(tt)


def emit_tok(cx, kind, D, final=False):
    conv = kind in ("conv0", "conv3")
    stop = None
    nc = cx.nc
    d_x, d_mem, d_vec = D["x"], D["memT"], D["vecs"]
    d_win = D.get("w_in")
    d_wout, d_wq, d_wkv, d_wo, d_w1, d_w2, d_y = D["w_out"], D["w_q"], D["w_kv"], D["w_o"], D["w1"], D["w2"], D["y"]
    if True:
        cx.off = 0
        cx.rot = {}
        S = cx.S
        vt = cx.vt
        vb = Buf("vecs")
        cx.vb = vb
        S.dma("sp", lambda e: e.dma_start(out=vt[:, 0:NVEC], in_=d_vec), writes=[vb])

        def V(name, c=0):
            i = VIDX[name] + c
            return vt[:, i:i + 1]

        o_x32 = cx.alloc(KC * NT)
        o_xbf = cx.alloc(KC * (NT + HALO) // 2)
        o_r1 = cx.alloc(KC * NT // 2)
        o_a = cx.alloc(KC * NT // 2)
        o_g = cx.alloc(2 * (NT // 2 + HALO))
        x32 = Ten(cx.f32(o_x32, KC, NT), "x32", KC, NT)
        xbf_all = cx.bf(o_xbf, KC, NT + HALO)
        xbf = Ten(xbf_all[:, :, HALO:HALO + NT], "xbf", KC, NT)
        A = Ten(cx.bf(o_a, KC, NT), "A", KC, NT)
        Q = Ten(cx.bf(o_r1, KC, NT), "Q", KC, NT)
        cx.make_rot("w", 2, KC * WB // 2, lambda o: cx.bf(o, KC, WB))
        cx.make_rot("lnb", 4, TT // 2, lambda o: cx.arena[:, o:o + TT // 2].bitcast(BF16))
        cx.make_rot("s", 4, TT, lambda o: cx.arena[:, o:o + TT])
        cx.make_rot("st", 3, TT, lambda o: cx.arena[:, o:o + TT])
        o_ones = cx.alloc(64)
        cx.ones = cx.arena[:, o_ones:o_ones + 64].bitcast(BF16)
        onesb = Buf("ones")
        cx.onesb = onesb
        S.op("dve", lambda e: e.memset(cx.ones, 1.0), writes=[onesb])
        o_eps = cx.alloc(1)
        cx.epsc = cx.arena[:, o_eps:o_eps + 1]
        S.op("dve", lambda e: e.memset(cx.epsc, LN_EPS), writes=[onesb])
        o_kt = cx.alloc(KC * 256 // 2)
        o_v = cx.alloc(2 * 1024 // 2)
        KT = Ten(cx.bf(o_kt, KC, 256), "KT", KC, 256, tw=256)
        Vt = Ten(cx.bf(o_v, 2, 1024), "V", 2, 1024, tw=1024)
        memT = Ten(cx.bf(o_g, KC, 256), "memT", KC, 256, tw=256)
        PT = [(cx.arena[:, o_g + 1024 + i * 512: o_g + 1024 + (i + 1) * 512].bitcast(BF16).rearrange("p (c t) -> p c t", c=2), Buf("PT%d" % i)) for i in range(2)]

        xoff = HALO if kind == "conv0" else 0
        for c in range(KC):
            S.dma("sp", lambda e, c=c: e.dma_start(out=x32.ap[:, c, :], in_=d_x[c * 128:(c + 1) * 128, xoff:xoff + NT]), reads=D.get("xB", []), writes=x32.b(c))

        ydst = [d_y] + list(D.get("y_extra", []))
        yB = D.get("yB", [Buf("ydram")])

        def finish_out():
            for c in range(KC):
                for yd in ydst:
                    S.dma("sp", lambda e, c=c, yd=yd: e.dma_start(out=yd[c * 128:(c + 1) * 128, :], in_=x32.ap[:, c, :]), reads=x32.b(c), writes=yB, is_output=final)
            if "tail_out" in D:
                S.dma("sp", lambda e: e.dma_start(out=D["tail_out"].rearrange("(c p) t -> p c t", p=128), in_=x32.ap[:, :, NT - HALO:NT]), reads=x32.b(None, NT - HALO, NT), writes=yB)

        if stop == "load":
            finish_out()
            return nc

        def resid_prep(bias_name):
            for c in range(KC):
                if bias_name is None:
                    S.op("act", lambda e, c=c: e.activation(out=x32.ap[:, c, :], in_=x32.ap[:, c, :], func=AF.Identity, scale=ALPHA), reads=x32.b(c), writes=x32.b(c))
                else:
                    S.op("act", lambda e, c=c: e.activation(out=x32.ap[:, c, :], in_=x32.ap[:, c, :], func=AF.Identity, scale=ALPHA, bias=V(bias_name, c)), reads=x32.b(c) + [vb], writes=x32.b(c))

        def evac_acc(m, tl, p, pb):
            t0, tn = tl
            S.op("dve", lambda e: e.tensor_tensor(out=x32.ap[:, m, t0:t0 + tn], in0=x32.ap[:, m, t0:t0 + tn], in1=p, op=ALU.add), reads=[pb] + x32.b(m, t0, t0 + tn), writes=x32.b(m, t0, t0 + tn))

        flip = [0]

        def evac_copy_to(dst):
            def ev(m, tl, p, pb):
                t0, tn = tl
                flip[0] ^= 1
                if flip[0]:
                    S.op("act", lambda e: e.activation(out=dst.ap[:, m, t0:t0 + tn], in_=p, func=AF.Identity), reads=[pb], writes=dst.b(m, t0, t0 + tn))
                else:
                    S.op("dve", lambda e: e.tensor_copy(out=dst.ap[:, m, t0:t0 + tn], in_=p), reads=[pb], writes=dst.b(m, t0, t0 + tn))
            return ev

        if conv:
            if kind == "conv0":
                S.dma("pool", lambda e: e.dma_start(out=xbf_all, in_=d_x.rearrange("(c p) t -> p c t", p=128)), reads=D.get("xB", []), writes=xbf.b())
            else:
                S.dma("pool", lambda e: e.dma_start(out=xbf.ap, in_=d_x.rearrange("(c p) t -> p c t", p=128)), reads=D.get("xB", []), writes=xbf.b())
                hg = D["halo_g"]
                hacc, haB = cx.get("st")
                hv = hacc[:, 0:KC * HALO].rearrange("p (c t) -> p c t", c=KC)
                for j in range(4):
                    ht, htB = cx.get("s")
                    htv = ht[:, 0:KC * HALO].rearrange("p (c t) -> p c t", c=KC)
                    S.dma("sp", lambda e, htv=htv, j=j: e.dma_start(out=htv, in_=hg[j * 1024:(j + 1) * 1024, :].rearrange("(c p) t -> p c t", p=128)), reads=D["halo_gB"], writes=[htB])
                    if j == 0:
                        S.op("dve", lambda e, htv=htv: e.tensor_scalar(out=hv, in0=htv, scalar1=V("hsel", 0), scalar2=None, op0=ALU.mult), reads=[htB, vb], writes=[haB])
                    else:
                        S.op("dve", lambda e, htv=htv, j=j: e.scalar_tensor_tensor(out=hv, in0=htv, scalar=V("hsel", j), in1=hv, op0=ALU.mult, op1=ALU.add), reads=[htB, vb, haB], writes=[haB])
                S.op("dve", lambda e: e.tensor_copy(out=xbf_all[:, :, 0:HALO], in_=hv), reads=[haB], writes=xbf.b(None, 0, 1))
            NH = NT // 2
            CO = Ten(cx.f32(o_r1, KC, NH), "CO", KC, NH)
            glus = [(cx.arena[:, o_g + i * (NH + HALO): o_g + (i + 1) * (NH + HALO)], Buf("glu%d" % i)) for i in range(2)]
            gi = 0
            for half in range(2):
                base = half * NH
                tiles = [(base, HALO), (base + HALO, TT), (base + HALO + TT, TT)]
                for m in range(KC):
                    glu, gb = glus[gi]
                    gi ^= 1
                    pend = {}

                    def ev(mg, tl, p, pb, glu=glu, gb=gb, m=m, pend=pend, base=base, half=half):
                        t0, tn = tl
                        which = mg % 2
                        if which == 0:
                            pend[t0] = (p, pb)
                            return
                        pv, pvb = pend.pop(t0)
                        sg, sb = cx.get("s")
                        S.op("act", lambda e: e.activation(out=sg[:, 0:tn], in_=p, func=AF.Sigmoid, bias=V("b_gate", m)), reads=[pb, vb], writes=[sb])
                        g0 = t0 - base
                        S.op("dve", lambda e: e.scalar_tensor_tensor(out=glu[:, g0:g0 + tn], in0=pv, scalar=V("b_val", m), in1=sg[:, 0:tn], op0=ALU.add, op1=ALU.mult), reads=[pvb, sb, vb], writes=[gb])
                        if g0 == 0 and half == 0:
                            S.op("dve", lambda e: e.tensor_scalar(out=glu[:, 0:HALO], in0=glu[:, 0:HALO], scalar1=V("halo_mask"), scalar2=None, op0=ALU.mult), reads=[gb, vb], writes=[gb])

                    class XH:
                        ap = xbf_all
                        N = NT + HALO

                        @staticmethod
                        def b(c, t0, t1):
                            return xbf.b(None, max(0, t0 - HALO), max(1, t1 - HALO))
                    linear(cx, XH, d_win, 0, m * 256, 256, ev, tiles=tiles)
                    if m % 2 == 0:
                        for j in range(CONVW):
                            src = glu[:, 2 + j: 2 + j + NH]
                            if j == 0:
                                S.op("dve", lambda e, src=src, m=m: e.tensor_scalar(out=CO.ap[:, m, :], in0=src, scalar1=V("dw", m), scalar2=V("dw_b", m), op0=ALU.mult, op1=ALU.add), reads=[gb, vb], writes=CO.b(m))
                            else:
                                S.op("dve", lambda e, src=src, m=m, j=j: e.scalar_tensor_tensor(out=CO.ap[:, m, :], in0=src, scalar=V("dw", j * KC + m), in1=CO.ap[:, m, :], op0=ALU.mult, op1=ALU.add), reads=[gb, vb] + CO.b(m), writes=CO.b(m))
                    else:
                        for sub in range(NH // TT):
                            s0 = sub * TT
                            for j in range(CONVW):
                                src = glu[:, 2 + j + s0: 2 + j + s0 + TT]
                                dst = CO.ap[:, m, s0:s0 + TT]
                                if j == 0:
                                    S.op("act", lambda e, src=src, dst=dst, m=m: e.activation(out=dst, in_=src, func=AF.Identity, scale=V("dw", m), bias=V("dw_b", m)), reads=[gb, vb], writes=CO.b(m, s0, s0 + TT))
                                else:
                                    tmp, tmb = cx.get("s")
                                    S.op("act", lambda e, src=src, tmp=tmp, m=m, j=j: e.activation(out=tmp, in_=src, func=AF.Identity, scale=V("dw", j * KC + m)), reads=[gb, vb], writes=[tmb])
                                    S.op("pool", lambda e, dst=dst, tmp=tmp: e.tensor_tensor(out=dst, in0=dst, in1=tmp, op=ALU.add), reads=[tmb] + CO.b(m, s0, s0 + TT), writes=CO.b(m, s0, s0 + TT))
                layernorm(cx, CO, lambda c: V("a_ln_g", c), lambda c: V("a_ln_b", c), [(A, half * NH)], func=AF.Silu)
            S.barrier()
            if stop == "conv":
                finish_out()
                return nc
            resid_prep("b_out")
        else:
            S.dma("pool", lambda e: e.dma_start(out=xbf.ap, in_=d_x.rearrange("(c p) t -> p c t", p=128)), reads=D.get("xB", []), writes=xbf.b())
            og = D["h_g"]
            for tt in range(NTT):
                for c in range(KC):
                    acc, accB = cx.get("st")
                    for j in range(4):
                        ht, htB = cx.get("s")
                        row0 = (2 * j + tt // 2) * 1024 + (c // 2) * 256 + (c % 2) * 128
                        off = (tt % 2) * TT
                        S.dma("sp", lambda e, ht=ht, row0=row0, off=off: e.dma_start(out=ht, in_=og[row0:row0 + 128, off:off + TT]), reads=D["h_gB"], writes=[htB])
                        if j == 0:
                            S.op("dve", lambda e, ht=ht, acc=acc: e.tensor_scalar(out=acc, in0=ht, scalar1=V("hsel", 0), scalar2=None, op0=ALU.mult), reads=[htB, vb], writes=[accB])
                        elif j < 3:
                            S.op("dve", lambda e, ht=ht, acc=acc, j=j: e.scalar_tensor_tensor(out=acc, in0=ht, scalar=V("hsel", j), in1=acc, op0=ALU.mult, op1=ALU.add), reads=[htB, vb, accB], writes=[accB])
                        else:
                            S.op("dve", lambda e, ht=ht, acc=acc, c=c, tt=tt: e.scalar_tensor_tensor(out=A.ap[:, c, tt * TT:(tt + 1) * TT], in0=ht, scalar=V("hsel", 3), in1=acc, op0=ALU.mult, op1=ALU.add),
                                 reads=[htB, vb, accB], writes=A.b(c, tt * TT, (tt + 1) * TT))
            resid_prep(None)
        if stop == "prep":
            finish_out()
            return nc
        linear(cx, A, d_wout, 0, 0, 1024, evac_acc)
        if stop == "wout":
            finish_out()
            return nc
        if stop == "ln0a":
            layernorm(cx, x32, lambda c: V("ln_g0", c), lambda c: V("ln_b0", c), [(x32, 0)])
        elif stop == "ln0b":
            layernorm(cx, x32, lambda c: V("ln_g0", c), lambda c: V("ln_b0", c), [(xbf, 0)])
        elif stop == "ln0c":
            layernorm(cx, x32, lambda c: V("ln_g0", c), lambda c: V("ln_b0", c), [(x32, 0)], ntok=512)
        else:
            layernorm(cx, x32, lambda c: V("ln_g0", c), lambda c: V("ln_b0", c), [(x32, 0), (xbf, 0)])
        S.barrier()
        if stop in ("ln0", "ln0a", "ln0b", "ln0c"):
            finish_out()
            return nc

        S.dma("pool", lambda e: e.dma_start(out=memT.ap, in_=d_mem.rearrange("(c p) t -> p c t", p=128)), writes=memT.b())
        linear(cx, memT, d_wkv, 0, 0, 1024, evac_copy_to(KT), tiles=[(0, 256)])
        for eb in range(4):
            wt, wb = cx.get("w")
            src = d_wkv[:, 1024 + eb * WB: 1024 + (eb + 1) * WB].rearrange("(c p) m -> p c m", p=128)
            S.dma("pool", lambda e, wt=wt, src=src: e.dma_start(out=wt, in_=src), writes=[wb])
            for mch in range(2):
                pt, pb = cx.psum()
                pairs = [(memT.ap[:, k, mch * 128:(mch + 1) * 128], wt[:, k, :]) for k in range(KC)]
                mm_group(S, pt[:, 0:WB], pairs, reads=[wb] + memT.b(), writes=[pb])
                S.op("act", lambda e, pt=pt, mch=mch, eb=eb: e.activation(out=Vt.ap[:, mch, eb * WB:(eb + 1) * WB], in_=pt[:, 0:WB], func=AF.Identity), reads=[pb], writes=Vt.b(mch))
        linear(cx, xbf, d_wq, 0, 0, 1024, evac_copy_to(Q))
        resid_prep(None)
        O = A
        pi = 0
        for tt in range(NTT):
            t0 = tt * TT
            for h in range(4):
                P, Pb = PT[pi]
                pi ^= 1
                for mch in range(2):
                    pl, plb = cx.psum()
                    pairs = [(KT.ap[:, 2 * h + ec, mch * 128:(mch + 1) * 128], Q.ap[:, 2 * h + ec, t0:t0 + TT]) for ec in range(2)]
                    mm_group(S, pl[:], pairs, reads=KT.b([2 * h, 2 * h + 1]) + Q.b([2 * h, 2 * h + 1], t0, t0 + TT), writes=[plb])
                    S.op("act", lambda e, pl=pl, P=P, mch=mch: e.activation(out=P[:, mch, :], in_=pl[:], func=AF.Exp, scale=1.0 / 16), reads=[plb], writes=[Pb])
                pd, pdb = cx.psum()
                mm_group(S, pd[:], [(cx.ones, P[:, 0, :]), (cx.ones, P[:, 1, :])], reads=[Pb, onesb], writes=[pdb])
                rd, rdb = cx.get("s")
                S.op("dve", lambda e, rd=rd, pd=pd: e.reciprocal(out=rd, in_=pd[:]), reads=[pdb], writes=[rdb])
                for ec in range(2):
                    po, pob = cx.psum()
                    ch = 2 * h + ec
                    pairs = [(Vt.ap[:, mch, ch * 128:(ch + 1) * 128], P[:, mch, :]) for mch in range(2)]
                    mm_group(S, po[:], pairs, reads=Vt.b() + [Pb], writes=[pob])
                    S.op("dve", lambda e, po=po, rd=rd, ch=ch, t0=t0: e.tensor_tensor(out=O.ap[:, ch, t0:t0 + TT], in0=po[:], in1=rd, op=ALU.mult), reads=[pob, rdb], writes=O.b(ch, t0, t0 + TT))
        linear(cx, O, d_wo, 0, 0, 1024, evac_acc)
        layernorm(cx, x32, lambda c: V("ln_g1", c), lambda c: V("ln_b1", c), [(x32, 0), (xbf, 0)])
        S.barrier()
        if stop == "xattn":
            finish_out()
            return nc

        resid_prep(None)
        H = A

        def evac_relu2(m, tl, p, pb):
            t0, tn = tl
            r, rb_ = cx.get("s")
            S.op("act", lambda e: e.activation(out=r[:, 0:tn], in_=p, func=AF.Relu), reads=[pb], writes=[rb_])
            S.op("pool", lambda e: e.tensor_tensor(out=H.ap[:, m, t0:t0 + tn], in0=r[:, 0:tn], in1=r[:, 0:tn], op=ALU.mult), reads=[rb_], writes=H.b(m, t0, t0 + tn))

        seq = []
        for fq in range(4):
            seq += [(xbf, d_w1, 0, fq * 1024, evac_relu2), (H, d_w2, fq * 1024, 0, evac_acc)]
        pre = None
        for i, (xin_, W_, r0_, c0_, ev_) in enumerate(seq):
            nw = None
            if i + 1 < len(seq):
                nw = (seq[i + 1][1], seq[i + 1][2], seq[i + 1][3], WB, KC)
            pre = linear(cx, xin_, W_, r0_, c0_, 1024, ev_, first=pre, nextw=nw)
        layernorm(cx, x32, lambda c: V("ln_g2", c), lambda c: V("ln_b2", c), [(x32, 0)])
        finish_out()
        return yB


VNAMES = [("ln_g0", 8), ("ln_b0", 8), ("ln_g1", 8), ("ln_b1", 8), ("ln_g2", 8), ("ln_b2", 8),
          ("b_val", 8), ("b_gate", 8), ("dw", 31 * 8), ("dw_b", 8), ("a_ln_g", 8), ("a_ln_b", 8), ("b_out", 8), ("halo_mask", 1), ("hsel", 4)]
VIDX = {}
_o = 0
for _n, _k in VNAMES:
    VIDX[_n] = _o
    _o += _k
NVEC = _o


def fcols(v):
    return np.ascontiguousarray(np.asarray(v, np.float32).reshape(-1, 128).T)


def make_vecs(d):
    out = np.zeros((128, NVEC), np.float32)
    for n, k in VNAMES:
        if n in d:
            out[:, VIDX[n]:VIDX[n] + k] = d[n]
    return out


from contextlib import ExitStack
import math
import numpy as np
import concourse.bass as bass
import concourse.mybir as mybir

SEQ = 8192
SB = 2048
NSB = SEQ // SB
GROUPS = ((128, 1), (512, 4), (2048, 16))
DILS = (1, 4, 16)


def t5_bucket(n):
    n = max(int(n), 0)
    if n < 16:
        return n
    v = np.float32(16) + (np.log(np.float32(n) / np.float32(16)) / np.float32(math.log(2048 / 16)) * np.float32(16)).astype(np.int32)
    return int(min(int(v), 31))


def oh_tables():
    keys, mats = [], []
    for g, d in enumerate(DILS):
        for chunk in range(2):
            q = np.arange(128)[None, :]
            k = np.arange(128)[:, None]
            rel = q + 128 - k if chunk == 0 else q - k
            valid = (rel >= 0) & (rel <= 128)
            bk = np.vectorize(t5_bucket)(np.maximum(rel, 0) * d)
            for b in range(32):
                m = (valid & (bk == b)).astype(np.float32)
                if m.any():
                    keys.append((g, chunk, b))
                    mats.append(m)
    return keys, np.stack(mats)


OH_KEYS, OH_MATS = oh_tables()


def emit_attn(cx, D):
    nc = cx.nc
    st = cx.stack
    d_x, d_w, d_rb, d_oh, d_o = D["x_g"], D["wqkv"], D["relb"], D["oh"], D["o"]
    if True:
        cx.off = 0
        cx.rot = {}
        S = cx.S
        A = cx.arena
        cx.D = D

        def bf2(off, n):
            return A[:, off:off + n // 2].bitcast(BF16)

        o_erb = cx.alloc(512)
        erb = A[:, o_erb:o_erb + 512]
        erbB = Buf("erb")
        S.dma("sp", lambda e: e.dma_start(out=erb, in_=d_rb), writes=[erbB])
        S.op("act", lambda e: e.activation(out=erb, in_=erb, func=AF.Exp), reads=[erbB], writes=[erbB])
        o_tab = cx.alloc(12 * 256 + 128)
        tabB = Buf("tab")
        S.op("dve", lambda e: e.memset(A[:, o_tab:o_tab + 12 * 256 + 128], 0.0), writes=[tabB])
        zero_tab = A[:, o_tab + 12 * 256: o_tab + 12 * 256 + 128]

        def tab(g, hh, chunk):
            o = o_tab + (g * 4 + hh) * 256 + chunk * 128
            return A[:, o:o + 128]
        cx.make_rot("oh", 3, 128, lambda o: A[:, o:o + 128])
        cx.hbase = None
        return_nc = nc
        cx.tab = tab
        cx.zero_tab = zero_tab
        cx.erb = erb
        cx.erbB = erbB
        cx.tabB = tabB
        return _attn_body(cx, nc, st, d_x, d_w, d_oh, d_o, bf2)


def _attn_body(cx, nc, st, d_x, d_w, d_oh, d_o, bf2):
    S = cx.S
    A = cx.arena
    tab, zero_tab, erb, erbB, tabB = cx.tab, cx.zero_tab, cx.erb, cx.erbB, cx.tabB
    for i, (g, chunk, b) in enumerate(OH_KEYS):
        oh, ohB = cx.get("oh")
        S.dma("sp", lambda e, oh=oh, i=i: e.dma_start(out=oh, in_=d_oh[i]), writes=[ohB])
        for hh in range(4):
            t = tab(g, hh, chunk)
            S.op("dve", lambda e, t=t, oh=oh, b=b, hh=hh: e.scalar_tensor_tensor(out=t, in0=oh, scalar=erb[:, b * 4 + hh: b * 4 + hh + 1], in1=t, op0=ALU.mult, op1=ALU.add),
                 reads=[ohB, erbB, tabB], writes=[tabB])

    o_x = cx.alloc(KC * SB // 2)
    xbf = bf2(o_x, KC * SB).rearrange("p (c t) -> p c t", c=KC)
    xB = Buf("x")
    o_w = cx.alloc(KC * 1152 // 2)
    wt = bf2(o_w, KC * 1152).rearrange("p (c m) -> p c m", c=KC)
    wB = Buf("w")
    o_q = cx.alloc(3 * SB // 2)
    QT = bf2(o_q, 3 * SB).rearrange("p (g t) -> p g t", g=3)
    qB = [Buf("q%d" % g) for g in range(3)]
    o_k = cx.alloc(2 * 3 * SB // 2)
    KT = bf2(o_k, 2 * 3 * SB).rearrange("p (s g t) -> p s g t", s=2, g=3)
    kB = [[Buf("k%d_%d" % (s, g)) for g in range(3)] for s in range(2)]
    o_v = cx.alloc(2 * 3 * 16 * 128 // 2)
    VT = bf2(o_v, 2 * 3 * 16 * 128).rearrange("p (s g j e) -> p s g j e", s=2, g=3, j=16)
    vB = [[Buf("v%d_%d" % (s, g)) for g in range(3)] for s in range(2)]
    o_ao = cx.alloc(2 * SB)
    accO = A[:, o_ao:o_ao + 2 * SB].rearrange("p (h t) -> p h t", h=2)
    o_ad = cx.alloc(2 * SB)
    accD = A[:, o_ad:o_ad + 2 * SB].rearrange("p (h t) -> p h t", h=2)
    aB = [Buf("acc%d" % h) for h in range(2)]
    cx.make_rot("E", 2, 512, lambda o: A[:, o:o + 512])
    cx.make_rot("P", 2, 256, lambda o: A[:, o:o + 256].bitcast(BF16))
    o_ones = cx.alloc(32)
    ones64 = A[:, o_ones:o_ones + 32].bitcast(BF16)
    onesB = Buf("ones")
    S.op("dve", lambda e: e.memset(ones64, 1.0), writes=[onesB])

    def tok_view(ap2d, d):
        return ap2d.rearrange("p (n r) -> p r n", r=d)

    flip = [0]
    oB = [Buf("o_dram")]
    for ps_ in range(2):
        S.dma("pool", lambda e, ps_=ps_: e.dma_start(out=wt, in_=d_w[:, ps_ * 1152:(ps_ + 1) * 1152].rearrange("(c p) m -> p c m", p=128)), writes=[wB])
        for sb in range(NSB):
            slot = sb % 2
            S.dma("pool", lambda e, sb=sb: e.dma_start(out=xbf, in_=d_x.rearrange("(c r p) t -> r p c t", c=8, r=4, p=128)[sb]), reads=cx.D["xgB"], writes=[xB])
            for g in range(3):
                d = DILS[g]
                for t in range(2):
                    for tt in range(SB // 512):
                        pt, pb = cx.psum()
                        blk = g * 3 + t
                        pairs = [(wt[:, k, blk * 128:(blk + 1) * 128], xbf[:, k, tt * 512:(tt + 1) * 512]) for k in range(KC)]
                        mm_group(S, pt[:], pairs, reads=[wB, xB], writes=[pb])
                        dst = QT[:, g, tt * 512:(tt + 1) * 512] if t == 0 else KT[:, slot, g, tt * 512:(tt + 1) * 512]
                        dB = qB[g] if t == 0 else kB[slot][g]
                        flip[0] ^= 1
                        if flip[0]:
                            S.op("act", lambda e, dst=dst, pt=pt: e.activation(out=dst, in_=pt[:], func=AF.Identity), reads=[pb], writes=[dB])
                        else:
                            S.op("dve", lambda e, dst=dst, pt=pt: e.tensor_copy(out=dst, in_=pt[:]), reads=[pb], writes=[dB])
                nper = 16 // d
                for j4 in range(4):
                    pt, pb = cx.psum()
                    for jj in range(4):
                        j = j4 * 4 + jj
                        r, n_ = j // nper, j % nper
                        blk = g * 3 + 2
                        pairs = []
                        for k in range(KC):
                            xv = xbf[:, k, :].rearrange("p (n r) -> p r n", r=d)[:, r, n_ * 128:(n_ + 1) * 128]
                            pairs.append((xv, wt[:, k, blk * 128:(blk + 1) * 128]))
                        mm_group(S, pt[:, jj * 128:(jj + 1) * 128], pairs, reads=[wB, xB], writes=[pb])
                    dst = VT[:, slot, g, j4 * 4:(j4 + 1) * 4, :]
                    S.op("act", lambda e, dst=dst, pt=pt: e.activation(out=dst, in_=pt[:].rearrange("p (j e) -> p j e", j=4), func=AF.Identity), reads=[pb], writes=[vB[slot][g]])
            for g in range(3):
                d = DILS[g]
                nper = 16 // d
                for hh in range(2):
                    hp = slice(hh * 64, (hh + 1) * 64)
                    habs = ps_ * 2 + hh
                    qv = QT[hp, g, :].rearrange("p (n r) -> p r n", r=d)
                    kcur = KT[hp, slot, g, :].rearrange("p (n r) -> p r n", r=d)
                    kprv = KT[hp, 1 - slot, g, :].rearrange("p (n r) -> p r n", r=d)
                    aOv = accO[0:64, hh, :].rearrange("p (n r) -> p r n", r=d)
                    aDv = accD[0:64, hh, :].rearrange("p (n r) -> p r n", r=d)
                    for u2 in range(8):
                        pl, plb = cx.psum()
                        units = []
                        for uu in range(2):
                            j = u2 * 2 + uu
                            r, n_ = j // nper, j % nper
                            qs = qv[:, r, n_ * 128:(n_ + 1) * 128]
                            first = (sb == 0 and n_ == 0)
                            if n_ > 0:
                                kp = kcur[:, r, (n_ - 1) * 128:n_ * 128]
                                vp = VT[:, slot, g, j - 1, hp]
                                rp = [kB[slot][g], vB[slot][g]]
                            elif first:
                                kp = kcur[:, r, 0:128]
                                vp = VT[:, slot, g, j, hp]
                                rp = [kB[slot][g], vB[slot][g]]
                            else:
                                kp = kprv[:, r, (nper - 1) * 128:nper * 128]
                                vp = VT[:, 1 - slot, g, r * nper + nper - 1, hp]
                                rp = [kB[1 - slot][g], vB[1 - slot][g]]
                            kc_ = kcur[:, r, n_ * 128:(n_ + 1) * 128]
                            vc = VT[:, slot, g, j, hp]
                            units.append((r, n_, first, vp, vc, rp))
                            mm_group(S, pl[:, uu * 256:uu * 256 + 128], [(kp, qs)], reads=[qB[g], kB[slot][g]] + rp, writes=[plb])
                            mm_group(S, pl[:, uu * 256 + 128:uu * 256 + 256], [(kc_, qs)], reads=[qB[g], kB[slot][g]], writes=[plb])
                        E, EB = cx.get("E")
                        S.op("act", lambda e, E=E, pl=pl: e.activation(out=E, in_=pl[:], func=AF.Exp, scale=0.125), reads=[plb], writes=[EB])
                        po, pob = cx.psum()
                        for uu in range(2):
                            r, n_, first, vp, vc, rp = units[uu]
                            P, PB = cx.get("P")
                            t0_ = zero_tab if first else cx.tab(g, habs, 0)
                            S.op("dve", lambda e, P=P, E=E, uu=uu, t0_=t0_: e.tensor_tensor(out=P[:, 0:128], in0=E[:, uu * 256:uu * 256 + 128], in1=t0_, op=ALU.mult), reads=[EB, tabB], writes=[PB])
                            t1_ = cx.tab(g, habs, 1)
                            S.op("dve", lambda e, P=P, E=E, uu=uu, t1_=t1_: e.tensor_tensor(out=P[:, 128:256], in0=E[:, uu * 256 + 128:uu * 256 + 256], in1=t1_, op=ALU.mult), reads=[EB, tabB], writes=[PB])
                            mm_group(S, po[0:64, uu * 128:(uu + 1) * 128], [(vp, P[:, 0:128]), (vc, P[:, 128:256])], reads=[PB, vB[slot][g]] + rp, writes=[pob])
                            mm_group(S, po[0:64, 256 + uu * 128:256 + (uu + 1) * 128], [(ones64, P[:, 0:128]), (ones64, P[:, 128:256])], reads=[PB, onesB], writes=[pob])
                        for uu in range(2):
                            r, n_ = units[uu][0], units[uu][1]
                            oO = aOv[:, r, n_ * 128:(n_ + 1) * 128]
                            oD = aDv[:, r, n_ * 128:(n_ + 1) * 128]
                            so = po[0:64, uu * 128:(uu + 1) * 128]
                            sd = po[0:64, 256 + uu * 128:256 + (uu + 1) * 128]
                            if g == 0:
                                S.op("act", lambda e, oO=oO, so=so: e.activation(out=oO, in_=so, func=AF.Identity), reads=[pob], writes=[aB[hh]])
                                S.op("act", lambda e, oD=oD, sd=sd: e.activation(out=oD, in_=sd, func=AF.Identity), reads=[pob], writes=[aB[hh]])
                            else:
                                S.op("dve", lambda e, oO=oO, so=so: e.tensor_tensor(out=oO, in0=oO, in1=so, op=ALU.add), reads=[pob, aB[hh]], writes=[aB[hh]])
                                S.op("dve", lambda e, oD=oD, sd=sd: e.tensor_tensor(out=oD, in0=oD, in1=sd, op=ALU.add), reads=[pob, aB[hh]], writes=[aB[hh]])
            for hh in range(2):
                S.op("dve", lambda e, hh=hh: e.reciprocal(out=accD[0:64, hh, :], in_=accD[0:64, hh, :]), reads=[aB[hh]], writes=[aB[hh]])
                S.op("dve", lambda e, hh=hh: e.tensor_tensor(out=accO[0:64, hh, :], in0=accO[0:64, hh, :], in1=accD[0:64, hh, :], op=ALU.mult), reads=[aB[hh]], writes=[aB[hh]])
                row = (ps_ * 2 + hh) * 64
                for kk in range(2):
                    r0 = (2 * sb + kk) * 256 + row
                    S.dma("sp", lambda e, hh=hh, r0=r0, kk=kk: e.dma_start(out=d_o[r0:r0 + 64, :], in_=accO[0:64, hh, kk * 1024:(kk + 1) * 1024]), reads=[aB[hh]], writes=oB)
    return oB


from contextlib import ExitStack
import math
import numpy as np
import concourse.bass as bass
import concourse.mybir as mybir

T = 8192
TW = 256
NTW = T // TW
SBLK = 128
NA = 6
NCONST = 1408
STOPB = [99]
GN_EPS = 64e-5
RV = {"mu": 0, "k_a": 48, "a0": 50, "lnx_g": 52, "lnx_b": 54, "r_k": 56}
NRV = 58


def emit_rwkv(cx, D):
    nc = cx.nc
    st = cx.stack
    d_xg, d_wrkv, d_wl1, d_wl2, d_vec, d_bc, d_blk, d_o, d_scr = D["x_g"], D["wrkv"], D["wl1"], D["wl2"], D["vecs"], D["bc"], D["blk"], D["o"], D["scr"]
    xgB = D["xgB"]
    NTC = 2048
    if True:
        cx.off = 0
        cx.rot = {}
        S = cx.S
        S.same_engine_sync = True
        A = cx.arena
        oB = [Buf("o_dram")]

        def f32v(off, n):
            return A[:, off:off + n]

        def bfv(off, n):
            return A[:, off:off + n // 2].bitcast(BF16)

        vt = cx.vt
        vb = Buf("vecs")
        S.dma("sp", lambda e: e.dma_start(out=vt[:, 0:NRV], in_=d_vec), writes=[vb])

        def V(name, c=0):
            i = RV[name] + c
            return vt[:, i:i + 1]

        o_y = cx.alloc(2 * T)
        YT = f32v(o_y, 2 * T).rearrange("p (s t) -> p s t", s=2)
        yB = [Buf("Y%d" % i) for i in range(NTW)]
        o_wrkv = cx.alloc(KC * 768 // 2)
        wrkv = bfv(o_wrkv, KC * 768).rearrange("p (c m) -> p c m", c=KC)
        o_wl1 = cx.alloc(KC * 256 // 2)
        wl1 = bfv(o_wl1, KC * 256).rearrange("p (c m) -> p c m", c=KC)
        o_wrkv2 = cx.alloc(KC * 768 // 2)
        wrkv2 = bfv(o_wrkv2, KC * 768).rearrange("p (c m) -> p c m", c=KC)
        o_wl12 = cx.alloc(KC * 256 // 2)
        wl12 = bfv(o_wl12, KC * 256).rearrange("p (c m) -> p c m", c=KC)
        o_wl2 = cx.alloc(768 // 2)
        wl2 = bfv(o_wl2, 768)
        o_blk = cx.alloc(64)
        blk = bfv(o_blk, 128)
        o_bc = cx.alloc(4 * 256)
        bc = f32v(o_bc, 1024).rearrange("p (a m) -> p a m", a=4)
        wB = Buf("weights")
        S.dma("pool", lambda e: e.dma_start(out=wrkv, in_=d_wrkv.rearrange("(c p) m -> p c m", p=128)), writes=[wB])
        S.dma("pool", lambda e: e.dma_start(out=wl1, in_=d_wl1.rearrange("(c p) m -> p c m", p=128)), writes=[wB])
        S.dma("pool", lambda e: e.dma_start(out=wl2, in_=d_wl2), writes=[wB])
        S.dma("pool", lambda e: e.dma_start(out=blk, in_=d_blk), writes=[wB])
        S.dma("sp", lambda e: e.dma_start(out=bc, in_=d_bc), writes=[wB])
        w2B = Buf("weights2")
        for k in range(KC):
            for (wt_, wt2_, c0, c1, s_) in ((wrkv, wrkv2, 0, 256, 0), (wrkv, wrkv2, 256, 512, 1), (wrkv, wrkv2, 512, 768, 2), (wl1, wl12, 0, 64, 3), (wl1, wl12, 64, 128, 4), (wl1, wl12, 128, 256, 5)):
                S.op("act", lambda e, k=k, wt_=wt_, wt2_=wt2_, c0=c0, c1=c1, s_=s_: e.activation(out=wt2_[:, k, c0:c1], in_=wt_[:, k, c0:c1], func=AF.Identity, scale=V("mu", s_ * 8 + k)), reads=[wB, vb], writes=[w2B])
        o_eps = cx.alloc(2)
        epsc = f32v(o_eps, 1)
        S.op("dve", lambda e: e.memset(epsc, GN_EPS), writes=[wB])
        REG = 25800
        o_reg = cx.alloc(REG)
        ro = [o_reg]

        def ralloc(n):
            o = ro[0]
            ro[0] += n
            assert ro[0] <= o_reg + REG, (ro[0] - o_reg)
            return o
        xs = f32v(ralloc(KC * (TW + 1)), KC * (TW + 1)).rearrange("p (c t) -> p c t", c=KC)
        xsB = Buf("xs")
        xxb = bfv(ralloc(KC * TW // 2), KC * TW).rearrange("p (c t) -> p c t", c=KC)
        xb = bfv(ralloc(KC * TW // 2), KC * TW).rearrange("p (c t) -> p c t", c=KC)
        xxB = Buf("xx")
        _tmo = [ralloc(4 * NA * 64) for _ in range(2)]
        TMt = [f32v(o, 4 * NA * 64).rearrange("p (h a e) -> p h a e", h=4, a=NA) for o in _tmo]
        TMf = [f32v(o, 4 * NA * 64) for o in _tmo]
        tmB = [Buf("TM%d" % i) for i in range(2)]
        tq = [(f32v(ralloc(256), 256), Buf("tq%d" % i)) for i in range(8)]
        fm = [(f32v(ralloc(TW), TW), Buf("fm%d" % i)) for i in range(6)]
        tqi = [0]
        fmi = [0]

        def gtq():
            x = tq[tqi[0]]
            tqi[0] = (tqi[0] + 1) % len(tq)
            return x

        def gfm():
            x = fm[fmi[0]]
            fmi[0] = (fmi[0] + 1) % len(fm)
            return x

        def load_x_and_mix(tt, which):
            t0 = tt * TW
            rk, l0 = t0 // NTC, t0 % NTC

            def xsrc(rk_, a, b_):
                return d_xg.rearrange("(c r p) t -> r p c t", c=8, r=4, p=128)[rk_][:, :, a:b_]
            if l0 > 0:
                S.dma("sp", lambda e: e.dma_start(out=xs[:, :, :], in_=xsrc(rk, l0 - 1, l0 + TW)), reads=xgB, writes=[xsB])
            else:
                if rk == 0:
                    S.op("dve", lambda e: e.memset(xs[:, :, 0:1], 0.0), writes=[xsB])
                else:
                    S.dma("sp", lambda e: e.dma_start(out=xs[:, :, 0:1], in_=xsrc(rk - 1, NTC - 1, NTC), allow_slow_non_contiguous=True), reads=xgB, writes=[xsB])
                S.dma("sp", lambda e: e.dma_start(out=xs[:, :, 1:TW + 1], in_=xsrc(rk, 0, TW)), reads=xgB, writes=[xsB])
            S.op("dve", lambda e: e.scalar_tensor_tensor(out=xxb, in0=xs[:, :, 1:TW + 1], scalar=-1.0, in1=xs[:, :, 0:TW], op0=ALU.mult, op1=ALU.add), reads=[xsB], writes=[xxB])
            S.op("act", lambda e: e.activation(out=xb, in_=xs[:, :, 1:TW + 1], func=AF.Identity), reads=[xsB], writes=[xxB])

        W2 = {id(wrkv): wrkv2, id(wl1): wl12}

        def fm_proj(wtile, col0, ncol):
            pt, pb = cx.psum()
            w2 = W2[id(wtile)]
            pairs = [(wtile[:, k, col0:col0 + ncol], xb[:, k, :]) for k in range(KC)] + [(w2[:, k, col0:col0 + ncol], xxb[:, k, :]) for k in range(KC)]
            mm_group(S, pt[0:ncol, 0:TW], pairs, reads=[wB, w2B, xxB], writes=[pb])
            return pt, pb

        def tm_pairs(tb0, c0, c1):
            return [(xb[:, k, tb0:tb0 + 128], wrkv[:, k, c0:c1]) for k in range(KC)] + [(xxb[:, k, tb0:tb0 + 128], wrkv2[:, k, c0:c1]) for k in range(KC)]

        def phaseA(tt):
            t0 = tt * TW
            load_x_and_mix(tt, None)
            pw, pwb = fm_proj(wl1, 0, 64)
            tw_, twB = gfm()
            tw_b = tw_[0:64, 0:TW // 2].bitcast(BF16)
            S.op("act", lambda e: e.activation(out=tw_b, in_=pw[0:64, 0:TW], func=AF.Tanh), reads=[pwb], writes=[twB])
            pa, pab = fm_proj(wl1, 64, 64)
            ta_, taB = gfm()
            ta_b = ta_[0:64, 0:TW // 2].bitcast(BF16)
            S.op("act", lambda e: e.activation(out=ta_b, in_=pa[0:64, 0:TW], func=AF.Identity), reads=[pab], writes=[taB])
            for bi in range(TW // 128):
                tb0 = bi * 128
                tm = TMt[(tt * (TW // 128) + bi) % 2]
                tmf = TMf[(tt * (TW // 128) + bi) % 2]
                tmb = tmB[(tt * (TW // 128) + bi) % 2]
                pr, prb = cx.psum()
                mm_group(S, pr[:, 0:256], tm_pairs(tb0, 0, 256), reads=[wB, w2B, xxB], writes=[prb])
                pk, pkb = cx.psum()
                mm_group(S, pk[:, 0:256], tm_pairs(tb0, 256, 512), reads=[wB, w2B, xxB], writes=[pkb])
                p2w, p2wb = cx.psum()
                mm_group(S, p2w[:, 0:256], [(tw_b[:, tb0:tb0 + 128], wl2[0:64, 0:256])], reads=[wB, twB], writes=[p2wb])
                p2a, p2ab = cx.psum()
                mm_group(S, p2a[:, 0:256], [(ta_b[:, tb0:tb0 + 128], wl2[0:64, 256:512])], reads=[wB, taB], writes=[p2ab])

                def h3(ap):
                    return ap.rearrange("p (h e) -> p h e", h=4)
                S.op("act", lambda e, tm=tm, pr=pr: e.activation(out=tm[:, :, 4, :], in_=h3(pr[:, 0:256]), func=AF.Identity), reads=[prb], writes=[tmb])
                ksb, ksB = gtq()
                S.op("act", lambda e, ksb=ksb, pk=pk: e.activation(out=ksb, in_=pk[:, 0:256], func=AF.Identity), reads=[pkb], writes=[ksB])
                u, uB = gtq()
                S.op("dve", lambda e, u=u, p2w=p2w: e.tensor_tensor(out=u, in0=p2w[:, 0:256], in1=bc[:, 0, :], op=ALU.add), reads=[p2wb, wB], writes=[uB])
                S.op("act", lambda e, u=u: e.activation(out=u, in_=u, func=AF.Sigmoid), reads=[uB], writes=[uB])
                S.op("act", lambda e, u=u, tm=tm: e.activation(out=tm[:, :, 1, :], in_=h3(u), func=AF.Identity, scale=-math.exp(-0.5)), reads=[uB], writes=[tmb])
                pvt, pvtb = cx.psum()
                mm_group(S, pvt[:, 0:256], tm_pairs(tb0, 512, 768), reads=[wB, w2B, xxB], writes=[pvtb])
                S.op("act", lambda e, tm=tm, pvt=pvt: e.activation(out=tm[:, :, 5, :], in_=h3(pvt[:, 0:256]), func=AF.Identity), reads=[pvtb], writes=[tmb])
                a_, aB = gtq()
                S.op("dve", lambda e, a_=a_, p2a=p2a: e.tensor_tensor(out=a_, in0=p2a[:, 0:256], in1=bc[:, 1, :], op=ALU.add), reads=[p2ab, wB], writes=[aB])
                S.op("act", lambda e, a_=a_: e.activation(out=a_, in_=a_, func=AF.Sigmoid), reads=[aB], writes=[aB])
                kr, krB = gtq()
                S.op("dve", lambda e, kr=kr, ksb=ksb: e.tensor_tensor(out=kr, in0=ksb, in1=bc[:, 2, :], op=ALU.mult), reads=[ksB, wB], writes=[krB])
                sq, sqB = gtq()
                S.op("dve", lambda e, sq=sq, kr=kr: e.tensor_tensor(out=sq, in0=kr, in1=kr, op=ALU.mult), reads=[krB], writes=[sqB])
                ss, ssB = gtq()
                S.op("dve", lambda e, ss=ss, sq=sq: e.tensor_reduce(out=ss[:, 0:4], in_=h3(sq), axis=mybir.AxisListType.X, op=ALU.add), reads=[sqB], writes=[ssB])
                S.op("act", lambda e, ss=ss: e.activation(out=ss[:, 0:4], in_=ss[:, 0:4], func=AF.Sqrt), reads=[ssB], writes=[ssB])
                S.op("dve", lambda e, ss=ss: e.tensor_scalar_max(out=ss[:, 0:4], in0=ss[:, 0:4], scalar1=1e-12), reads=[ssB], writes=[ssB])
                S.op("dve", lambda e, ss=ss: e.reciprocal(out=ss[:, 0:4], in_=ss[:, 0:4]), reads=[ssB], writes=[ssB])
                for h in range(4):
                    S.op("dve", lambda e, h=h, tm=tm, kr=kr, ss=ss: e.tensor_scalar(out=tm[:, h, 0, :], in0=kr[:, h * 64:(h + 1) * 64], scalar1=ss[:, h:h + 1], scalar2=None, op0=ALU.mult), reads=[krB, ssB], writes=[tmb])
                tk, tkB = gtq()
                S.op("dve", lambda e, tk=tk, a_=a_: e.scalar_tensor_tensor(out=tk, in0=a_, scalar=-1.0, in1=bc[:, 3, :], op0=ALU.add, op1=ALU.mult), reads=[aB, wB], writes=[tkB])
                S.op("dve", lambda e, tk=tk, ksb=ksb, tm=tm: e.scalar_tensor_tensor(out=tm[:, :, 3, :], in0=h3(tk), scalar=1.0, in1=h3(ksb), op0=ALU.add, op1=ALU.mult), reads=[tkB, ksB], writes=[tmb])
                S.op("dve", lambda e, tm=tm, a_=a_: e.tensor_tensor(out=tm[:, :, 2, :], in0=tm[:, :, 0, :], in1=h3(a_), op=ALU.mult), reads=[tmb, aB], writes=[tmb])
                S.op("dve", lambda e, tm=tm: e.tensor_scalar(out=tm[:, :, 0, :], in0=tm[:, :, 0, :], scalar1=-1.0, scalar2=None, op0=ALU.mult), reads=[tmb], writes=[tmb])
                S.dma("sp", lambda e, tmf=tmf, tb0=tb0: e.dma_start(out=d_scr[t0 + tb0:t0 + tb0 + 128].rearrange("t h m -> t (h m)"), in_=tmf), reads=[tmb], writes=[scrB[(t0 + tb0) // SBLK + i] for i in range(128 // SBLK)])

        scrB = [Buf("scr%d" % i) for i in range(T // SBLK)]
        for tt in range(NTW):
            phaseA(tt)
        S.barrier()
        ro[0] = o_reg

        d_rc = D["rc"]
        cB = Buf("rconst")
        c_bf = bfv(ralloc(192), 384)
        triU, triL, ident = c_bf[:, 0:128], c_bf[:, 128:256], c_bf[:, 256:384]
        if STOPB[0] >= 0 or STOPB[0] == -2:
            S.dma("pool", lambda e: e.dma_start(out=c_bf, in_=d_rc[:, 0:384]), writes=[cB])
        M4 = f32v(ralloc(512), 512)
        SL4 = f32v(ralloc(512), 512)
        if STOPB[0] >= 0 or STOPB[0] == -2:
            S.dma("sp", lambda e: e.dma_start(out=M4, in_=d_rc[:, 384:896]), writes=[cB])
            S.dma("sp", lambda e: e.dma_start(out=SL4, in_=d_rc[:, 896:1408]), writes=[cB])
        onec = bfv(ralloc(4), 8)
        S.op("dve", lambda e: e.memset(onec, 1.0), writes=[cB])
        def mkbufs():
            TMi1 = (f32v(ralloc(4 * NA * 64), 4 * NA * 64), Buf("tmi"))
            LW = bfv(ralloc(256), 512).rearrange("p (s m) -> p s m", s=2)
            lwB = Buf("LW")
            ft = [f32v(ralloc(256), 256) for _ in range(5)]
            ftB = [Buf("ft%d" % i) for i in range(5)]
            Rt, Bt, Kt, Bh, Kh = [bfv(ralloc(128), 256) for _ in range(5)]
            tmB_ = Buf("tmprod")
            _o = ralloc(256)
            Vpf = f32v(_o, 256)
            Vp = bfv(_o, 512).rearrange("p (h m) -> p h m", h=4)
            vpB = Buf("Vp")
            FMt = bfv(ralloc(1024), 2048).rearrange("p (h a t) -> p h a t", h=4, a=4)
            fmB = Buf("FMt")
            AM = bfv(ralloc(1024), 2048).rearrange("p (h a t) -> p h a t", h=4, a=4)
            amB = Buf("AM")
            NPt = [[bfv(ralloc(256), 512).rearrange("p (h t) -> p h t", h=4) for _ in range(2)] for _ in range(2)]
            npB = [[Buf("NP%d%d" % (i, j)) for j in range(2)] for i in range(2)]
            _o = ralloc(768)
            X32f = f32v(_o, 768)
            X32 = f32v(_o, 768).rearrange("p (h m) -> p h m", h=4)
            _o = ralloc(384)
            Xbff = f32v(_o, 384)
            Xbf = bfv(_o, 768).rearrange("p (h m) -> p h m", h=4)
            x32B, xbB = Buf("X32"), Buf("Xbf")
            RhT = bfv(ralloc(256), 512).rearrange("p (h t) -> p h t", h=4)
            rhB = Buf("RhT")
            GT = bfv(ralloc(128), 256).rearrange("p (h t) -> p h t", h=4)
            gtB = Buf("GT")
            Hh = f32v(ralloc(256), 256).rearrange("p (h t) -> p h t", h=4)
            hhB = Buf("H")
            gl = f32v(ralloc(32), 32)
            glB = Buf("gl")


            return dict(locals())
        BUFS = [mkbufs(), mkbufs()]
        _o = ralloc(256)
        S32f = f32v(_o, 256)
        S32 = f32v(_o, 256).rearrange("p (h t) -> p h t", h=4)
        s32B = Buf("S32")
        _o = ralloc(256)
        Spadf = f32v(_o, 256)
        Spad = bfv(_o, 512).rearrange("p (h m) -> p h m", h=4)
        spB = Buf("Spad")

        def q4(ap3):
            return ap3.rearrange("p (q f) m -> p q f m", q=2)

        def h3(ap):
            return ap.rearrange("p (h e) -> p h e", h=4)

        def acts(h):
            return slice(0, 128) if h % 2 == 0 else slice(64, 192)

        def ahc(h):
            return slice(0, 64) if h % 2 == 0 else slice(128, 192)

        def upad(h):
            return slice(64, 192) if h % 2 == 0 else slice(0, 128)

        def vd(h):
            return slice(0, 64) if h % 2 == 0 else slice(64, 128)

        def run_chunks():
            for B_ in BUFS:
                S.op("dve", lambda e, B_=B_: e.memset(B_["X32f"], 0.0), writes=[B_["x32B"]])
                S.op("dve", lambda e, B_=B_: e.memset(B_["Xbff"], 0.0), writes=[B_["xbB"]])
                S.op("dve", lambda e, B_=B_: e.memset(B_["Vpf"], 0.0), writes=[B_["vpB"]])
            S.op("dve", lambda e: e.memset(Spadf, 0.0), writes=[spB])
            S.op("dve", lambda e: e.memset(S32f, 0.0), writes=[s32B])
            for c0 in range(0, T // 128, 2):
                g0, g1 = chunk(c0), chunk(c0 + 1)
                d0 = d1 = False
                while not (d0 and d1):
                    if not d0:
                        d0 = next(g0) == "split"
                    if not d1:
                        d1 = next(g1) == "split"
                for _ in g0:
                    pass
                for _ in g1:
                    pass

        def chunk(c):
            B_ = BUFS[c % 2]
            tmf, tib = B_["TMi1"]
            LW, lwB, ft, ftB, Rt, Bt, Kt, Bh, Kh, tmB_, Vp, vpB, FMt, fmB, AM, amB, NPt, npB, X32, Xbf, x32B, xbB, RhT, rhB, GT, gtB, Hh, hhB, gl, glB = [B_[k] for k in (
                "LW", "lwB", "ft", "ftB", "Rt", "Bt", "Kt", "Bh", "Kh", "tmB_", "Vp", "vpB", "FMt", "fmB", "AM", "amB", "NPt", "npB", "X32", "Xbf", "x32B", "xbB", "RhT", "rhB", "GT", "gtB", "Hh", "hhB", "gl", "glB")]
            tm = tmf.rearrange("p (h a e) -> p h a e", h=4, a=NA)
            tmq = tmf.rearrange("p (q f a e) -> p q f a e", q=2, f=2, a=NA)
            S.dma("sp", lambda e: e.dma_start(out=tmf, in_=d_scr[c * 128:(c + 1) * 128].rearrange("t h m -> t (h m)")), reads=[scrB[c]], writes=[tib])

            def arr(i):
                return tm[:, :, i, :]
            S.op("dve", lambda e: e.tensor_copy(out=h3(LW[:, 0, :]), in_=arr(1)), reads=[tib], writes=[lwB])
            S.op("dve", lambda e: e.scalar_tensor_tensor(out=h3(LW[:, 1, :]), in0=h3(LW[:, 0, :]), scalar=-1.0, in1=arr(1), op0=ALU.mult, op1=ALU.add), reads=[tib, lwB], writes=[lwB])
            pCI, pCIb = cx.psum()
            mm_group(S, pCI[:, 0:256], [(triU, LW[:, 0, :]), (triU, LW[:, 1, :])], reads=[cB, lwB], writes=[pCIb])
            pCR, pCRb = cx.psum()
            mm_group(S, pCR[:, 0:256], [(triL, LW[:, 0, :]), (triL, LW[:, 1, :])], reads=[cB, lwB], writes=[pCRb])
            pGL, pGLb = cx.psum()
            for h in range(4):
                mm_group(S, pGL[0:64, h * 8:(h + 1) * 8], [(LW[:, 0, h * 64:(h + 1) * 64], onec), (LW[:, 1, h * 64:(h + 1) * 64], onec)], reads=[cB, lwB], writes=[pGLb])
            e_ex, e_in, e_ninv, e_rev, tmpx = ft
            S.op("act", lambda e: e.activation(out=e_in, in_=pCI[:, 0:256], func=AF.Exp), reads=[pCIb], writes=[ftB[1]])
            S.op("act", lambda e: e.activation(out=e_ninv, in_=pCI[:, 0:256], func=AF.Exp, scale=-1.0), reads=[pCIb], writes=[ftB[2]])
            S.op("act", lambda e: e.activation(out=e_rev, in_=pCR[:, 0:256], func=AF.Exp), reads=[pCRb], writes=[ftB[3]])
            S.op("act", lambda e: e.activation(out=h3(tmpx), in_=arr(1), func=AF.Exp, scale=-1.0), reads=[tib], writes=[ftB[4]])
            S.op("dve", lambda e: e.tensor_tensor(out=e_ex, in0=tmpx, in1=e_in, op=ALU.mult), reads=[ftB[4], ftB[1]], writes=[ftB[0]])
            S.op("act", lambda e: e.activation(out=gl[0:64, 0:32], in_=pGL[0:64, 0:32], func=AF.Exp), reads=[pGLb], writes=[glB])
            yield 1
            for hf in range(2):
                ac = slice(0, 64) if hf == 0 else slice(128, 192)
                S.op("dve", lambda e, hf=hf, ac=ac: e.tensor_tensor(out=q4(X32)[:, :, hf, ac], in0=tmq[:, :, hf, 0, :], in1=e_ex.rearrange("p (q f e) -> p q f e", q=2, f=2)[:, :, hf, :], op=ALU.mult),
                     reads=[tib, ftB[0]], writes=[x32B])
                S.op("act", lambda e, hf=hf, ac=ac: e.activation(out=q4(Xbf)[:, :, hf, ac], in_=q4(X32)[:, :, hf, ac], func=AF.Identity), reads=[x32B], writes=[xbB])
                S.op("act", lambda e, hf=hf: e.activation(out=q4(Vp)[:, :, hf, hf * 64:(hf + 1) * 64], in_=tmq[:, :, hf, 5, :], func=AF.Identity), reads=[tib], writes=[vpB])
            for (dst, ai, ee, eb) in ((Rt, 4, e_in, 1), (Bt, 2, e_ninv, 2), (Kt, 3, e_ninv, 2), (Bh, 2, e_rev, 3), (Kh, 3, e_rev, 3)):
                S.op("dve", lambda e, dst=dst, ai=ai, ee=ee: e.tensor_tensor(out=h3(dst), in0=arr(ai), in1=h3(ee), op=ALU.mult), reads=[tib, ftB[eb]], writes=[tmB_])
            yield 2
            for q in range(2):
                pT, pTb = cx.psum()
                pT16 = pT[:, :].bitcast(BF16)

                def tr(e, q=q, pT16=pT16):
                    ins = None
                    for hh in range(2):
                        h = 2 * q + hh
                        srcs = (Xbf[:, h, ahc(h)], Rt[:, h * 64:(h + 1) * 64], Bt[:, h * 64:(h + 1) * 64], Kt[:, h * 64:(h + 1) * 64])
                        for ai, src in enumerate(srcs):
                            o0 = (hh * 4 + ai) * 128
                            ins = e.transpose(pT16[0:64, o0:o0 + 128], src, ident)
                    return ins
                S.op("pe", tr, reads=[xbB, tmB_, cB], writes=[pTb])
                S.op("act", lambda e, q=q, pT16=pT16: e.activation(out=FMt[0:64, 2 * q:2 * q + 2, :, :], in_=pT16[0:64, 0:1024].rearrange("p (h a t) -> p h a t", h=2, a=4), func=AF.Identity), reads=[pTb], writes=[fmB])
            yield 3
            pP, pPb = cx.psum()
            for h in range(4):
                pA, pAb = cx.psum()
                rhs = FMt[0:64, h, 0:2, :].rearrange("p a t -> p (a t)")
                mm_group(S, pA[:, 0:256], [(FMt[0:64, h, 2, :], rhs)], reads=[fmB], writes=[pAb])
                mm_group(S, pA[:, 256:512], [(FMt[0:64, h, 3, :], rhs)], reads=[fmB], writes=[pAb])
                S.op("dve", lambda e, h=h, pA=pA: e.tensor_tensor(out=AM[:, h, :, :].rearrange("p a t -> p (a t)"), in0=pA[:, :], in1=M4, op=ALU.mult), reads=[pAb, cB], writes=[amB])
                mm_group(S, pP[:, h * 128:(h + 1) * 128], [(FMt[0:64, h, 0, :], FMt[0:64, h, 2, :])], reads=[fmB], writes=[pPb])
            S.op("dve", lambda e: e.tensor_tensor(out=NPt[0][1].rearrange("p h t -> p (h t)"), in0=pP[:, :], in1=SL4, op=ALU.mult), reads=[pPb, cB], writes=[npB[0][1]])
            yield 4
            pW, pWb = cx.psum()
            for h in range(4):
                mm_group(S, pW[:, h * 64:(h + 1) * 64], [(AM[:, h, 2, :], Vp[:, h, vd(h)])], reads=[amB, vpB], writes=[pWb])
            S.op("act", lambda e: e.activation(out=X32[:, :, 64:128], in_=h3(pW[:, 0:256]), func=AF.Identity), reads=[pWb], writes=[x32B])
            S.op("dve", lambda e: e.tensor_copy(out=Xbf[:, :, 64:128], in_=X32[:, :, 64:128]), reads=[x32B], writes=[xbB])
            yield 5
            for lev in range(7):
                if lev == 0:
                    Nc = [AM[:, h, 0, :] for h in range(4)]
                    Pc = [NPt[0][1][:, h, :] for h in range(4)]
                    nB, pB_ = amB, npB[0][1]
                    wset = 1
                else:
                    cs_ = lev % 2
                    Nc = [NPt[cs_][0][:, h, :] for h in range(4)]
                    Pc = [NPt[cs_][1][:, h, :] for h in range(4)]
                    nB, pB_ = npB[cs_][0], npB[cs_][1]
                    wset = (lev + 1) % 2
                pX, pXb = cx.psum()
                for h in range(4):
                    mm_group(S, pX[:, h * 128:(h + 1) * 128], [(Nc[h], Xbf[:, h, acts(h)])], reads=[nB, xbB], writes=[pXb])
                pXq = pX[:, :].rearrange("p (q f t) -> p q f t", q=2, f=2)
                for hf in range(2):
                    ac = slice(0, 128) if hf == 0 else slice(64, 192)
                    S.op("dve", lambda e, hf=hf, ac=ac, pXq=pXq: e.tensor_tensor(out=q4(X32)[:, :, hf, ac], in0=q4(X32)[:, :, hf, ac], in1=pXq[:, :, hf, :], op=ALU.add), reads=[pXb, x32B], writes=[x32B])
                    S.op("act", lambda e, hf=hf, ac=ac: e.activation(out=q4(Xbf)[:, :, hf, ac], in_=q4(X32)[:, :, hf, ac], func=AF.Identity), reads=[x32B], writes=[xbB])
                if lev < 6:
                    pN, pNb = cx.psum()
                    pQ, pQb = cx.psum()
                    for h in range(4):
                        mm_group(S, pN[:, h * 128:(h + 1) * 128], [(Pc[h], Nc[h])], reads=[nB, pB_], writes=[pNb])
                        mm_group(S, pQ[:, h * 128:(h + 1) * 128], [(Nc[h], Pc[h])], reads=[nB, pB_], writes=[pQb])
                    S.op("act", lambda e, pN=pN, wset=wset: e.activation(out=NPt[wset][0].rearrange("p h t -> p (h t)"), in_=pN[:, :], func=AF.Identity), reads=[pNb], writes=[npB[wset][0]])
                    S.op("dve", lambda e, pQ=pQ, wset=wset: e.tensor_copy(out=NPt[wset][1].rearrange("p h t -> p (h t)"), in_=pQ[:, :]), reads=[pQb], writes=[npB[wset][1]])
                yield 50 + lev
            yield 6
            pR, pRb = cx.psum()
            for h in range(4):
                mm_group(S, pR[0:64, h * 128:(h + 1) * 128], [(Xbf[:, h, ahc(h)], AM[:, h, 1, :])], reads=[xbB, amB], writes=[pRb])
            S.op("dve", lambda e: e.tensor_tensor(out=RhT[0:64, :, :], in0=pR[0:64, :].rearrange("p (h t) -> p h t", h=4), in1=FMt[0:64, :, 1, :], op=ALU.add), reads=[pRb, fmB], writes=[rhB])
            pG, pGb = cx.psum()
            for h in range(4):
                mm_group(S, pG[0:64, h * 64:(h + 1) * 64], [(Xbf[:, h, ahc(h)], Bh[:, h * 64:(h + 1) * 64])], reads=[xbB, tmB_], writes=[pGb])
                mm_group(S, pG[0:64, 256 + h * 64:256 + (h + 1) * 64], [(Bh[:, h * 64:(h + 1) * 64], Xbf[:, h, 64:128]), (Kh[:, h * 64:(h + 1) * 64], Vp[:, h, vd(h)])], reads=[xbB, tmB_, vpB], writes=[pGb])
            S.op("act", lambda e: e.activation(out=GT[0:64, :, :], in_=pG[0:64, 0:256].rearrange("p (h t) -> p h t", h=4), func=AF.Identity), reads=[pGb], writes=[gtB])
            S.op("act", lambda e: e.activation(out=Hh[0:64, :, :], in_=pG[0:64, 256:512].rearrange("p (h t) -> p h t", h=4), func=AF.Identity), reads=[pGb], writes=[hhB])
            yield "split"
            for q in range(2):
                pY, pYb = cx.psum()
                pairs = []
                for hh in range(2):
                    h = 2 * q + hh
                    pairs += [(Spad[0:64, h, :], RhT[0:64, h, :]), (Xbf[:, h, upad(h)], AM[:, h, 1, :]), (Vp[:, h, :], AM[:, h, 3, :])]
                mm_group(S, pY[:, 0:128], pairs, reads=[spB, rhB, xbB, amB, vpB], writes=[pYb])
                S.op("act", lambda e, q=q, pY=pY: e.activation(out=YT[:, q, c * 128:(c + 1) * 128], in_=pY[:, 0:128], func=AF.Identity), reads=[pYb], writes=[yB[(c * 128) // TW]])
            pS, pSb = cx.psum()
            for h in range(4):
                mm_group(S, pS[0:64, h * 64:(h + 1) * 64], [(GT[0:64, h, :], Spad[0:64, h, vd(h)])], reads=[gtB, spB], writes=[pSb])
            for h in range(4):
                S.op("dve", lambda e, h=h, pS=pS: e.scalar_tensor_tensor(out=S32[0:64, h, :], in0=S32[0:64, h, :], scalar=gl[0:64, h * 8:h * 8 + 1], in1=pS[0:64, h * 64:(h + 1) * 64], op0=ALU.mult, op1=ALU.add),
                     reads=[pSb, glB, s32B], writes=[s32B])
            S.op("dve", lambda e: e.tensor_tensor(out=S32[0:64, :, :], in0=S32[0:64, :, :], in1=Hh[0:64, :, :], op=ALU.add), reads=[s32B, hhB], writes=[s32B])
            for hf in range(2):
                S.op("act", lambda e, hf=hf: e.activation(out=q4(Spad)[0:64, :, hf, hf * 64:(hf + 1) * 64], in_=q4(S32)[0:64, :, hf, :], func=AF.Identity), reads=[s32B], writes=[spB])

        def phaseC(s, tt):
            t0 = tt * TW
            if s == 0:
                load_x_and_mix(tt, (0, 1, 4, 5))
            pr, prb = fm_proj(wrkv, s * 128, 128)
            pk, pkb = fm_proj(wrkv, 256 + s * 128, 128)
            pa, pab = fm_proj(wl1, 64, 64)
            ta_, taB = gfm()
            ta_b = ta_[0:64, 0:TW // 2].bitcast(BF16)
            S.op("act", lambda e: e.activation(out=ta_b, in_=pa[0:64, 0:TW], func=AF.Identity), reads=[pab], writes=[taB])
            pa2, pa2b = cx.psum()
            mm_group(S, pa2[:, 0:TW], [(wl2[0:64, 256 + s * 128:256 + (s + 1) * 128], ta_b)], reads=[wB, taB], writes=[pa2b])
            aT, aTB = gfm()
            S.op("act", lambda e: e.activation(out=aT, in_=pa2[:, 0:TW], func=AF.Sigmoid, bias=V("a0", s)), reads=[pa2b, vb], writes=[aTB])
            S.op("dve", lambda e: e.tensor_scalar(out=aT, in0=aT, scalar1=-1.0, scalar2=V("k_a", s), op0=ALU.add, op1=ALU.mult), reads=[aTB, vb], writes=[aTB])
            S.op("dve", lambda e: e.scalar_tensor_tensor(out=aT, in0=aT, scalar=1.0, in1=pk[:, 0:TW], op0=ALU.add, op1=ALU.mult), reads=[aTB, pkb], writes=[aTB])
            rk_, rkB = gfm()
            rkb16 = rk_[:, 0:TW // 2].bitcast(BF16)
            S.op("dve", lambda e: e.scalar_tensor_tensor(out=rkb16, in0=pr[:, 0:TW], scalar=V("r_k", s), in1=aT, op0=ALU.mult, op1=ALU.mult), reads=[prb, aTB, vb], writes=[rkB])
            pbs, pbsb = cx.psum()
            mm_group(S, pbs[:, 0:TW], [(blk, rkb16)], reads=[wB, rkB], writes=[pbsb])
            bon, bonB = gfm()
            pv, pvb = fm_proj(wrkv, 512 + s * 128, 128)
            S.op("act", lambda e: e.activation(out=bon, in_=pv[:, 0:TW], func=AF.Identity), reads=[pvb], writes=[bonB])
            S.op("dve", lambda e: e.tensor_tensor(out=bon, in0=pbs[:, 0:TW], in1=bon, op=ALU.mult), reads=[pbsb, bonB], writes=[bonB])
            pg, pgb = fm_proj(wl1, 128, 128)
            gs_, gsB = gfm()
            gs16 = gs_[:, 0:TW // 2].bitcast(BF16)
            S.op("act", lambda e: e.activation(out=gs16, in_=pg[:, 0:TW], func=AF.Sigmoid), reads=[pgb], writes=[gsB])
            pg2, pg2b = cx.psum()
            mm_group(S, pg2[:, 0:TW], [(wl2[:, 512 + s * 128:512 + (s + 1) * 128], gs16)], reads=[wB, gsB], writes=[pg2b])
            y = YT[:, s, t0:t0 + TW]
            yb16_, y16B = gfm()
            yb16 = yb16_[:, 0:TW // 2].bitcast(BF16)
            ysq16 = yb16_[:, TW // 2:TW].bitcast(BF16)
            S.op("pool", lambda e: e.tensor_copy(out=yb16, in_=y), reads=[yB[tt]], writes=[y16B])
            S.op("act", lambda e: e.activation(out=ysq16, in_=y, func=AF.Square), reads=[yB[tt]], writes=[y16B])
            p1, p1b = cx.psum()
            mm_group(S, p1[:, 0:TW], [(blk, yb16)], reads=[wB, y16B], writes=[p1b])
            p2, p2b = cx.psum()
            mm_group(S, p2[:, 0:TW], [(blk, ysq16)], reads=[wB, y16B], writes=[p2b])
            nm, nmB = gfm()
            S.op("act", lambda e: e.activation(out=nm, in_=p1[:, 0:TW], func=AF.Identity, scale=-1.0 / 64), reads=[p1b], writes=[nmB])
            S.op("dve", lambda e: e.scalar_tensor_tensor(out=gs_, in0=nm, scalar=-1.0, in1=nm, op0=ALU.mult, op1=ALU.mult), reads=[nmB, gsB, pg2b], writes=[gsB])
            S.op("dve", lambda e: e.scalar_tensor_tensor(out=gs_, in0=p2[:, 0:TW], scalar=1.0 / 64, in1=gs_, op0=ALU.mult, op1=ALU.add), reads=[p2b, gsB], writes=[gsB])
            S.op("act", lambda e: e.activation(out=gs_, in_=gs_, func=AF.Sqrt, bias=epsc, scale=1.0), reads=[gsB, wB], writes=[gsB])
            S.op("dve", lambda e: e.reciprocal(out=gs_, in_=gs_), reads=[gsB], writes=[gsB])
            S.op("dve", lambda e: e.tensor_tensor(out=nm, in0=y, in1=nm, op=ALU.add), reads=[yB[tt], nmB], writes=[nmB])
            S.op("dve", lambda e: e.tensor_tensor(out=nm, in0=nm, in1=gs_, op=ALU.mult), reads=[nmB, gsB], writes=[nmB])
            S.op("act", lambda e: e.activation(out=nm, in_=nm, func=AF.Identity, scale=V("lnx_g", s), bias=V("lnx_b", s)), reads=[nmB, vb], writes=[nmB])
            S.op("dve", lambda e: e.tensor_tensor(out=nm, in0=nm, in1=bon, op=ALU.add), reads=[nmB, bonB], writes=[nmB])
            S.op("dve", lambda e: e.tensor_tensor(out=nm, in0=nm, in1=pg2[:, 0:TW], op=ALU.mult), reads=[nmB, pg2b], writes=[nmB])
            S.dma("sp", lambda e: e.dma_start(out=d_o[(t0 // 1024) * 256 + s * 128:(t0 // 1024) * 256 + (s + 1) * 128, t0 % 1024:t0 % 1024 + TW], in_=nm), reads=[nmB], writes=oB)

        run_chunks()
        S.barrier()
        for tt in range(NTW if STOPB[0] >= 99 else 1):
            for s in range(2):
                phaseC(s, tt)
        S.barrier()
        return oB


import numpy as np
from contextlib import ExitStack
import concourse.bass as bass
import concourse.mybir as mybir
from concourse.bass_utils import run_bass_kernel_spmd

GROUPS4 = [[0, 1, 2, 3], [4, 5, 6, 7]]
TOKW = ("w_out", [1024, 1024]), ("w_q", [1024, 1024]), ("w_kv", [1024, 2048]), ("w_o", [1024, 1024]), ("w1", [1024, 4096]), ("w2", [4096, 1024])


def build_fused(upto=9):
    nc = bass.Bass("TRN2", target_bir_lowering=False)

    def din(name, shape):
        return nc.dram_tensor(name, shape, F32, kind="ExternalInput").ap()

    def dint(name, shape):
        return nc.dram_tensor(name, shape, F32).ap()

    x0 = din("xT", [1024, NT + HALO])
    memT = din("memT", [1024, 256])
    L = []
    for i in range(4):
        d = {"memT": memT, "vecs": din("vecs%d" % i, [128, NVEC])}
        for n, shp in TOKW:
            d[n] = din("%s%d" % (n, i), shp)
        if i in (0, 3):
            d["w_in"] = din("w_in%d" % i, [1024, 2048])
        L.append(d)
    at = {"wqkv": din("wqkv", [1024, 2304]), "relb": din("relb", [128, 512]), "oh": din("oh", [len(OH_KEYS), 128, 128])}
    rw = {"wrkv": din("wrkv", [1024, 768]), "wl1": din("wl1", [1024, 256]), "wl2": din("wl2", [128, 768]), "vecs": din("rvecs", [128, NRV]),
          "bc": din("bc", [128, 4, 256]), "blk": din("blk", [128, 128]), "rc": din("rc", [128, NCONST])}
    yT = nc.dram_tensor("yT", [1024, NT], F32, kind="ExternalOutput").ap()
    xs = [dint("xs%d" % i, [1024, NT]) for i in range(1, 4)]
    xg = [dint("xg%d" % i, [4096, NT]) for i in range(1, 3)]
    osrc = [dint("os%d" % i, [8 * 256, 1024]) for i in range(1, 3)]
    og = [dint("og%d" % i, [8 * 4 * 256, 1024]) for i in range(1, 3)]
    hs = dint("hs", [1024, HALO])
    hg = dint("hg", [4096, HALO])
    rw["scr"] = dint("scr", [T, 4, NA * 64])

    with ExitStack() as st:
        cx = Ctx(nc, st, 51980)
        cx.vt = st.enter_context(nc.sbuf_tensor("vt", [128, NVEC], F32))
        S = cx.S

        def gather(src, dst, srcB, name, nchunk=8):
            S.barrier()
            rows = src.shape[0] // nchunk
            bs = []
            for k in range(nchunk):
                b = Buf("%s_%d" % (name, k))
                S.dma("pool", lambda e, k=k: e.collective_compute("AllGather", ALU.bypass, replica_groups=GROUPS4, ins=[src[k * rows:(k + 1) * rows, :].opt()],
                                                                    outs=[dst[k * 4 * rows:(k + 1) * 4 * rows, :].opt()]), reads=srcB, writes=[b], inc=1)
                bs.append(b)
            return bs

        d = dict(L[0]); d.update({"x": x0, "y": xs[0]})
        yB1 = emit_tok(cx, "conv0", d)
        g1 = gather(xs[0], xg[0], yB1, "xg1")

        def early(src_ap, srcB):
            S.dma("sp", lambda e: e.dma_start(out=yT.rearrange("(c p) t -> c p t", p=128) if len(src_ap.shape) == 3 else yT, in_=src_ap), reads=srcB, is_output=True)
            S.finish("sp")
            S.emit(st)
            return nc
        def early_o(ogt, srcB):
            for k in range(2):
                S.dma("sp", lambda e, k=k: e.dma_start(out=yT[:, k * 1024:(k + 1) * 1024], in_=ogt[k * 1024:(k + 1) * 1024, :]), reads=srcB, is_output=True)
            S.finish("sp")
            S.emit(st)
            return nc
        if upto == 1:
            return early(xg[0].rearrange("(c r p) t -> r c p t", c=8, r=4, p=128)[0], g1)
        d = dict(at); d.update({"x_g": xg[0], "xgB": g1, "o": osrc[0]})
        oB1 = emit_attn(cx, d)
        gB1 = gather(osrc[0], og[0], oB1, "og1")
        if upto == 2:
            return early_o(og[0], gB1)
        d = dict(L[1]); d.update({"x": xs[0], "xB": yB1, "h_g": og[0], "h_gB": gB1, "y": xs[1]})
        yB2 = emit_tok(cx, "tail", d)
        if upto == 3:
            return early(xs[1], yB2)
        g2 = gather(xs[1], xg[1], yB2, "xg2")
        d = dict(rw); d.update({"x_g": xg[1], "xgB": g2, "o": osrc[1]})
        oB2 = emit_rwkv(cx, d)
        gB2 = gather(osrc[1], og[1], oB2, "og2")
        if upto == 4:
            return early_o(og[1], gB2)
        d = dict(L[2]); d.update({"x": xs[1], "xB": yB2, "h_g": og[1], "h_gB": gB2, "y": xs[2], "tail_out": hs})
        yB3 = emit_tok(cx, "tail", d)
        if upto == 5:
            return early(xs[2], yB3)
        gh = gather(hs, hg, yB3, "hg", nchunk=1)
        d = dict(L[3]); d.update({"x": xs[2], "xB": yB3, "halo_g": hg, "halo_gB": gh, "y": yT})
        emit_tok(cx, "conv3", d, final=True)
        S.finish("sp")
        S.emit(st)
    return nc


_prog = {}


def rwkv_consts():
    i = np.arange(128)
    ui = (i[:, None] <= i[None, :]).astype(np.float32)
    su = (i[:, None] < i[None, :]).astype(np.float32)
    sl = (i[:, None] > i[None, :]).astype(np.float32)
    return np.ascontiguousarray(np.concatenate([ui, sl, np.eye(128, dtype=np.float32), su, ui, su, ui, sl, sl, sl, sl], axis=1))

UPTO = [9]


def fcols_(v):
    return fcols(v)


def kernel(**inputs):
    inp = {k: np.asarray(v) for k, v in inputs.items()}
    x = np.ascontiguousarray(inp["x"], dtype=np.float32)
    if "fused" not in _prog:
        _prog["fused"] = build_fused(UPTO[0])
    nc = _prog["fused"]
    blk = np.zeros((128, 128), np.float32); blk[:64, :64] = 1; blk[64:, 64:] = 1

    def interleave(win):
        return np.ascontiguousarray(np.concatenate([np.concatenate([win[:, m * 128:(m + 1) * 128], win[:, 1024 + m * 128:1024 + (m + 1) * 128]], axis=1) for m in range(8)], axis=1))
    win = {0: interleave(inp["a_w_in"][0]), 3: interleave(inp["a_w_in"][1])}
    mix_out = {0: inp["a_w_out"][0], 1: inp["b_w_out"][0], 2: inp["c_w_out"][0], 3: inp["a_w_out"][1]}
    Wqkv = inp["b_w_qkv"][0]
    Wr = inp["c_w_rkv"][0]
    wl1 = np.ascontiguousarray(np.concatenate([inp["c_w1"][0], inp["c_a1"][0], inp["c_g1"][0]], axis=1))
    in_maps = []
    for c in range(8):
        b, q = c // 4, c % 4
        h0 = 4 * q
        cs = slice(h0 * 64, h0 * 64 + 256)
        m = {}
        xt = np.zeros((1024, NT + HALO), np.float32)
        if q == 0:
            xt[:, HALO:] = x[b, 0:NT].T
        else:
            xt[:, :] = x[b, q * NT - HALO:(q + 1) * NT].T
        m["xT"] = xt
        m["memT"] = np.ascontiguousarray(inp["mem"][b].T)
        for i in range(4):
            d = {}
            for s in range(3):
                d["ln_g%d" % s] = fcols(inp["ln_g"][i, s]); d["ln_b%d" % s] = fcols(inp["ln_b"][i, s])
            if i in (0, 3):
                j = i // 3
                d["b_val"] = fcols(inp["a_b_in"][j][:1024]); d["b_gate"] = fcols(inp["a_b_in"][j][1024:])
                d["dw"] = np.concatenate([fcols(inp["a_dw"][j][t]) for t in range(31)], axis=1)
                d["dw_b"] = fcols(inp["a_dw_b"][j]); d["a_ln_g"] = fcols(inp["a_ln_g"][j]); d["a_ln_b"] = fcols(inp["a_ln_b"][j])
                d["b_out"] = fcols(inp["a_b_out"][j])
                d["halo_mask"] = np.full((128, 1), 0.0 if q == 0 else 1.0, np.float32)
                m["w_in%d" % i] = win[i]
            hsel = np.zeros((128, 4), np.float32)
            if i in (1, 2):
                hsel[:, q] = 1.0
            elif i == 3 and q > 0:
                hsel[:, q - 1] = 1.0
            d["hsel"] = hsel
            m["vecs%d" % i] = make_vecs(d)
            m["w_out%d" % i] = mix_out[i]
            m["w_q%d" % i] = inp["x_w_q"][i]; m["w_kv%d" % i] = inp["x_w_kv"][i]; m["w_o%d" % i] = inp["x_w_out"][i]
            m["w1%d" % i] = inp["m_w1"][i]; m["w2%d" % i] = inp["m_w2"][i]
        cols = []
        for p in range(2):
            for g in range(3):
                for t in range(3):
                    c0 = g * 3072 + t * 1024 + (h0 + 2 * p) * 64
                    cols.append(Wqkv[:, c0:c0 + 128])
        m["wqkv"] = np.ascontiguousarray(np.concatenate(cols, axis=1))
        relb = np.zeros((128, 512), np.float32)
        for bb in range(32):
            for hh in range(4):
                relb[:, bb * 4 + hh] = inp["rel_bias"][bb, h0 + hh]
        m["relb"] = relb
        m["oh"] = OH_MATS
        vec = np.zeros((128, NRV), np.float32)
        for s in range(6):
            vec[:, s * 8:(s + 1) * 8] = fcols(inp["c_mu"][0][s])
        vec[:, RV["k_a"]:RV["k_a"] + 2] = fcols(inp["c_k_a"][0][cs])
        vec[:, RV["a0"]:RV["a0"] + 2] = fcols(inp["c_a0"][0][cs])
        vec[:, RV["lnx_g"]:RV["lnx_g"] + 2] = fcols(inp["c_lnx_g"][0][cs])
        vec[:, RV["lnx_b"]:RV["lnx_b"] + 2] = fcols(inp["c_lnx_b"][0][cs])
        vec[:, RV["r_k"]:RV["r_k"] + 2] = fcols(inp["c_r_k"][0].reshape(-1)[cs])
        m["rvecs"] = vec
        m["bc"] = np.ascontiguousarray(np.stack([np.tile(inp[n][0][cs][None, :], (128, 1)) for n in ("c_w0", "c_a0", "c_k_k", "c_k_a")], axis=1).astype(np.float32))
        wl2 = np.zeros((128, 768), np.float32)
        wl2[:64, 0:256] = inp["c_w2"][0][:, cs]; wl2[:64, 256:512] = inp["c_a2"][0][:, cs]; wl2[:, 512:768] = inp["c_g2"][0][:, cs]
        m["wl2"] = wl2
        m["wrkv"] = np.ascontiguousarray(np.concatenate([Wr[0][:, cs], Wr[1][:, cs], Wr[2][:, cs]], axis=1))
        m["wl1"] = wl1
        m["blk"] = blk
        m["rc"] = rwkv_consts()
        in_maps.append(m)
    res = run_bass_kernel_spmd(nc, in_maps, core_ids=list(range(8)))
    out = np.empty_like(x)
    for c in range(8):
        b, q = c // 4, c % 4
        out[b, q * NT:(q + 1) * NT] = res.results[c]["yT"].T
    return np.ascontiguousarray(out, dtype=np.float32)
```

```python
import concourse.bass as bass
import concourse.mybir as mybir

ENG = ("pe", "act", "dve", "pool", "sp")


class Buf:
    __slots__ = ("name", "w", "r")

    def __init__(self, name):
        self.name = name
        self.w = None
        self.r = []


class Sched:
    def __init__(self, nc):
        self.nc = nc
        self.streams = {e: [] for e in ENG}
        self.cnt = {e: 0 for e in ENG}
        self.known = {e: {} for e in ENG}
        self.pools = {"hw": ["dmah%d" % i for i in range(24)], "sw": ["dmas%d" % i for i in range(12)], "cc": ["dmac%d" % i for i in range(4)]}
        self.pool_rr = {"hw": 0, "sw": 0, "cc": 0}
        self.dma_use = {k: 0 for p in self.pools.values() for k in p}
        self.sems = {}
        self.out_tokens = []
        self.same_engine_sync = True

    def _need(self, eng, tok):
        if tok is None:
            return
        key, val, src = tok
        if src == eng and not (self.same_engine_sync and eng != "pe"):
            return
        if self.known[eng].get(key, 0) >= val:
            return
        self.known[eng][key] = val
        self.streams[eng].append(("wait", key, val))

    def _deps(self, eng, reads, writes):
        for b in reads:
            self._need(eng, b.w)
        for b in writes:
            self._need(eng, b.w)
            for t in b.r:
                self._need(eng, t)

    def _mark(self, tok, reads, writes):
        for b in reads:
            b.r.append(tok)
        for b in writes:
            b.w = tok
            b.r = []

    def op(self, eng, fn, reads=(), writes=()):
        self._deps(eng, reads, writes)
        self.cnt[eng] += 1
        tok = (eng, self.cnt[eng], eng)
        self.streams[eng].append(("op", fn, (eng, 1)))
        self._mark(tok, reads, writes)
        return tok

    def dma(self, eng, fn, reads=(), writes=(), is_output=False, inc=16):
        self._deps(eng, reads, writes)
        kind = "cc" if inc == 1 else ("sw" if eng == "pool" else "hw")
        pool = self.pools[kind]
        key = pool[self.pool_rr[kind]]
        self.pool_rr[kind] = (self.pool_rr[kind] + 1) % len(pool)
        i = key
        prev = self.dma_use[i]
        if prev:
            if self.known[eng].get(key, 0) < prev:
                self.known[eng][key] = prev
                self.streams[eng].append(("wait", key, prev))
        self.dma_use[i] = prev + inc
        tok = (key, prev + inc, None)
        self.streams[eng].append(("op", fn, (key, inc)))
        self._mark(tok, reads, writes)
        if is_output:
            self.out_tokens.append(tok)
        return tok

    def barrier(self):
        toks = [(e, self.cnt[e], e) for e in ENG if self.cnt[e]]
        toks += [(k, v, None) for k, v in self.dma_use.items() if v]
        for e in ENG:
            for t in toks:
                self._need(e, t)

    def finish(self, eng="sp"):
        for tok in self.out_tokens:
            self._need(eng, tok)

    def emit(self, stack):
        nc = self.nc
        keys = list(ENG) + [k for p in self.pools.values() for k in p]
        for k in keys:
            self.sems[k] = stack.enter_context(nc.semaphore("s_" + k))
        block = stack.enter_context(nc.Block())
        sems = self.sems

        def replay(stream):
            def run(e):
                for item in stream:
                    if item[0] == "wait":
                        e.wait_ge(sems[item[1]], item[2])
                    else:
                        ins = item[1](e)
                        k, amt = item[2]
                        ins.then_inc(sems[k], amt)
            return run

        if self.streams["sp"]:
            block.sync(replay(self.streams["sp"]))
        if self.streams["pe"]:
            block.tensor(replay(self.streams["pe"]))
        if self.streams["act"]:
            block.scalar(replay(self.streams["act"]))
        if self.streams["dve"]:
            block.vector(replay(self.streams["dve"]))
        if self.streams["pool"]:
            block.gpsimd(replay(self.streams["pool"]))


from contextlib import ExitStack
import numpy as np
import concourse.bass as bass
import concourse.mybir as mybir

F32 = mybir.dt.float32
BF16 = mybir.dt.bfloat16
AF = mybir.ActivationFunctionType
ALU = mybir.AluOpType

NT = 2048
TT = 512
NTT = NT // TT
KC = 8
HALO = 32
ALPHA = float((2 * 4) ** 0.25)
LN_EPS = 1e-5
WB = 256
CONVW = 31


class Ten:
    def __init__(self, ap, name, C, N, tw=TT):
        self.ap = ap
        self.C, self.N, self.tw = C, N, tw
        self.nt = (N + tw - 1) // tw
        self.bufs = [[Buf("%s_%d_%d" % (name, c, t)) for t in range(self.nt)] for c in range(C)]

    def b(self, c=None, t0=0, t1=None):
        t1 = self.N if t1 is None else t1
        cs = range(self.C) if c is None else ([c] if isinstance(c, int) else c)
        out = []
        for cc in cs:
            for t in range(t0 // self.tw, (t1 - 1) // self.tw + 1):
                out.append(self.bufs[cc][t])
        return out


class Ctx:
    def __init__(self, nc, stack, arena_words):
        self.nc = nc
        self.S = Sched(nc)
        self.stack = stack
        self.arena = stack.enter_context(nc.sbuf_tensor("arena", [128, arena_words], F32))
        self.off = 0
        self.ps = []
        for i in range(8):
            t = stack.enter_context(nc.psum_tensor("ps%d" % i, [128, 512], F32))
            self.ps.append((t, Buf("ps%d" % i)))
        self.ps_rr = 0
        self.rot = {}

    def alloc(self, words):
        o = self.off
        self.off += words
        assert self.off <= self.arena.shape[1], (self.off, self.arena.shape)
        return o

    def f32(self, off, C, N):
        return self.arena[:, off:off + C * N].rearrange("p (c t) -> p c t", c=C)

    def bf(self, off, C, N):
        assert (C * N) % 2 == 0
        return self.arena[:, off:off + C * N // 2].bitcast(BF16).rearrange("p (c t) -> p c t", c=C)

    def psum(self):
        t, b = self.ps[self.ps_rr]
        self.ps_rr = (self.ps_rr + 1) % 8
        return t, b

    def make_rot(self, name, n, words, mk):
        lst = []
        for i in range(n):
            o = self.alloc(words)
            lst.append((mk(o), Buf("%s%d" % (name, i))))
        self.rot[name] = [lst, 0]

    def get(self, name):
        r = self.rot[name]
        x = r[0][r[1]]
        r[1] = (r[1] + 1) % len(r[0])
        return x


def mm_group(S, out_ap, pairs, reads, writes):
    def fn(e):
        n = len(pairs)
        ins = None
        for i, (l, r) in enumerate(pairs):
            ins = e.matmul(out_ap, lhsT=l, rhs=r, start=(i == 0), stop=(i == n - 1))
        return ins
    return S.op("pe", fn, reads, writes)


def _wload(cx, W2d, row0, c0, nc_, kc):
    S = cx.S
    wt, wb = cx.get("w")
    src = W2d[row0:row0 + kc * 128, c0:c0 + nc_].rearrange("(c p) m -> p c m", p=128)
    S.dma("pool", lambda e, wt=wt, src=src, nc_=nc_: e.dma_start(out=wt[:, 0:kc, 0:nc_], in_=src), writes=[wb])
    return wt, wb, nc_


def linear(cx, xin, W2d, row0, col0, ncols, evac, tiles=None, kc=KC, first=None, nextw=None):
    S = cx.S
    if tiles is None:
        tiles = [(t * TT, TT) for t in range(xin.N // TT)]
    nblk = (ncols + WB - 1) // WB

    def load(bi):
        c0 = col0 + bi * WB
        nc_ = min(WB, col0 + ncols - c0)
        return _wload(cx, W2d, row0, c0, nc_, kc)

    nxt = first if first is not None else load(0)
    pre = None
    for bi in range(nblk):
        wt, wb, nc_ = nxt
        if bi + 1 < nblk:
            nxt = load(bi + 1)
        elif nextw is not None:
            pre = _wload(cx, *nextw)
        for mi in range(nc_ // 128):
            mglob = (bi * WB) // 128 + mi
            for (t0, tn) in tiles:
                pt, pb = cx.psum()
                pairs = [(wt[:, k, mi * 128:(mi + 1) * 128], xin.ap[:, k, t0:t0 + tn]) for k in range(kc)]
                mm_group(S, pt[:, 0:tn], pairs, reads=[wb] + xin.b(None, t0, t0 + tn), writes=[pb])
                evac(mglob, (t0, tn), pt[:, 0:tn], pb)
    return pre


def layernorm(cx, z, gcol, bcol, outs, func=None, tok0=0, ntok=None):
    S = cx.S
    func = AF.Identity if func is None else func
    ntok = z.N if ntok is None else ntok
    ones = cx.ones
    def do_tile(tt):
        t0 = tok0 + tt * TT
        p1, pb1 = cx.psum()
        p2, pb2 = cx.psum()
        zbs, zss = [], []
        for c in range(KC):
            zb, zbb = cx.get("lnb")
            zs, zsb = cx.get("lnb")
            S.op("pool", lambda e, zb=zb, c=c: e.tensor_copy(out=zb, in_=z.ap[:, c, t0:t0 + TT]), reads=z.b(c, t0, t0 + TT), writes=[zbb])
            S.op("act", lambda e, zs=zs, c=c: e.activation(out=zs, in_=z.ap[:, c, t0:t0 + TT], func=AF.Square), reads=z.b(c, t0, t0 + TT), writes=[zsb])
            S.op("pe", lambda e, zb=zb, c=c: e.matmul(p1[:], lhsT=ones, rhs=zb, start=(c == 0), stop=(c == KC - 1)), reads=[zbb, cx.onesb], writes=[pb1])
            S.op("pe", lambda e, zs=zs, c=c: e.matmul(p2[:], lhsT=ones, rhs=zs, start=(c == 0), stop=(c == KC - 1)), reads=[zsb, cx.onesb], writes=[pb2])
        nmean, mb = cx.get("st")
        msq, qb = cx.get("st")
        rstd, rb = cx.get("st")
        S.op("act", lambda e: e.activation(out=nmean, in_=p1[:], func=AF.Identity, scale=-1.0 / 1024), reads=[pb1], writes=[mb])
        S.op("dve", lambda e: e.scalar_tensor_tensor(out=msq, in0=nmean, scalar=-1.0, in1=nmean, op0=ALU.mult, op1=ALU.mult), reads=[mb], writes=[qb])
        S.op("dve", lambda e: e.scalar_tensor_tensor(out=msq, in0=p2[:], scalar=1.0 / 1024, in1=msq, op0=ALU.mult, op1=ALU.add), reads=[pb2, qb], writes=[qb])
        S.op("act", lambda e: e.activation(out=rstd, in_=msq, func=AF.Sqrt, bias=cx.epsc, scale=1.0), reads=[qb], writes=[rb])
        S.op("dve", lambda e: e.reciprocal(out=rstd, in_=rstd), reads=[rb], writes=[rb])
        for c in range(KC):
            t1, tb1 = cx.get("s")
            S.op("dve", lambda e, c=c, t1=t1: e.tensor_tensor(out=t1, in0=z.ap[:, c, t0:t0 + TT], in1=nmean, op=ALU.add), reads=z.b(c, t0, t0 + TT) + [mb], writes=[tb1])
            S.op("pool", lambda e, t1=t1: e.tensor_tensor(out=t1, in0=t1, in1=rstd, op=ALU.mult), reads=[tb1, rb], writes=[tb1])
            for (o, ooff) in outs:
                oo = t0 - tok0 + ooff
                S.op("act", lambda e, c=c, t1=t1, o=o, oo=oo: e.activation(out=o.ap[:, c, oo:oo + TT], in_=t1, func=func, scale=gcol(c), bias=bcol(c)),
                     reads=[tb1, cx.vb], writes=o.b(c, oo, oo + TT))

    for tt in range(ntok // TT):
        do_tile(tt)


def emit_tok(cx, kind, D, final=False):
    conv = kind in ("conv0", "conv3")
    stop = None
    nc = cx.nc
    d_x, d_mem, d_vec = D["x"], D["memT"], D["vecs"]
    d_win = D.get("w_in")
    d_wout, d_wq, d_wkv, d_wo, d_w1, d_w2, d_y = D["w_out"], D["w_q"], D["w_kv"], D["w_o"], D["w1"], D["w2"], D["y"]
    if True:
        cx.off = 0
        cx.rot = {}
        S = cx.S
        vt = cx.vt
        vb = Buf("vecs")
        cx.vb = vb
        S.dma("sp", lambda e: e.dma_start(out=vt[:, 0:NVEC], in_=d_vec), writes=[vb])

        def V(name, c=0):
            i = VIDX[name] + c
            return vt[:, i:i + 1]

        o_x32 = cx.alloc(KC * NT)
        o_xbf = cx.alloc(KC * (NT + HALO) // 2)
        o_r1 = cx.alloc(KC * NT // 2)
        o_a = cx.alloc(KC * NT // 2)
        o_g = cx.alloc(2 * (NT // 2 + HALO))
        x32 = Ten(cx.f32(o_x32, KC, NT), "x32", KC, NT)
        xbf_all = cx.bf(o_xbf, KC, NT + HALO)
        xbf = Ten(xbf_all[:, :, HALO:HALO + NT], "xbf", KC, NT)
        A = Ten(cx.bf(o_a, KC, NT), "A", KC, NT)
        Q = Ten(cx.bf(o_r1, KC, NT), "Q", KC, NT)
        cx.make_rot("w", 2, KC * WB // 2, lambda o: cx.bf(o, KC, WB))
        cx.make_rot("lnb", 4, TT // 2, lambda o: cx.arena[:, o:o + TT // 2].bitcast(BF16))
        cx.make_rot("s", 4, TT, lambda o: cx.arena[:, o:o + TT])
        cx.make_rot("st", 3, TT, lambda o: cx.arena[:, o:o + TT])
        o_ones = cx.alloc(64)
        cx.ones = cx.arena[:, o_ones:o_ones + 64].bitcast(BF16)
        onesb = Buf("ones")
        cx.onesb = onesb
        S.op("dve", lambda e: e.memset(cx.ones, 1.0), writes=[onesb])
        o_eps = cx.alloc(1)
        cx.epsc = cx.arena[:, o_eps:o_eps + 1]
        S.op("dve", lambda e: e.memset(cx.epsc, LN_EPS), writes=[onesb])
        o_kt = cx.alloc(KC * 256 // 2)
        o_v = cx.alloc(2 * 1024 // 2)
        KT = Ten(cx.bf(o_kt, KC, 256), "KT", KC, 256, tw=256)
        Vt = Ten(cx.bf(o_v, 2, 1024), "V", 2, 1024, tw=1024)
        memT = Ten(cx.bf(o_g, KC, 256), "memT", KC, 256, tw=256)
        PT = [(cx.arena[:, o_g + 1024 + i * 512: o_g + 1024 + (i + 1) * 512].bitcast(BF16).rearrange("p (c t) -> p c t", c=2), Buf("PT%d" % i)) for i in range(2)]

        xoff = HALO if kind == "conv0" else 0
        for c in range(KC):
            S.dma("sp", lambda e, c=c: e.dma_start(out=x32.ap[:, c, :], in_=d_x[c * 128:(c + 1) * 128, xoff:xoff + NT]), reads=D.get("xB", []), writes=x32.b(c))

        ydst = [d_y] + list(D.get("y_extra", []))
        yB = D.get("yB", [Buf("ydram")])

        def finish_out():
            for c in range(KC):
                for yd in ydst:
                    S.dma("sp", lambda e, c=c, yd=yd: e.dma_start(out=yd[c * 128:(c + 1) * 128, :], in_=x32.ap[:, c, :]), reads=x32.b(c), writes=yB, is_output=final)
            if "tail_out" in D:
                S.dma("sp", lambda e: e.dma_start(out=D["tail_out"].rearrange("(c p) t -> p c t", p=128), in_=x32.ap[:, :, NT - HALO:NT]), reads=x32.b(None, NT - HALO, NT), writes=yB)

        if stop == "load":
            finish_out()
            return nc

        def resid_prep(bias_name):
            for c in range(KC):
                if bias_name is None:
                    S.op("act", lambda e, c=c: e.activation(out=x32.ap[:, c, :], in_=x32.ap[:, c, :], func=AF.Identity, scale=ALPHA), reads=x32.b(c), writes=x32.b(c))
                else:
                    S.op("act", lambda e, c=c: e.activation(out=x32.ap[:, c, :], in_=x32.ap[:, c, :], func=AF.Identity, scale=ALPHA, bias=V(bias_name, c)), reads=x32.b(c) + [vb], writes=x32.b(c))

        def evac_acc(m, tl, p, pb):
            t0, tn = tl
            S.op("dve", lambda e: e.tensor_tensor(out=x32.ap[:, m, t0:t0 + tn], in0=x32.ap[:, m, t0:t0 + tn], in1=p, op=ALU.add), reads=[pb] + x32.b(m, t0, t0 + tn), writes=x32.b(m, t0, t0 + tn))

        flip = [0]

        def evac_copy_to(dst):
            def ev(m, tl, p, pb):
                t0, tn = tl
                flip[0] ^= 1
                if flip[0]:
                    S.op("act", lambda e: e.activation(out=dst.ap[:, m, t0:t0 + tn], in_=p, func=AF.Identity), reads=[pb], writes=dst.b(m, t0, t0 + tn))
                else:
                    S.op("dve", lambda e: e.tensor_copy(out=dst.ap[:, m, t0:t0 + tn], in_=p), reads=[pb], writes=dst.b(m, t0, t0 + tn))
            return ev

        if conv:
            if kind == "conv0":
                S.dma("pool", lambda e: e.dma_start(out=xbf_all, in_=d_x.rearrange("(c p) t -> p c t", p=128)), reads=D.get("xB", []), writes=xbf.b())
            else:
                S.dma("pool", lambda e: e.dma_start(out=xbf.ap, in_=d_x.rearrange("(c p) t -> p c t", p=128)), reads=D.get("xB", []), writes=xbf.b())
                hg = D["halo_g"]
                hacc, haB = cx.get("st")
                hv = hacc[:, 0:KC * HALO].rearrange("p (c t) -> p c t", c=KC)
                for j in range(4):
                    ht, htB = cx.get("s")
                    htv = ht[:, 0:KC * HALO].rearrange("p (c t) -> p c t", c=KC)
                    S.dma("sp", lambda e, htv=htv, j=j: e.dma_start(out=htv, in_=hg[j * 1024:(j + 1) * 1024, :].rearrange("(c p) t -> p c t", p=128)), reads=D["halo_gB"], writes=[htB])
                    if j == 0:
                        S.op("dve", lambda e, htv=htv: e.tensor_scalar(out=hv, in0=htv, scalar1=V("hsel", 0), scalar2=None, op0=ALU.mult), reads=[htB, vb], writes=[haB])
                    else:
                        S.op("dve", lambda e, htv=htv, j=j: e.scalar_tensor_tensor(out=hv, in0=htv, scalar=V("hsel", j), in1=hv, op0=ALU.mult, op1=ALU.add), reads=[htB, vb, haB], writes=[haB])
                S.op("dve", lambda e: e.tensor_copy(out=xbf_all[:, :, 0:HALO], in_=hv), reads=[haB], writes=xbf.b(None, 0, 1))
            NH = NT // 2
            CO = Ten(cx.f32(o_r1, KC, NH), "CO", KC, NH)
            glus = [(cx.arena[:, o_g + i * (NH + HALO): o_g + (i + 1) * (NH + HALO)], Buf("glu%d" % i)) for i in range(2)]
            gi = 0
            for half in range(2):
                base = half * NH
                tiles = [(base, HALO), (base + HALO, TT), (base + HALO + TT, TT)]
                for m in range(KC):
                    glu, gb = glus[gi]
                    gi ^= 1
                    pend = {}

                    def ev(mg, tl, p, pb, glu=glu, gb=gb, m=m, pend=pend, base=base, half=half):
                        t0, tn = tl
                        which = mg % 2
                        if which == 0:
                            pend[t0] = (p, pb)
                            return
                        pv, pvb = pend.pop(t0)
                        sg, sb = cx.get("s")
                        S.op("act", lambda e: e.activation(out=sg[:, 0:tn], in_=p, func=AF.Sigmoid, bias=V("b_gate", m)), reads=[pb, vb], writes=[sb])
                        g0 = t0 - base
                        S.op("dve", lambda e: e.scalar_tensor_tensor(out=glu[:, g0:g0 + tn], in0=pv, scalar=V("b_val", m), in1=sg[:, 0:tn], op0=ALU.add, op1=ALU.mult), reads=[pvb, sb, vb], writes=[gb])
                        if g0 == 0 and half == 0:
                            S.op("dve", lambda e: e.tensor_scalar(out=glu[:, 0:HALO], in0=glu[:, 0:HALO], scalar1=V("halo_mask"), scalar2=None, op0=ALU.mult), reads=[gb, vb], writes=[gb])

                    class XH:
                        ap = xbf_all
                        N = NT + HALO

                        @staticmethod
                        def b(c, t0, t1):
                            return xbf.b(None, max(0, t0 - HALO), max(1, t1 - HALO))
                    linear(cx, XH, d_win, 0, m * 256, 256, ev, tiles=tiles)
                    if m % 2 == 0:
                        for j in range(CONVW):
                            src = glu[:, 2 + j: 2 + j + NH]
                            if j == 0:
                                S.op("dve", lambda e, src=src, m=m: e.tensor_scalar(out=CO.ap[:, m, :], in0=src, scalar1=V("dw", m), scalar2=V("dw_b", m), op0=ALU.mult, op1=ALU.add), reads=[gb, vb], writes=CO.b(m))
                            else:
                                S.op("dve", lambda e, src=src, m=m, j=j: e.scalar_tensor_tensor(out=CO.ap[:, m, :], in0=src, scalar=V("dw", j * KC + m), in1=CO.ap[:, m, :], op0=ALU.mult, op1=ALU.add), reads=[gb, vb] + CO.b(m), writes=CO.b(m))
                    else:
                        for sub in range(NH // TT):
                            s0 = sub * TT
                            for j in range(CONVW):
                                src = glu[:, 2 + j + s0: 2 + j + s0 + TT]
                                dst = CO.ap[:, m, s0:s0 + TT]
                                if j == 0:
                                    S.op("act", lambda e, src=src, dst=dst, m=m: e.activation(out=dst, in_=src, func=AF.Identity, scale=V("dw", m), bias=V("dw_b", m)), reads=[gb, vb], writes=CO.b(m, s0, s0 + TT))
                                else:
                                    tmp, tmb = cx.get("s")
                                    S.op("act", lambda e, src=src, tmp=tmp, m=m, j=j: e.activation(out=tmp, in_=src, func=AF.Identity, scale=V("dw", j * KC + m)), reads=[gb, vb], writes=[tmb])
                                    S.op("pool", lambda e, dst=dst, tmp=tmp: e.tensor_tensor(out=dst, in0=dst, in1=tmp, op=ALU.add), reads=[tmb] + CO.b(m, s0, s0 + TT), writes=CO.b(m, s0, s0 + TT))
                layernorm(cx, CO, lambda c: V("a_ln_g", c), lambda c: V("a_ln_b", c), [(A, half * NH)], func=AF.Silu)
            S.barrier()
            if stop == "conv":
                finish_out()
                return nc
            resid_prep("b_out")
        else:
            S.dma("pool", lambda e: e.dma_start(out=xbf.ap, in_=d_x.rearrange("(c p) t -> p c t", p=128)), reads=D.get("xB", []), writes=xbf.b())
            og = D["h_g"]
            for tt in range(NTT):
                for c in range(KC):
                    acc, accB = cx.get("st")
                    for j in range(4):
                        ht, htB = cx.get("s")
                        row0 = (2 * j + tt // 2) * 1024 + (c // 2) * 256 + (c % 2) * 128
                        off = (tt % 2) * TT
                        S.dma("sp", lambda e, ht=ht, row0=row0, off=off: e.dma_start(out=ht, in_=og[row0:row0 + 128, off:off + TT]), reads=D["h_gB"], writes=[htB])
                        if j == 0:
                            S.op("dve", lambda e, ht=ht, acc=acc: e.tensor_scalar(out=acc, in0=ht, scalar1=V("hsel", 0), scalar2=None, op0=ALU.mult), reads=[htB, vb], writes=[accB])
                        elif j < 3:
                            S.op("dve", lambda e, ht=ht, acc=acc, j=j: e.scalar_tensor_tensor(out=acc, in0=ht, scalar=V("hsel", j), in1=acc, op0=ALU.mult, op1=ALU.add), reads=[htB, vb, accB], writes=[accB])
                        else:
                            S.op("dve", lambda e, ht=ht, acc=acc, c=c, tt=tt: e.scalar_tensor_tensor(out=A.ap[:, c, tt * TT:(tt + 1) * TT], in0=ht, scalar=V("hsel", 3), in1=acc, op0=ALU.mult, op1=ALU.add),
                                 reads=[htB, vb, accB], writes=A.b(c, tt * TT, (tt + 1) * TT))
            resid_prep(None)
        if stop == "prep":
            finish_out()
            return nc
        pre_kv = linear(cx, A, d_wout, 0, 0, 1024, evac_acc, nextw=(d_wkv, 0, 0, WB, KC))
        if stop == "wout":
            finish_out()
            return nc
        if stop == "ln0a":
            layernorm(cx, x32, lambda c: V("ln_g0", c), lambda c: V("ln_b0", c), [(x32, 0)])
        elif stop == "ln0b":
            layernorm(cx, x32, lambda c: V("ln_g0", c), lambda c: V("ln_b0", c), [(xbf, 0)])
        elif stop == "ln0c":
            layernorm(cx, x32, lambda c: V("ln_g0", c), lambda c: V("ln_b0", c), [(x32, 0)], ntok=512)
        else:
            layernorm(cx, x32, lambda c: V("ln_g0", c), lambda c: V("ln_b0", c), [(x32, 0), (xbf, 0)])
        S.barrier()
        if stop in ("ln0", "ln0a", "ln0b", "ln0c"):
            finish_out()
            return nc

        S.dma("pool", lambda e: e.dma_start(out=memT.ap, in_=d_mem.rearrange("(c p) t -> p c t", p=128)), writes=memT.b())
        linear(cx, memT, d_wkv, 0, 0, 1024, evac_copy_to(KT), tiles=[(0, 256)], first=pre_kv)
        for eb in range(4):
            wt, wb = cx.get("w")
            src = d_wkv[:, 1024 + eb * WB: 1024 + (eb + 1) * WB].rearrange("(c p) m -> p c m", p=128)
            S.dma("pool", lambda e, wt=wt, src=src: e.dma_start(out=wt, in_=src), writes=[wb])
            for mch in range(2):
                pt, pb = cx.psum()
                pairs = [(memT.ap[:, k, mch * 128:(mch + 1) * 128], wt[:, k, :]) for k in range(KC)]
                mm_group(S, pt[:, 0:WB], pairs, reads=[wb] + memT.b(), writes=[pb])
                S.op("act", lambda e, pt=pt, mch=mch, eb=eb: e.activation(out=Vt.ap[:, mch, eb * WB:(eb + 1) * WB], in_=pt[:, 0:WB], func=AF.Identity), reads=[pb], writes=Vt.b(mch))
        linear(cx, xbf, d_wq, 0, 0, 1024, evac_copy_to(Q))
        resid_prep(None)
        O = A
        pi = 0
        for tt in range(NTT):
            t0 = tt * TT
            for h in range(4):
                P, Pb = PT[pi]
                pi ^= 1
                for mch in range(2):
                    pl, plb = cx.psum()
                    pairs = [(KT.ap[:, 2 * h + ec, mch * 128:(mch + 1) * 128], Q.ap[:, 2 * h + ec, t0:t0 + TT]) for ec in range(2)]
                    mm_group(S, pl[:], pairs, reads=KT.b([2 * h, 2 * h + 1]) + Q.b([2 * h, 2 * h + 1], t0, t0 + TT), writes=[plb])
                    S.op("act", lambda e, pl=pl, P=P, mch=mch: e.activation(out=P[:, mch, :], in_=pl[:], func=AF.Exp, scale=1.0 / 16), reads=[plb], writes=[Pb])
                pd, pdb = cx.psum()
                mm_group(S, pd[:], [(cx.ones, P[:, 0, :]), (cx.ones, P[:, 1, :])], reads=[Pb, onesb], writes=[pdb])
                rd, rdb = cx.get("s")
                S.op("dve", lambda e, rd=rd, pd=pd: e.reciprocal(out=rd, in_=pd[:]), reads=[pdb], writes=[rdb])
                for ec in range(2):
                    po, pob = cx.psum()
                    ch = 2 * h + ec
                    pairs = [(Vt.ap[:, mch, ch * 128:(ch + 1) * 128], P[:, mch, :]) for mch in range(2)]
                    mm_group(S, po[:], pairs, reads=Vt.b() + [Pb], writes=[pob])
                    S.op("dve", lambda e, po=po, rd=rd, ch=ch, t0=t0: e.tensor_tensor(out=O.ap[:, ch, t0:t0 + TT], in0=po[:], in1=rd, op=ALU.mult), reads=[pob, rdb], writes=O.b(ch, t0, t0 + TT))
        pre_mlp = linear(cx, O, d_wo, 0, 0, 1024, evac_acc, nextw=(d_w1, 0, 0, WB, KC))
        layernorm(cx, x32, lambda c: V("ln_g1", c), lambda c: V("ln_b1", c), [(x32, 0), (xbf, 0)])
        S.barrier()
        if stop == "xattn":
            finish_out()
            return nc

        resid_prep(None)
        H = A

        def evac_relu2(m, tl, p, pb):
            t0, tn = tl
            r, rb_ = cx.get("s")
            S.op("act", lambda e: e.activation(out=r[:, 0:tn], in_=p, func=AF.Relu), reads=[pb], writes=[rb_])
            S.op("pool", lambda e: e.tensor_tensor(out=H.ap[:, m, t0:t0 + tn], in0=r[:, 0:tn], in1=r[:, 0:tn], op=ALU.mult), reads=[rb_], writes=H.b(m, t0, t0 + tn))

        seq = []
        for fq in range(4):
            seq += [(xbf, d_w1, 0, fq * 1024, evac_relu2), (H, d_w2, fq * 1024, 0, evac_acc)]
        pre = pre_mlp
        for i, (xin_, W_, r0_, c0_, ev_) in enumerate(seq):
            nw = None
            if i + 1 < len(seq):
                nw = (seq[i + 1][1], seq[i + 1][2], seq[i + 1][3], WB, KC)
            pre = linear(cx, xin_, W_, r0_, c0_, 1024, ev_, first=pre, nextw=nw)
        layernorm(cx, x32, lambda c: V("ln_g2", c), lambda c: V("ln_b2", c), [(x32, 0)])
        finish_out()
        return yB


VNAMES = [("ln_g0", 8), ("ln_b0", 8), ("ln_g1", 8), ("ln_b1", 8), ("ln_g2", 8), ("ln_b2", 8),
          ("b_val", 8), ("b_gate", 8), ("dw", 31 * 8), ("dw_b", 8), ("a_ln_g", 8), ("a_ln_b", 8), ("b_out", 8), ("halo_mask", 1), ("hsel", 4)]
VIDX = {}
_o = 0
for _n, _k in VNAMES:
    VIDX[_n] = _o
    _o += _k
NVEC = _o


def fcols(v):
    return np.ascontiguousarray(np.asarray(v, np.float32).reshape(-1, 128).T)


def make_vecs(d):
    out = np.zeros((128, NVEC), np.float32)
    for n, k in VNAMES:
        if n in d:
            out[:, VIDX[n]:VIDX[n] + k] = d[n]
    return out


from contextlib import ExitStack
import math
import numpy as np
import concourse.bass as bass
import concourse.mybir as mybir

SEQ = 8192
SB = 2048
NSB = SEQ // SB
GROUPS = ((128, 1), (512, 4), (2048, 16))
DILS = (1, 4, 16)


def t5_bucket(n):
    n = max(int(n), 0)
    if n < 16:
        return n
    v = np.float32(16) + (np.log(np.float32(n) / np.float32(16)) / np.float32(math.log(2048 / 16)) * np.float32(16)).astype(np.int32)
    return int(min(int(v), 31))


def oh_tables():
    keys, mats = [], []
    for g, d in enumerate(DILS):
        for chunk in range(2):
            q = np.arange(128)[None, :]
            k = np.arange(128)[:, None]
            rel = q + 128 - k if chunk == 0 else q - k
            valid = (rel >= 0) & (rel <= 128)
            bk = np.vectorize(t5_bucket)(np.maximum(rel, 0) * d)
            for b in range(32):
                m = (valid & (bk == b)).astype(np.float32)
                if m.any():
                    keys.append((g, chunk, b))
                    mats.append(m)
    return keys, np.stack(mats)


OH_KEYS, OH_MATS = oh_tables()


def emit_attn(cx, D):
    nc = cx.nc
    st = cx.stack
    d_x, d_w, d_rb, d_oh, d_o = D["x_g"], D["wqkv"], D["relb"], D["oh"], D["o"]
    if True:
        cx.off = 0
        cx.rot = {}
        S = cx.S
        A = cx.arena
        cx.D = D

        def bf2(off, n):
            return A[:, off:off + n // 2].bitcast(BF16)

        o_erb = cx.alloc(512)
        erb = A[:, o_erb:o_erb + 512]
        erbB = Buf("erb")
        S.dma("sp", lambda e: e.dma_start(out=erb, in_=d_rb), writes=[erbB])
        S.op("act", lambda e: e.activation(out=erb, in_=erb, func=AF.Exp), reads=[erbB], writes=[erbB])
        o_tab = cx.alloc(12 * 256 + 128)
        tabB = Buf("tab")
        S.op("dve", lambda e: e.memset(A[:, o_tab:o_tab + 12 * 256 + 128], 0.0), writes=[tabB])
        zero_tab = A[:, o_tab + 12 * 256: o_tab + 12 * 256 + 128]

        def tab(g, hh, chunk):
            o = o_tab + (g * 4 + hh) * 256 + chunk * 128
            return A[:, o:o + 128]
        cx.make_rot("oh", 3, 128, lambda o: A[:, o:o + 128])
        cx.hbase = None
        return_nc = nc
        cx.tab = tab
        cx.zero_tab = zero_tab
        cx.erb = erb
        cx.erbB = erbB
        cx.tabB = tabB
        return _attn_body(cx, nc, st, d_x, d_w, d_oh, d_o, bf2)


def _attn_body(cx, nc, st, d_x, d_w, d_oh, d_o, bf2):
    S = cx.S
    A = cx.arena
    tab, zero_tab, erb, erbB, tabB = cx.tab, cx.zero_tab, cx.erb, cx.erbB, cx.tabB
    for i, (g, chunk, b) in enumerate(OH_KEYS):
        oh, ohB = cx.get("oh")
        S.dma("sp", lambda e, oh=oh, i=i: e.dma_start(out=oh, in_=d_oh[i]), writes=[ohB])
        for hh in range(4):
            t = tab(g, hh, chunk)
            S.op("dve", lambda e, t=t, oh=oh, b=b, hh=hh: e.scalar_tensor_tensor(out=t, in0=oh, scalar=erb[:, b * 4 + hh: b * 4 + hh + 1], in1=t, op0=ALU.mult, op1=ALU.add),
                 reads=[ohB, erbB, tabB], writes=[tabB])

    o_x = cx.alloc(KC * SB // 2)
    xbf = bf2(o_x, KC * SB).rearrange("p (c t) -> p c t", c=KC)
    xB = Buf("x")
    o_w = cx.alloc(KC * 1152 // 2)
    wt = bf2(o_w, KC * 1152).rearrange("p (c m) -> p c m", c=KC)
    wB = Buf("w")
    o_q = cx.alloc(3 * SB // 2)
    QT = bf2(o_q, 3 * SB).rearrange("p (g t) -> p g t", g=3)
    qB = [Buf("q%d" % g) for g in range(3)]
    o_k = cx.alloc(2 * 3 * SB // 2)
    KT = bf2(o_k, 2 * 3 * SB).rearrange("p (s g t) -> p s g t", s=2, g=3)
    kB = [[Buf("k%d_%d" % (s, g)) for g in range(3)] for s in range(2)]
    o_v = cx.alloc(2 * 3 * 16 * 128 // 2)
    VT = bf2(o_v, 2 * 3 * 16 * 128).rearrange("p (s g j e) -> p s g j e", s=2, g=3, j=16)
    vB = [[Buf("v%d_%d" % (s, g)) for g in range(3)] for s in range(2)]
    o_ao = cx.alloc(2 * SB)
    accO = A[:, o_ao:o_ao + 2 * SB].rearrange("p (h t) -> p h t", h=2)
    o_ad = cx.alloc(2 * SB)
    accD = A[:, o_ad:o_ad + 2 * SB].rearrange("p (h t) -> p h t", h=2)
    aB = [Buf("acc%d" % h) for h in range(2)]
    cx.make_rot("E", 2, 512, lambda o: A[:, o:o + 512])
    cx.make_rot("P", 2, 256, lambda o: A[:, o:o + 256].bitcast(BF16))
    o_ones = cx.alloc(32)
    ones64 = A[:, o_ones:o_ones + 32].bitcast(BF16)
    onesB = Buf("ones")
    S.op("dve", lambda e: e.memset(ones64, 1.0), writes=[onesB])

    def tok_view(ap2d, d):
        return ap2d.rearrange("p (n r) -> p r n", r=d)

    flip = [0]
    oB = [Buf("o_dram")]
    for ps_ in range(2):
        S.dma("pool", lambda e, ps_=ps_: e.dma_start(out=wt, in_=d_w[:, ps_ * 1152:(ps_ + 1) * 1152].rearrange("(c p) m -> p c m", p=128)), writes=[wB])
        for sb in range(NSB):
            slot = sb % 2
            S.dma("pool", lambda e, sb=sb: e.dma_start(out=xbf, in_=d_x.rearrange("(c r p) t -> r p c t", c=8, r=4, p=128)[sb]), reads=cx.D["xgB"], writes=[xB])
            for g in range(3):
                d = DILS[g]
                for t in range(2):
                    for tt in range(SB // 512):
                        pt, pb = cx.psum()
                        blk = g * 3 + t
                        pairs = [(wt[:, k, blk * 128:(blk + 1) * 128], xbf[:, k, tt * 512:(tt + 1) * 512]) for k in range(KC)]
                        mm_group(S, pt[:], pairs, reads=[wB, xB], writes=[pb])
                        dst = QT[:, g, tt * 512:(tt + 1) * 512] if t == 0 else KT[:, slot, g, tt * 512:(tt + 1) * 512]
                        dB = qB[g] if t == 0 else kB[slot][g]
                        flip[0] ^= 1
                        if flip[0]:
                            S.op("act", lambda e, dst=dst, pt=pt: e.activation(out=dst, in_=pt[:], func=AF.Identity), reads=[pb], writes=[dB])
                        else:
                            S.op("dve", lambda e, dst=dst, pt=pt: e.tensor_copy(out=dst, in_=pt[:]), reads=[pb], writes=[dB])
                nper = 16 // d
                for j4 in range(4):
                    pt, pb = cx.psum()
                    for jj in range(4):
                        j = j4 * 4 + jj
                        r, n_ = j // nper, j % nper
                        blk = g * 3 + 2
                        pairs = []
                        for k in range(KC):
                            xv = xbf[:, k, :].rearrange("p (n r) -> p r n", r=d)[:, r, n_ * 128:(n_ + 1) * 128]
                            pairs.append((xv, wt[:, k, blk * 128:(blk + 1) * 128]))
                        mm_group(S, pt[:, jj * 128:(jj + 1) * 128], pairs, reads=[wB, xB], writes=[pb])
                    dst = VT[:, slot, g, j4 * 4:(j4 + 1) * 4, :]
                    S.op("act", lambda e, dst=dst, pt=pt: e.activation(out=dst, in_=pt[:].rearrange("p (j e) -> p j e", j=4), func=AF.Identity), reads=[pb], writes=[vB[slot][g]])
            for g in range(3):
                d = DILS[g]
                nper = 16 // d
                for hh in range(2):
                    hp = slice(hh * 64, (hh + 1) * 64)
                    habs = ps_ * 2 + hh
                    qv = QT[hp, g, :].rearrange("p (n r) -> p r n", r=d)
                    kcur = KT[hp, slot, g, :].rearrange("p (n r) -> p r n", r=d)
                    kprv = KT[hp, 1 - slot, g, :].rearrange("p (n r) -> p r n", r=d)
                    aOv = accO[0:64, hh, :].rearrange("p (n r) -> p r n", r=d)
                    aDv = accD[0:64, hh, :].rearrange("p (n r) -> p r n", r=d)
                    for u2 in range(8):
                        pl, plb = cx.psum()
                        units = []
                        for uu in range(2):
                            j = u2 * 2 + uu
                            r, n_ = j // nper, j % nper
                            qs = qv[:, r, n_ * 128:(n_ + 1) * 128]
                            first = (sb == 0 and n_ == 0)
                            if n_ > 0:
                                kp = kcur[:, r, (n_ - 1) * 128:n_ * 128]
                                vp = VT[:, slot, g, j - 1, hp]
                                rp = [kB[slot][g], vB[slot][g]]
                            elif first:
                                kp = kcur[:, r, 0:128]
                                vp = VT[:, slot, g, j, hp]
                                rp = [kB[slot][g], vB[slot][g]]
                            else:
                                kp = kprv[:, r, (nper - 1) * 128:nper * 128]
                                vp = VT[:, 1 - slot, g, r * nper + nper - 1, hp]
                                rp = [kB[1 - slot][g], vB[1 - slot][g]]
                            kc_ = kcur[:, r, n_ * 128:(n_ + 1) * 128]
                            vc = VT[:, slot, g, j, hp]
                            units.append((r, n_, first, vp, vc, rp))
                            mm_group(S, pl[:, uu * 256:uu * 256 + 128], [(kp, qs)], reads=[qB[g], kB[slot][g]] + rp, writes=[plb])
                            mm_group(S, pl[:, uu * 256 + 128:uu * 256 + 256], [(kc_, qs)], reads=[qB[g], kB[slot][g]], writes=[plb])
                        E, EB = cx.get("E")
                        S.op("act", lambda e, E=E, pl=pl: e.activation(out=E, in_=pl[:], func=AF.Exp, scale=0.125), reads=[plb], writes=[EB])
                        po, pob = cx.psum()
                        for uu in range(2):
                            r, n_, first, vp, vc, rp = units[uu]
                            P, PB = cx.get("P")
                            t0_ = zero_tab if first else cx.tab(g, habs, 0)
                            S.op("dve", lambda e, P=P, E=E, uu=uu, t0_=t0_: e.tensor_tensor(out=P[:, 0:128], in0=E[:, uu * 256:uu * 256 + 128], in1=t0_, op=ALU.mult), reads=[EB, tabB], writes=[PB])
                            t1_ = cx.tab(g, habs, 1)
                            S.op("dve", lambda e, P=P, E=E, uu=uu, t1_=t1_: e.tensor_tensor(out=P[:, 128:256], in0=E[:, uu * 256 + 128:uu * 256 + 256], in1=t1_, op=ALU.mult), reads=[EB, tabB], writes=[PB])
                            mm_group(S, po[0:64, uu * 128:(uu + 1) * 128], [(vp, P[:, 0:128]), (vc, P[:, 128:256])], reads=[PB, vB[slot][g]] + rp, writes=[pob])
                            mm_group(S, po[0:64, 256 + uu * 128:256 + (uu + 1) * 128], [(ones64, P[:, 0:128]), (ones64, P[:, 128:256])], reads=[PB, onesB], writes=[pob])
                        for uu in range(2):
                            r, n_ = units[uu][0], units[uu][1]
                            oO = aOv[:, r, n_ * 128:(n_ + 1) * 128]
                            oD = aDv[:, r, n_ * 128:(n_ + 1) * 128]
                            so = po[0:64, uu * 128:(uu + 1) * 128]
                            sd = po[0:64, 256 + uu * 128:256 + (uu + 1) * 128]
                            if g == 0:
                                S.op("act", lambda e, oO=oO, so=so: e.activation(out=oO, in_=so, func=AF.Identity), reads=[pob], writes=[aB[hh]])
                                S.op("act", lambda e, oD=oD, sd=sd: e.activation(out=oD, in_=sd, func=AF.Identity), reads=[pob], writes=[aB[hh]])
                            else:
                                S.op("dve", lambda e, oO=oO, so=so: e.tensor_tensor(out=oO, in0=oO, in1=so, op=ALU.add), reads=[pob, aB[hh]], writes=[aB[hh]])
                                S.op("dve", lambda e, oD=oD, sd=sd: e.tensor_tensor(out=oD, in0=oD, in1=sd, op=ALU.add), reads=[pob, aB[hh]], writes=[aB[hh]])
            for hh in range(2):
                S.op("dve", lambda e, hh=hh: e.reciprocal(out=accD[0:64, hh, :], in_=accD[0:64, hh, :]), reads=[aB[hh]], writes=[aB[hh]])
                S.op("dve", lambda e, hh=hh: e.tensor_tensor(out=accO[0:64, hh, :], in0=accO[0:64, hh, :], in1=accD[0:64, hh, :], op=ALU.mult), reads=[aB[hh]], writes=[aB[hh]])
                row = (ps_ * 2 + hh) * 64
                for kk in range(2):
                    r0 = (2 * sb + kk) * 256 + row
                    S.dma("sp", lambda e, hh=hh, r0=r0, kk=kk: e.dma_start(out=d_o[r0:r0 + 64, :], in_=accO[0:64, hh, kk * 1024:(kk + 1) * 1024]), reads=[aB[hh]], writes=oB)
    return oB


from contextlib import ExitStack
import math
import numpy as np
import concourse.bass as bass
import concourse.mybir as mybir

T = 8192
TW = 256
NTW = T // TW
SBLK = 128
NA = 6
NCONST = 1408
STOPB = [99]
GN_EPS = 64e-5
RV = {"mu": 0, "k_a": 48, "a0": 50, "lnx_g": 52, "lnx_b": 54, "r_k": 56}
NRV = 58


def emit_rwkv(cx, D):
    nc = cx.nc
    st = cx.stack
    d_xg, d_wrkv, d_wl1, d_wl2, d_vec, d_bc, d_blk, d_o, d_scr = D["x_g"], D["wrkv"], D["wl1"], D["wl2"], D["vecs"], D["bc"], D["blk"], D["o"], D["scr"]
    xgB = D["xgB"]
    NTC = 2048
    if True:
        cx.off = 0
        cx.rot = {}
        S = cx.S
        S.same_engine_sync = True
        A = cx.arena
        oB = [Buf("o_dram")]

        def f32v(off, n):
            return A[:, off:off + n]

        def bfv(off, n):
            return A[:, off:off + n // 2].bitcast(BF16)

        vt = cx.vt
        vb = Buf("vecs")
        S.dma("sp", lambda e: e.dma_start(out=vt[:, 0:NRV], in_=d_vec), writes=[vb])

        def V(name, c=0):
            i = RV[name] + c
            return vt[:, i:i + 1]

        o_y = cx.alloc(2 * T)
        YT = f32v(o_y, 2 * T).rearrange("p (s t) -> p s t", s=2)
        yB = [Buf("Y%d" % i) for i in range(NTW)]
        o_wrkv = cx.alloc(KC * 768 // 2)
        wrkv = bfv(o_wrkv, KC * 768).rearrange("p (c m) -> p c m", c=KC)
        o_wl1 = cx.alloc(KC * 256 // 2)
        wl1 = bfv(o_wl1, KC * 256).rearrange("p (c m) -> p c m", c=KC)
        o_wrkv2 = cx.alloc(KC * 768 // 2)
        wrkv2 = bfv(o_wrkv2, KC * 768).rearrange("p (c m) -> p c m", c=KC)
        o_wl12 = cx.alloc(KC * 256 // 2)
        wl12 = bfv(o_wl12, KC * 256).rearrange("p (c m) -> p c m", c=KC)
        o_wl2 = cx.alloc(768 // 2)
        wl2 = bfv(o_wl2, 768)
        o_blk = cx.alloc(64)
        blk = bfv(o_blk, 128)
        o_bc = cx.alloc(4 * 256)
        bc = f32v(o_bc, 1024).rearrange("p (a m) -> p a m", a=4)
        wB = Buf("weights")
        S.dma("pool", lambda e: e.dma_start(out=wrkv, in_=d_wrkv.rearrange("(c p) m -> p c m", p=128)), writes=[wB])
        S.dma("pool", lambda e: e.dma_start(out=wl1, in_=d_wl1.rearrange("(c p) m -> p c m", p=128)), writes=[wB])
        S.dma("pool", lambda e: e.dma_start(out=wl2, in_=d_wl2), writes=[wB])
        S.dma("pool", lambda e: e.dma_start(out=blk, in_=d_blk), writes=[wB])
        S.dma("sp", lambda e: e.dma_start(out=bc, in_=d_bc), writes=[wB])
        w2B = Buf("weights2")
        for k in range(KC):
            for (wt_, wt2_, c0, c1, s_) in ((wrkv, wrkv2, 0, 256, 0), (wrkv, wrkv2, 256, 512, 1), (wrkv, wrkv2, 512, 768, 2), (wl1, wl12, 0, 64, 3), (wl1, wl12, 64, 128, 4), (wl1, wl12, 128, 256, 5)):
                S.op("act", lambda e, k=k, wt_=wt_, wt2_=wt2_, c0=c0, c1=c1, s_=s_: e.activation(out=wt2_[:, k, c0:c1], in_=wt_[:, k, c0:c1], func=AF.Identity, scale=V("mu", s_ * 8 + k)), reads=[wB, vb], writes=[w2B])
        o_eps = cx.alloc(2)
        epsc = f32v(o_eps, 1)
        S.op("dve", lambda e: e.memset(epsc, GN_EPS), writes=[wB])
        REG = 25800
        o_reg = cx.alloc(REG)
        ro = [o_reg]

        def ralloc(n):
            o = ro[0]
            ro[0] += n
            assert ro[0] <= o_reg + REG, (ro[0] - o_reg)
            return o
        xs = f32v(ralloc(KC * (TW + 1)), KC * (TW + 1)).rearrange("p (c t) -> p c t", c=KC)
        xsB = Buf("xs")
        xxb = bfv(ralloc(KC * TW // 2), KC * TW).rearrange("p (c t) -> p c t", c=KC)
        xb = bfv(ralloc(KC * TW // 2), KC * TW).rearrange("p (c t) -> p c t", c=KC)
        xxB = Buf("xx")
        _tmo = [ralloc(4 * NA * 64) for _ in range(2)]
        TMt = [f32v(o, 4 * NA * 64).rearrange("p (h a e) -> p h a e", h=4, a=NA) for o in _tmo]
        TMf = [f32v(o, 4 * NA * 64) for o in _tmo]
        tmB = [Buf("TM%d" % i) for i in range(2)]
        tq = [(f32v(ralloc(256), 256), Buf("tq%d" % i)) for i in range(8)]
        fm = [(f32v(ralloc(TW), TW), Buf("fm%d" % i)) for i in range(6)]
        tqi = [0]
        fmi = [0]

        def gtq():
            x = tq[tqi[0]]
            tqi[0] = (tqi[0] + 1) % len(tq)
            return x

        def gfm():
            x = fm[fmi[0]]
            fmi[0] = (fmi[0] + 1) % len(fm)
            return x

        def load_x_and_mix(tt, which):
            t0 = tt * TW
            rk, l0 = t0 // NTC, t0 % NTC

            def xsrc(rk_, a, b_):
                return d_xg.rearrange("(c r p) t -> r p c t", c=8, r=4, p=128)[rk_][:, :, a:b_]
            if l0 > 0:
                S.dma("sp", lambda e: e.dma_start(out=xs[:, :, :], in_=xsrc(rk, l0 - 1, l0 + TW)), reads=xgB, writes=[xsB])
            else:
                if rk == 0:
                    S.op("dve", lambda e: e.memset(xs[:, :, 0:1], 0.0), writes=[xsB])
                else:
                    S.dma("sp", lambda e: e.dma_start(out=xs[:, :, 0:1], in_=xsrc(rk - 1, NTC - 1, NTC), allow_slow_non_contiguous=True), reads=xgB, writes=[xsB])
                S.dma("sp", lambda e: e.dma_start(out=xs[:, :, 1:TW + 1], in_=xsrc(rk, 0, TW)), reads=xgB, writes=[xsB])
            S.op("dve", lambda e: e.scalar_tensor_tensor(out=xxb, in0=xs[:, :, 1:TW + 1], scalar=-1.0, in1=xs[:, :, 0:TW], op0=ALU.mult, op1=ALU.add), reads=[xsB], writes=[xxB])
            S.op("act", lambda e: e.activation(out=xb, in_=xs[:, :, 1:TW + 1], func=AF.Identity), reads=[xsB], writes=[xxB])

        W2 = {id(wrkv): wrkv2, id(wl1): wl12}

        def fm_proj(wtile, col0, ncol):
            pt, pb = cx.psum()
            w2 = W2[id(wtile)]
            pairs = [(wtile[:, k, col0:col0 + ncol], xb[:, k, :]) for k in range(KC)] + [(w2[:, k, col0:col0 + ncol], xxb[:, k, :]) for k in range(KC)]
            mm_group(S, pt[0:ncol, 0:TW], pairs, reads=[wB, w2B, xxB], writes=[pb])
            return pt, pb

        def tm_pairs(tb0, c0, c1):
            return [(xb[:, k, tb0:tb0 + 128], wrkv[:, k, c0:c1]) for k in range(KC)] + [(xxb[:, k, tb0:tb0 + 128], wrkv2[:, k, c0:c1]) for k in range(KC)]

        def phaseA(tt):
            t0 = tt * TW
            load_x_and_mix(tt, None)
            pw, pwb = fm_proj(wl1, 0, 64)
            tw_, twB = gfm()
            tw_b = tw_[0:64, 0:TW // 2].bitcast(BF16)
            S.op("act", lambda e: e.activation(out=tw_b, in_=pw[0:64, 0:TW], func=AF.Tanh), reads=[pwb], writes=[twB])
            pa, pab = fm_proj(wl1, 64, 64)
            ta_, taB = gfm()
            ta_b = ta_[0:64, 0:TW // 2].bitcast(BF16)
            S.op("act", lambda e: e.activation(out=ta_b, in_=pa[0:64, 0:TW], func=AF.Identity), reads=[pab], writes=[taB])
            for bi in range(TW // 128):
                tb0 = bi * 128
                tm = TMt[(tt * (TW // 128) + bi) % 2]
                tmf = TMf[(tt * (TW // 128) + bi) % 2]
                tmb = tmB[(tt * (TW // 128) + bi) % 2]
                pr, prb = cx.psum()
                mm_group(S, pr[:, 0:256], tm_pairs(tb0, 0, 256), reads=[wB, w2B, xxB], writes=[prb])
                pk, pkb = cx.psum()
                mm_group(S, pk[:, 0:256], tm_pairs(tb0, 256, 512), reads=[wB, w2B, xxB], writes=[pkb])
                p2w, p2wb = cx.psum()
                mm_group(S, p2w[:, 0:256], [(tw_b[:, tb0:tb0 + 128], wl2[0:64, 0:256])], reads=[wB, twB], writes=[p2wb])
                p2a, p2ab = cx.psum()
                mm_group(S, p2a[:, 0:256], [(ta_b[:, tb0:tb0 + 128], wl2[0:64, 256:512])], reads=[wB, taB], writes=[p2ab])

                def h3(ap):
                    return ap.rearrange("p (h e) -> p h e", h=4)
                S.op("act", lambda e, tm=tm, pr=pr: e.activation(out=tm[:, :, 4, :], in_=h3(pr[:, 0:256]), func=AF.Identity), reads=[prb], writes=[tmb])
                ksb, ksB = gtq()
                S.op("act", lambda e, ksb=ksb, pk=pk: e.activation(out=ksb, in_=pk[:, 0:256], func=AF.Identity), reads=[pkb], writes=[ksB])
                u, uB = gtq()
                S.op("dve", lambda e, u=u, p2w=p2w: e.tensor_tensor(out=u, in0=p2w[:, 0:256], in1=bc[:, 0, :], op=ALU.add), reads=[p2wb, wB], writes=[uB])
                S.op("act", lambda e, u=u: e.activation(out=u, in_=u, func=AF.Sigmoid), reads=[uB], writes=[uB])
                S.op("act", lambda e, u=u, tm=tm: e.activation(out=tm[:, :, 1, :], in_=h3(u), func=AF.Identity, scale=-math.exp(-0.5)), reads=[uB], writes=[tmb])
                pvt, pvtb = cx.psum()
                mm_group(S, pvt[:, 0:256], tm_pairs(tb0, 512, 768), reads=[wB, w2B, xxB], writes=[pvtb])
                S.op("act", lambda e, tm=tm, pvt=pvt: e.activation(out=tm[:, :, 5, :], in_=h3(pvt[:, 0:256]), func=AF.Identity), reads=[pvtb], writes=[tmb])
                a_, aB = gtq()
                S.op("dve", lambda e, a_=a_, p2a=p2a: e.tensor_tensor(out=a_, in0=p2a[:, 0:256], in1=bc[:, 1, :], op=ALU.add), reads=[p2ab, wB], writes=[aB])
                S.op("act", lambda e, a_=a_: e.activation(out=a_, in_=a_, func=AF.Sigmoid), reads=[aB], writes=[aB])
                kr, krB = gtq()
                S.op("dve", lambda e, kr=kr, ksb=ksb: e.tensor_tensor(out=kr, in0=ksb, in1=bc[:, 2, :], op=ALU.mult), reads=[ksB, wB], writes=[krB])
                sq, sqB = gtq()
                S.op("dve", lambda e, sq=sq, kr=kr: e.tensor_tensor(out=sq, in0=kr, in1=kr, op=ALU.mult), reads=[krB], writes=[sqB])
                ss, ssB = gtq()
                S.op("dve", lambda e, ss=ss, sq=sq: e.tensor_reduce(out=ss[:, 0:4], in_=h3(sq), axis=mybir.AxisListType.X, op=ALU.add), reads=[sqB], writes=[ssB])
                S.op("act", lambda e, ss=ss: e.activation(out=ss[:, 0:4], in_=ss[:, 0:4], func=AF.Sqrt), reads=[ssB], writes=[ssB])
                S.op("dve", lambda e, ss=ss: e.tensor_scalar_max(out=ss[:, 0:4], in0=ss[:, 0:4], scalar1=1e-12), reads=[ssB], writes=[ssB])
                S.op("dve", lambda e, ss=ss: e.reciprocal(out=ss[:, 0:4], in_=ss[:, 0:4]), reads=[ssB], writes=[ssB])
                for h in range(4):
                    S.op("dve", lambda e, h=h, tm=tm, kr=kr, ss=ss: e.tensor_scalar(out=tm[:, h, 0, :], in0=kr[:, h * 64:(h + 1) * 64], scalar1=ss[:, h:h + 1], scalar2=None, op0=ALU.mult), reads=[krB, ssB], writes=[tmb])
                tk, tkB = gtq()
                S.op("dve", lambda e, tk=tk, a_=a_: e.scalar_tensor_tensor(out=tk, in0=a_, scalar=-1.0, in1=bc[:, 3, :], op0=ALU.add, op1=ALU.mult), reads=[aB, wB], writes=[tkB])
                S.op("dve", lambda e, tk=tk, ksb=ksb, tm=tm: e.scalar_tensor_tensor(out=tm[:, :, 3, :], in0=h3(tk), scalar=1.0, in1=h3(ksb), op0=ALU.add, op1=ALU.mult), reads=[tkB, ksB], writes=[tmb])
                S.op("dve", lambda e, tm=tm, a_=a_: e.tensor_tensor(out=tm[:, :, 2, :], in0=tm[:, :, 0, :], in1=h3(a_), op=ALU.mult), reads=[tmb, aB], writes=[tmb])
                S.op("dve", lambda e, tm=tm: e.tensor_scalar(out=tm[:, :, 0, :], in0=tm[:, :, 0, :], scalar1=-1.0, scalar2=None, op0=ALU.mult), reads=[tmb], writes=[tmb])
                S.dma("sp", lambda e, tmf=tmf, tb0=tb0: e.dma_start(out=d_scr[t0 + tb0:t0 + tb0 + 128].rearrange("t h m -> t (h m)"), in_=tmf), reads=[tmb], writes=[scrB[(t0 + tb0) // SBLK + i] for i in range(128 // SBLK)])

        scrB = [Buf("scr%d" % i) for i in range(T // SBLK)]
        for tt in range(NTW):
            phaseA(tt)
        S.barrier()
        ro[0] = o_reg

        d_rc = D["rc"]
        cB = Buf("rconst")
        c_bf = bfv(ralloc(192), 384)
        triU, triL, ident = c_bf[:, 0:128], c_bf[:, 128:256], c_bf[:, 256:384]
        if STOPB[0] >= 0 or STOPB[0] == -2:
            S.dma("pool", lambda e: e.dma_start(out=c_bf, in_=d_rc[:, 0:384]), writes=[cB])
        M4 = f32v(ralloc(512), 512)
        SL4 = f32v(ralloc(512), 512)
        if STOPB[0] >= 0 or STOPB[0] == -2:
            S.dma("sp", lambda e: e.dma_start(out=M4, in_=d_rc[:, 384:896]), writes=[cB])
            S.dma("sp", lambda e: e.dma_start(out=SL4, in_=d_rc[:, 896:1408]), writes=[cB])
        onec = bfv(ralloc(4), 8)
        S.op("dve", lambda e: e.memset(onec, 1.0), writes=[cB])
        def mkbufs():
            TMi1 = (f32v(ralloc(4 * NA * 64), 4 * NA * 64), Buf("tmi"))
            LW = bfv(ralloc(256), 512).rearrange("p (s m) -> p s m", s=2)
            lwB = Buf("LW")
            ft = [f32v(ralloc(256), 256) for _ in range(5)]
            ftB = [Buf("ft%d" % i) for i in range(5)]
            Rt, Bt, Kt, Bh, Kh = [bfv(ralloc(128), 256) for _ in range(5)]
            tmB_ = Buf("tmprod")
            _o = ralloc(256)
            Vpf = f32v(_o, 256)
            Vp = bfv(_o, 512).rearrange("p (h m) -> p h m", h=4)
            vpB = Buf("Vp")
            FMt = bfv(ralloc(1024), 2048).rearrange("p (h a t) -> p h a t", h=4, a=4)
            fmB = Buf("FMt")
            AM = bfv(ralloc(1024), 2048).rearrange("p (h a t) -> p h a t", h=4, a=4)
            amB = Buf("AM")
            NPt = [[bfv(ralloc(256), 512).rearrange("p (h t) -> p h t", h=4) for _ in range(2)] for _ in range(2)]
            npB = [[Buf("NP%d%d" % (i, j)) for j in range(2)] for i in range(2)]
            _o = ralloc(768)
            X32f = f32v(_o, 768)
            X32 = f32v(_o, 768).rearrange("p (h m) -> p h m", h=4)
            _o = ralloc(384)
            Xbff = f32v(_o, 384)
            Xbf = bfv(_o, 768).rearrange("p (h m) -> p h m", h=4)
            x32B, xbB = Buf("X32"), Buf("Xbf")
            RhT = bfv(ralloc(256), 512).rearrange("p (h t) -> p h t", h=4)
            rhB = Buf("RhT")
            GT = bfv(ralloc(128), 256).rearrange("p (h t) -> p h t", h=4)
            gtB = Buf("GT")
            Hh = f32v(ralloc(256), 256).rearrange("p (h t) -> p h t", h=4)
            hhB = Buf("H")
            gl = f32v(ralloc(32), 32)
            glB = Buf("gl")


            return dict(locals())
        BUFS = [mkbufs(), mkbufs()]
        _o = ralloc(256)
        S32f = f32v(_o, 256)
        S32 = f32v(_o, 256).rearrange("p (h t) -> p h t", h=4)
        s32B = Buf("S32")
        _o = ralloc(256)
        Spadf = f32v(_o, 256)
        Spad = bfv(_o, 512).rearrange("p (h m) -> p h m", h=4)
        spB = Buf("Spad")

        def q4(ap3):
            return ap3.rearrange("p (q f) m -> p q f m", q=2)

        def h3(ap):
            return ap.rearrange("p (h e) -> p h e", h=4)

        def acts(h):
            return slice(0, 128) if h % 2 == 0 else slice(64, 192)

        def ahc(h):
            return slice(0, 64) if h % 2 == 0 else slice(128, 192)

        def upad(h):
            return slice(64, 192) if h % 2 == 0 else slice(0, 128)

        def vd(h):
            return slice(0, 64) if h % 2 == 0 else slice(64, 128)

        def run_chunks():
            for B_ in BUFS:
                S.op("dve", lambda e, B_=B_: e.memset(B_["X32f"], 0.0), writes=[B_["x32B"]])
                S.op("dve", lambda e, B_=B_: e.memset(B_["Xbff"], 0.0), writes=[B_["xbB"]])
                S.op("dve", lambda e, B_=B_: e.memset(B_["Vpf"], 0.0), writes=[B_["vpB"]])
            S.op("dve", lambda e: e.memset(Spadf, 0.0), writes=[spB])
            S.op("dve", lambda e: e.memset(S32f, 0.0), writes=[s32B])
            for c0 in range(0, T // 128, 2):
                g0, g1 = chunk(c0), chunk(c0 + 1)
                d0 = d1 = False
                while not (d0 and d1):
                    if not d0:
                        d0 = next(g0) == "split"
                    if not d1:
                        d1 = next(g1) == "split"
                for _ in g0:
                    pass
                for _ in g1:
                    pass

        def chunk(c):
            B_ = BUFS[c % 2]
            tmf, tib = B_["TMi1"]
            LW, lwB, ft, ftB, Rt, Bt, Kt, Bh, Kh, tmB_, Vp, vpB, FMt, fmB, AM, amB, NPt, npB, X32, Xbf, x32B, xbB, RhT, rhB, GT, gtB, Hh, hhB, gl, glB = [B_[k] for k in (
                "LW", "lwB", "ft", "ftB", "Rt", "Bt", "Kt", "Bh", "Kh", "tmB_", "Vp", "vpB", "FMt", "fmB", "AM", "amB", "NPt", "npB", "X32", "Xbf", "x32B", "xbB", "RhT", "rhB", "GT", "gtB", "Hh", "hhB", "gl", "glB")]
            tm = tmf.rearrange("p (h a e) -> p h a e", h=4, a=NA)
            tmq = tmf.rearrange("p (q f a e) -> p q f a e", q=2, f=2, a=NA)
            S.dma("sp", lambda e: e.dma_start(out=tmf, in_=d_scr[c * 128:(c + 1) * 128].rearrange("t h m -> t (h m)")), reads=[scrB[c]], writes=[tib])

            def arr(i):
                return tm[:, :, i, :]
            S.op("dve", lambda e: e.tensor_copy(out=h3(LW[:, 0, :]), in_=arr(1)), reads=[tib], writes=[lwB])
            S.op("dve", lambda e: e.scalar_tensor_tensor(out=h3(LW[:, 1, :]), in0=h3(LW[:, 0, :]), scalar=-1.0, in1=arr(1), op0=ALU.mult, op1=ALU.add), reads=[tib, lwB], writes=[lwB])
            pCI, pCIb = cx.psum()
            mm_group(S, pCI[:, 0:256], [(triU, LW[:, 0, :]), (triU, LW[:, 1, :])], reads=[cB, lwB], writes=[pCIb])
            pCR, pCRb = cx.psum()
            mm_group(S, pCR[:, 0:256], [(triL, LW[:, 0, :]), (triL, LW[:, 1, :])], reads=[cB, lwB], writes=[pCRb])
            pGL, pGLb = cx.psum()
            for h in range(4):
                mm_group(S, pGL[0:64, h * 8:(h + 1) * 8], [(LW[:, 0, h * 64:(h + 1) * 64], onec), (LW[:, 1, h * 64:(h + 1) * 64], onec)], reads=[cB, lwB], writes=[pGLb])
            e_ex, e_in, e_ninv, e_rev, tmpx = ft
            S.op("act", lambda e: e.activation(out=e_in, in_=pCI[:, 0:256], func=AF.Exp), reads=[pCIb], writes=[ftB[1]])
            S.op("act", lambda e: e.activation(out=e_ninv, in_=pCI[:, 0:256], func=AF.Exp, scale=-1.0), reads=[pCIb], writes=[ftB[2]])
            S.op("act", lambda e: e.activation(out=e_rev, in_=pCR[:, 0:256], func=AF.Exp), reads=[pCRb], writes=[ftB[3]])
            S.op("act", lambda e: e.activation(out=h3(tmpx), in_=arr(1), func=AF.Exp, scale=-1.0), reads=[tib], writes=[ftB[4]])
            S.op("dve", lambda e: e.tensor_tensor(out=e_ex, in0=tmpx, in1=e_in, op=ALU.mult), reads=[ftB[4], ftB[1]], writes=[ftB[0]])
            S.op("act", lambda e: e.activation(out=gl[0:64, 0:32], in_=pGL[0:64, 0:32], func=AF.Exp), reads=[pGLb], writes=[glB])
            yield 1
            for hf in range(2):
                ac = slice(0, 64) if hf == 0 else slice(128, 192)
                S.op("dve", lambda e, hf=hf, ac=ac: e.tensor_tensor(out=q4(X32)[:, :, hf, ac], in0=tmq[:, :, hf, 0, :], in1=e_ex.rearrange("p (q f e) -> p q f e", q=2, f=2)[:, :, hf, :], op=ALU.mult),
                     reads=[tib, ftB[0]], writes=[x32B])
                S.op("act", lambda e, hf=hf, ac=ac: e.activation(out=q4(Xbf)[:, :, hf, ac], in_=q4(X32)[:, :, hf, ac], func=AF.Identity), reads=[x32B], writes=[xbB])
                S.op("act", lambda e, hf=hf: e.activation(out=q4(Vp)[:, :, hf, hf * 64:(hf + 1) * 64], in_=tmq[:, :, hf, 5, :], func=AF.Identity), reads=[tib], writes=[vpB])
            for (dst, ai, ee, eb) in ((Rt, 4, e_in, 1), (Bt, 2, e_ninv, 2), (Kt, 3, e_ninv, 2), (Bh, 2, e_rev, 3), (Kh, 3, e_rev, 3)):
                S.op("dve", lambda e, dst=dst, ai=ai, ee=ee: e.tensor_tensor(out=h3(dst), in0=arr(ai), in1=h3(ee), op=ALU.mult), reads=[tib, ftB[eb]], writes=[tmB_])
            yield 2
            for q in range(2):
                pT, pTb = cx.psum()
                pT16 = pT[:, :].bitcast(BF16)

                def tr(e, q=q, pT16=pT16):
                    ins = None
                    for hh in range(2):
                        h = 2 * q + hh
                        srcs = (Xbf[:, h, ahc(h)], Rt[:, h * 64:(h + 1) * 64], Bt[:, h * 64:(h + 1) * 64], Kt[:, h * 64:(h + 1) * 64])
                        for ai, src in enumerate(srcs):
                            o0 = (hh * 4 + ai) * 128
                            ins = e.transpose(pT16[0:64, o0:o0 + 128], src, ident)
                    return ins
                S.op("pe", tr, reads=[xbB, tmB_, cB], writes=[pTb])
                S.op("act", lambda e, q=q, pT16=pT16: e.activation(out=FMt[0:64, 2 * q:2 * q + 2, :, :], in_=pT16[0:64, 0:1024].rearrange("p (h a t) -> p h a t", h=2, a=4), func=AF.Identity), reads=[pTb], writes=[fmB])
            yield 3
            pP, pPb = cx.psum()
            for h in range(4):
                pA, pAb = cx.psum()
                rhs = FMt[0:64, h, 0:2, :].rearrange("p a t -> p (a t)")
                mm_group(S, pA[:, 0:256], [(FMt[0:64, h, 2, :], rhs)], reads=[fmB], writes=[pAb])
                mm_group(S, pA[:, 256:512], [(FMt[0:64, h, 3, :], rhs)], reads=[fmB], writes=[pAb])
                S.op("dve", lambda e, h=h, pA=pA: e.tensor_tensor(out=AM[:, h, :, :].rearrange("p a t -> p (a t)"), in0=pA[:, :], in1=M4, op=ALU.mult), reads=[pAb, cB], writes=[amB])
                mm_group(S, pP[:, h * 128:(h + 1) * 128], [(FMt[0:64, h, 0, :], FMt[0:64, h, 2, :])], reads=[fmB], writes=[pPb])
            S.op("dve", lambda e: e.tensor_tensor(out=NPt[0][1].rearrange("p h t -> p (h t)"), in0=pP[:, :], in1=SL4, op=ALU.mult), reads=[pPb, cB], writes=[npB[0][1]])
            yield 4
            pW, pWb = cx.psum()
            for h in range(4):
                mm_group(S, pW[:, h * 64:(h + 1) * 64], [(AM[:, h, 2, :], Vp[:, h, vd(h)])], reads=[amB, vpB], writes=[pWb])
            S.op("act", lambda e: e.activation(out=X32[:, :, 64:128], in_=h3(pW[:, 0:256]), func=AF.Identity), reads=[pWb], writes=[x32B])
            S.op("dve", lambda e: e.tensor_copy(out=Xbf[:, :, 64:128], in_=X32[:, :, 64:128]), reads=[x32B], writes=[xbB])
            yield 5
            for lev in range(7):
                if lev == 0:
                    Nc = [AM[:, h, 0, :] for h in range(4)]
                    Pc = [NPt[0][1][:, h, :] for h in range(4)]
                    nB, pB_ = amB, npB[0][1]
                    wset = 1
                else:
                    cs_ = lev % 2
                    Nc = [NPt[cs_][0][:, h, :] for h in range(4)]
                    Pc = [NPt[cs_][1][:, h, :] for h in range(4)]
                    nB, pB_ = npB[cs_][0], npB[cs_][1]
                    wset = (lev + 1) % 2
                pX, pXb = cx.psum()
                for h in range(4):
                    mm_group(S, pX[:, h * 128:(h + 1) * 128], [(Nc[h], Xbf[:, h, acts(h)])], reads=[nB, xbB], writes=[pXb])
                pXq = pX[:, :].rearrange("p (q f t) -> p q f t", q=2, f=2)
                for hf in range(2):
                    ac = slice(0, 128) if hf == 0 else slice(64, 192)
                    S.op("dve", lambda e, hf=hf, ac=ac, pXq=pXq: e.tensor_tensor(out=q4(X32)[:, :, hf, ac], in0=q4(X32)[:, :, hf, ac], in1=pXq[:, :, hf, :], op=ALU.add), reads=[pXb, x32B], writes=[x32B])
                    S.op("act", lambda e, hf=hf, ac=ac: e.activation(out=q4(Xbf)[:, :, hf, ac], in_=q4(X32)[:, :, hf, ac], func=AF.Identity), reads=[x32B], writes=[xbB])
                if lev < 6:
                    pN, pNb = cx.psum()
                    pQ, pQb = cx.psum()
                    for h in range(4):
                        mm_group(S, pN[:, h * 128:(h + 1) * 128], [(Pc[h], Nc[h])], reads=[nB, pB_], writes=[pNb])
                        mm_group(S, pQ[:, h * 128:(h + 1) * 128], [(Nc[h], Pc[h])], reads=[nB, pB_], writes=[pQb])
                    S.op("act", lambda e, pN=pN, wset=wset: e.activation(out=NPt[wset][0].rearrange("p h t -> p (h t)"), in_=pN[:, :], func=AF.Identity), reads=[pNb], writes=[npB[wset][0]])
                    S.op("dve", lambda e, pQ=pQ, wset=wset: e.tensor_copy(out=NPt[wset][1].rearrange("p h t -> p (h t)"), in_=pQ[:, :]), reads=[pQb], writes=[npB[wset][1]])
                yield 50 + lev
            yield 6
            pR, pRb = cx.psum()
            for h in range(4):
                mm_group(S, pR[0:64, h * 128:(h + 1) * 128], [(Xbf[:, h, ahc(h)], AM[:, h, 1, :])], reads=[xbB, amB], writes=[pRb])
            S.op("dve", lambda e: e.tensor_tensor(out=RhT[0:64, :, :], in0=pR[0:64, :].rearrange("p (h t) -> p h t", h=4), in1=FMt[0:64, :, 1, :], op=ALU.add), reads=[pRb, fmB], writes=[rhB])
            pG, pGb = cx.psum()
            for h in range(4):
                mm_group(S, pG[0:64, h * 64:(h + 1) * 64], [(Xbf[:, h, ahc(h)], Bh[:, h * 64:(h + 1) * 64])], reads=[xbB, tmB_], writes=[pGb])
                mm_group(S, pG[0:64, 256 + h * 64:256 + (h + 1) * 64], [(Bh[:, h * 64:(h + 1) * 64], Xbf[:, h, 64:128]), (Kh[:, h * 64:(h + 1) * 64], Vp[:, h, vd(h)])], reads=[xbB, tmB_, vpB], writes=[pGb])
            S.op("act", lambda e: e.activation(out=GT[0:64, :, :], in_=pG[0:64, 0:256].rearrange("p (h t) -> p h t", h=4), func=AF.Identity), reads=[pGb], writes=[gtB])
            S.op("act", lambda e: e.activation(out=Hh[0:64, :, :], in_=pG[0:64, 256:512].rearrange("p (h t) -> p h t", h=4), func=AF.Identity), reads=[pGb], writes=[hhB])
            yield "split"
            for q in range(2):
                pY, pYb = cx.psum()
                pairs = []
                for hh in range(2):
                    h = 2 * q + hh
                    pairs += [(Spad[0:64, h, :], RhT[0:64, h, :]), (Xbf[:, h, upad(h)], AM[:, h, 1, :]), (Vp[:, h, :], AM[:, h, 3, :])]
                mm_group(S, pY[:, 0:128], pairs, reads=[spB, rhB, xbB, amB, vpB], writes=[pYb])
                S.op("act", lambda e, q=q, pY=pY: e.activation(out=YT[:, q, c * 128:(c + 1) * 128], in_=pY[:, 0:128], func=AF.Identity), reads=[pYb], writes=[yB[(c * 128) // TW]])
            pS, pSb = cx.psum()
            for h in range(4):
                mm_group(S, pS[0:64, h * 64:(h + 1) * 64], [(GT[0:64, h, :], Spad[0:64, h, vd(h)])], reads=[gtB, spB], writes=[pSb])
            for h in range(4):
                S.op("dve", lambda e, h=h, pS=pS: e.scalar_tensor_tensor(out=S32[0:64, h, :], in0=S32[0:64, h, :], scalar=gl[0:64, h * 8:h * 8 + 1], in1=pS[0:64, h * 64:(h + 1) * 64], op0=ALU.mult, op1=ALU.add),
                     reads=[pSb, glB, s32B], writes=[s32B])
            S.op("dve", lambda e: e.tensor_tensor(out=S32[0:64, :, :], in0=S32[0:64, :, :], in1=Hh[0:64, :, :], op=ALU.add), reads=[s32B, hhB], writes=[s32B])
            for hf in range(2):
                S.op("act", lambda e, hf=hf: e.activation(out=q4(Spad)[0:64, :, hf, hf * 64:(hf + 1) * 64], in_=q4(S32)[0:64, :, hf, :], func=AF.Identity), reads=[s32B], writes=[spB])

        def phaseC(s, tt):
            t0 = tt * TW
            if s == 0:
                load_x_and_mix(tt, (0, 1, 4, 5))
            pr, prb = fm_proj(wrkv, s * 128, 128)
            pk, pkb = fm_proj(wrkv, 256 + s * 128, 128)
            pa, pab = fm_proj(wl1, 64, 64)
            ta_, taB = gfm()
            ta_b = ta_[0:64, 0:TW // 2].bitcast(BF16)
            S.op("act", lambda e: e.activation(out=ta_b, in_=pa[0:64, 0:TW], func=AF.Identity), reads=[pab], writes=[taB])
            pa2, pa2b = cx.psum()
            mm_group(S, pa2[:, 0:TW], [(wl2[0:64, 256 + s * 128:256 + (s + 1) * 128], ta_b)], reads=[wB, taB], writes=[pa2b])
            aT, aTB = gfm()
            S.op("act", lambda e: e.activation(out=aT, in_=pa2[:, 0:TW], func=AF.Sigmoid, bias=V("a0", s)), reads=[pa2b, vb], writes=[aTB])
            S.op("dve", lambda e: e.tensor_scalar(out=aT, in0=aT, scalar1=-1.0, scalar2=V("k_a", s), op0=ALU.add, op1=ALU.mult), reads=[aTB, vb], writes=[aTB])
            S.op("dve", lambda e: e.scalar_tensor_tensor(out=aT, in0=aT, scalar=1.0, in1=pk[:, 0:TW], op0=ALU.add, op1=ALU.mult), reads=[aTB, pkb], writes=[aTB])
            rk_, rkB = gfm()
            rkb16 = rk_[:, 0:TW // 2].bitcast(BF16)
            S.op("dve", lambda e: e.scalar_tensor_tensor(out=rkb16, in0=pr[:, 0:TW], scalar=V("r_k", s), in1=aT, op0=ALU.mult, op1=ALU.mult), reads=[prb, aTB, vb], writes=[rkB])
            pbs, pbsb = cx.psum()
            mm_group(S, pbs[:, 0:TW], [(blk, rkb16)], reads=[wB, rkB], writes=[pbsb])
            bon, bonB = gfm()
            pv, pvb = fm_proj(wrkv, 512 + s * 128, 128)
            S.op("act", lambda e: e.activation(out=bon, in_=pv[:, 0:TW], func=AF.Identity), reads=[pvb], writes=[bonB])
            S.op("dve", lambda e: e.tensor_tensor(out=bon, in0=pbs[:, 0:TW], in1=bon, op=ALU.mult), reads=[pbsb, bonB], writes=[bonB])
            pg, pgb = fm_proj(wl1, 128, 128)
            gs_, gsB = gfm()
            gs16 = gs_[:, 0:TW // 2].bitcast(BF16)
            S.op("act", lambda e: e.activation(out=gs16, in_=pg[:, 0:TW], func=AF.Sigmoid), reads=[pgb], writes=[gsB])
            pg2, pg2b = cx.psum()
            mm_group(S, pg2[:, 0:TW], [(wl2[:, 512 + s * 128:512 + (s + 1) * 128], gs16)], reads=[wB, gsB], writes=[pg2b])
            y = YT[:, s, t0:t0 + TW]
            yb16_, y16B = gfm()
            yb16 = yb16_[:, 0:TW // 2].bitcast(BF16)
            ysq16 = yb16_[:, TW // 2:TW].bitcast(BF16)
            S.op("pool", lambda e: e.tensor_copy(out=yb16, in_=y), reads=[yB[tt]], writes=[y16B])
            S.op("act", lambda e: e.activation(out=ysq16, in_=y, func=AF.Square), reads=[yB[tt]], writes=[y16B])
            p1, p1b = cx.psum()
            mm_group(S, p1[:, 0:TW], [(blk, yb16)], reads=[wB, y16B], writes=[p1b])
            p2, p2b = cx.psum()
            mm_group(S, p2[:, 0:TW], [(blk, ysq16)], reads=[wB, y16B], writes=[p2b])
            nm, nmB = gfm()
            S.op("act", lambda e: e.activation(out=nm, in_=p1[:, 0:TW], func=AF.Identity, scale=-1.0 / 64), reads=[p1b], writes=[nmB])
            S.op("dve", lambda e: e.scalar_tensor_tensor(out=gs_, in0=nm, scalar=-1.0, in1=nm, op0=ALU.mult, op1=ALU.mult), reads=[nmB, gsB, pg2b], writes=[gsB])
            S.op("dve", lambda e: e.scalar_tensor_tensor(out=gs_, in0=p2[:, 0:TW], scalar=1.0 / 64, in1=gs_, op0=ALU.mult, op1=ALU.add), reads=[p2b, gsB], writes=[gsB])
            S.op("act", lambda e: e.activation(out=gs_, in_=gs_, func=AF.Sqrt, bias=epsc, scale=1.0), reads=[gsB, wB], writes=[gsB])
            S.op("dve", lambda e: e.reciprocal(out=gs_, in_=gs_), reads=[gsB], writes=[gsB])
            S.op("dve", lambda e: e.tensor_tensor(out=nm, in0=y, in1=nm, op=ALU.add), reads=[yB[tt], nmB], writes=[nmB])
            S.op("dve", lambda e: e.tensor_tensor(out=nm, in0=nm, in1=gs_, op=ALU.mult), reads=[nmB, gsB], writes=[nmB])
            S.op("act", lambda e: e.activation(out=nm, in_=nm, func=AF.Identity, scale=V("lnx_g", s), bias=V("lnx_b", s)), reads=[nmB, vb], writes=[nmB])
            S.op("dve", lambda e: e.tensor_tensor(out=nm, in0=nm, in1=bon, op=ALU.add), reads=[nmB, bonB], writes=[nmB])
            S.op("dve", lambda e: e.tensor_tensor(out=nm, in0=nm, in1=pg2[:, 0:TW], op=ALU.mult), reads=[nmB, pg2b], writes=[nmB])
            S.dma("sp", lambda e: e.dma_start(out=d_o[(t0 // 1024) * 256 + s * 128:(t0 // 1024) * 256 + (s + 1) * 128, t0 % 1024:t0 % 1024 + TW], in_=nm), reads=[nmB], writes=oB)

        run_chunks()
        S.barrier()
        for tt in range(NTW if STOPB[0] >= 99 else 1):
            for s in range(2):
                phaseC(s, tt)
        S.barrier()
        return oB


import numpy as np
from contextlib import ExitStack
import concourse.bass as bass
import concourse.mybir as mybir
from concourse.bass_utils import run_bass_kernel_spmd

GROUPS4 = [[0, 1, 2, 3], [4, 5, 6, 7]]
TOKW = ("w_out", [1024, 1024]), ("w_q", [1024, 1024]), ("w_kv", [1024, 2048]), ("w_o", [1024, 1024]), ("w1", [1024, 4096]), ("w2", [4096, 1024])


def build_fused(upto=9):
    nc = bass.Bass("TRN2", target_bir_lowering=False)

    def din(name, shape):
        return nc.dram_tensor(name, shape, F32, kind="ExternalInput").ap()

    def dint(name, shape):
        return nc.dram_tensor(name, shape, F32).ap()

    x0 = din("xT", [1024, NT + HALO])
    memT = din("memT", [1024, 256])
    L = []
    for i in range(4):
        d = {"memT": memT, "vecs": din("vecs%d" % i, [128, NVEC])}
        for n, shp in TOKW:
            d[n] = din("%s%d" % (n, i), shp)
        if i in (0, 3):
            d["w_in"] = din("w_in%d" % i, [1024, 2048])
        L.append(d)
    at = {"wqkv": din("wqkv", [1024, 2304]), "relb": din("relb", [128, 512]), "oh": din("oh", [len(OH_KEYS), 128, 128])}
    rw = {"wrkv": din("wrkv", [1024, 768]), "wl1": din("wl1", [1024, 256]), "wl2": din("wl2", [128, 768]), "vecs": din("rvecs", [128, NRV]),
          "bc": din("bc", [128, 4, 256]), "blk": din("blk", [128, 128]), "rc": din("rc", [128, NCONST])}
    yT = nc.dram_tensor("yT", [1024, NT], F32, kind="ExternalOutput").ap()
    xs = [dint("xs%d" % i, [1024, NT]) for i in range(1, 4)]
    xg = [dint("xg%d" % i, [4096, NT]) for i in range(1, 3)]
    osrc = [dint("os%d" % i, [8 * 256, 1024]) for i in range(1, 3)]
    og = [dint("og%d" % i, [8 * 4 * 256, 1024]) for i in range(1, 3)]
    hs = dint("hs", [1024, HALO])
    hg = dint("hg", [4096, HALO])
    rw["scr"] = dint("scr", [T, 4, NA * 64])

    with ExitStack() as st:
        cx = Ctx(nc, st, 51980)
        cx.vt = st.enter_context(nc.sbuf_tensor("vt", [128, NVEC], F32))
        S = cx.S

        def gather(src, dst, srcB, name, nchunk=8):
            S.barrier()
            rows = src.shape[0] // nchunk
            bs = []
            for k in range(nchunk):
                b = Buf("%s_%d" % (name, k))
                S.dma("pool", lambda e, k=k: e.collective_compute("AllGather", ALU.bypass, replica_groups=GROUPS4, ins=[src[k * rows:(k + 1) * rows, :].opt()],
                                                                    outs=[dst[k * 4 * rows:(k + 1) * 4 * rows, :].opt()]), reads=srcB, writes=[b], inc=1)
                bs.append(b)
            return bs

        d = dict(L[0]); d.update({"x": x0, "y": xs[0]})
        yB1 = emit_tok(cx, "conv0", d)
        g1 = gather(xs[0], xg[0], yB1, "xg1")

        def early(src_ap, srcB):
            S.dma("sp", lambda e: e.dma_start(out=yT.rearrange("(c p) t -> c p t", p=128) if len(src_ap.shape) == 3 else yT, in_=src_ap), reads=srcB, is_output=True)
            S.finish("sp")
            S.emit(st)
            return nc
        def early_o(ogt, srcB):
            for k in range(2):
                S.dma("sp", lambda e, k=k: e.dma_start(out=yT[:, k * 1024:(k + 1) * 1024], in_=ogt[k * 1024:(k + 1) * 1024, :]), reads=srcB, is_output=True)
            S.finish("sp")
            S.emit(st)
            return nc
        if upto == 1:
            return early(xg[0].rearrange("(c r p) t -> r c p t", c=8, r=4, p=128)[0], g1)
        d = dict(at); d.update({"x_g": xg[0], "xgB": g1, "o": osrc[0]})
        oB1 = emit_attn(cx, d)
        gB1 = gather(osrc[0], og[0], oB1, "og1")
        if upto == 2:
            return early_o(og[0], gB1)
        d = dict(L[1]); d.update({"x": xs[0], "xB": yB1, "h_g": og[0], "h_gB": gB1, "y": xs[1]})
        yB2 = emit_tok(cx, "tail", d)
        if upto == 3:
            return early(xs[1], yB2)
        g2 = gather(xs[1], xg[1], yB2, "xg2")
        d = dict(rw); d.update({"x_g": xg[1], "xgB": g2, "o": osrc[1]})
        oB2 = emit_rwkv(cx, d)
        gB2 = gather(osrc[1], og[1], oB2, "og2")
        if upto == 4:
            return early_o(og[1], gB2)
        d = dict(L[2]); d.update({"x": xs[1], "xB": yB2, "h_g": og[1], "h_gB": gB2, "y": xs[2], "tail_out": hs})
        yB3 = emit_tok(cx, "tail", d)
        if upto == 5:
            return early(xs[2], yB3)
        gh = gather(hs, hg, yB3, "hg", nchunk=1)
        d = dict(L[3]); d.update({"x": xs[2], "xB": yB3, "halo_g": hg, "halo_gB": gh, "y": yT})
        emit_tok(cx, "conv3", d, final=True)
        S.finish("sp")
        S.emit(st)
    return nc


_prog = {}


def rwkv_consts():
    i = np.arange(128)
    ui = (i[:, None] <= i[None, :]).astype(np.float32)
    su = (i[:, None] < i[None, :]).astype(np.float32)
    sl = (i[:, None] > i[None, :]).astype(np.float32)
    return np.ascontiguousarray(np.concatenate([ui, sl, np.eye(128, dtype=np.float32), su, ui, su, ui, sl, sl, sl, sl], axis=1))

UPTO = [9]


def fcols_(v):
    return fcols(v)


def kernel(**inputs):
    inp = {k: np.asarray(v) for k, v in inputs.items()}
    x = np.ascontiguousarray(inp["x"], dtype=np.float32)
    if "fused" not in _prog:
        _prog["fused"] = build_fused(UPTO[0])
    nc = _prog["fused"]
    blk = np.zeros((128, 128), np.float32); blk[:64, :64] = 1; blk[64:, 64:] = 1

    def interleave(win):
        return np.ascontiguousarray(np.concatenate([np.concatenate([win[:, m * 128:(m + 1) * 128], win[:, 1024 + m * 128:1024 + (m + 1) * 128]], axis=1) for m in range(8)], axis=1))
    win = {0: interleave(inp["a_w_in"][0]), 3: interleave(inp["a_w_in"][1])}
    mix_out = {0: inp["a_w_out"][0], 1: inp["b_w_out"][0], 2: inp["c_w_out"][0], 3: inp["a_w_out"][1]}
    Wqkv = inp["b_w_qkv"][0]
    Wr = inp["c_w_rkv"][0]
    wl1 = np.ascontiguousarray(np.concatenate([inp["c_w1"][0], inp["c_a1"][0], inp["c_g1"][0]], axis=1))
    in_maps = []
    for c in range(8):
        b, q = c // 4, c % 4
        h0 = 4 * q
        cs = slice(h0 * 64, h0 * 64 + 256)
        m = {}
        xt = np.zeros((1024, NT + HALO), np.float32)
        if q == 0:
            xt[:, HALO:] = x[b, 0:NT].T
        else:
            xt[:, :] = x[b, q * NT - HALO:(q + 1) * NT].T
        m["xT"] = xt
        m["memT"] = np.ascontiguousarray(inp["mem"][b].T)
        for i in range(4):
            d = {}
            for s in range(3):
                d["ln_g%d" % s] = fcols(inp["ln_g"][i, s]); d["ln_b%d" % s] = fcols(inp["ln_b"][i, s])
            if i in (0, 3):
                j = i // 3
                d["b_val"] = fcols(inp["a_b_in"][j][:1024]); d["b_gate"] = fcols(inp["a_b_in"][j][1024:])
                d["dw"] = np.concatenate([fcols(inp["a_dw"][j][t]) for t in range(31)], axis=1)
                d["dw_b"] = fcols(inp["a_dw_b"][j]); d["a_ln_g"] = fcols(inp["a_ln_g"][j]); d["a_ln_b"] = fcols(inp["a_ln_b"][j])
                d["b_out"] = fcols(inp["a_b_out"][j])
                d["halo_mask"] = np.full((128, 1), 0.0 if q == 0 else 1.0, np.float32)
                m["w_in%d" % i] = win[i]
            hsel = np.zeros((128, 4), np.float32)
            if i in (1, 2):
                hsel[:, q] = 1.0
            elif i == 3 and q > 0:
                hsel[:, q - 1] = 1.0
            d["hsel"] = hsel
            m["vecs%d" % i] = make_vecs(d)
            m["w_out%d" % i] = mix_out[i]
            m["w_q%d" % i] = inp["x_w_q"][i]; m["w_kv%d" % i] = inp["x_w_kv"][i]; m["w_o%d" % i] = inp["x_w_out"][i]
            m["w1%d" % i] = inp["m_w1"][i]; m["w2%d" % i] = inp["m_w2"][i]
        cols = []
        for p in range(2):
            for g in range(3):
                for t in range(3):
                    c0 = g * 3072 + t * 1024 + (h0 + 2 * p) * 64
                    cols.append(Wqkv[:, c0:c0 + 128])
        m["wqkv"] = np.ascontiguousarray(np.concatenate(cols, axis=1))
        relb = np.zeros((128, 512), np.float32)
        for bb in range(32):
            for hh in range(4):
                relb[:, bb * 4 + hh] = inp["rel_bias"][bb, h0 + hh]
        m["relb"] = relb
        m["oh"] = OH_MATS
        vec = np.zeros((128, NRV), np.float32)
        for s in range(6):
            vec[:, s * 8:(s + 1) * 8] = fcols(inp["c_mu"][0][s])
        vec[:, RV["k_a"]:RV["k_a"] + 2] = fcols(inp["c_k_a"][0][cs])
        vec[:, RV["a0"]:RV["a0"] + 2] = fcols(inp["c_a0"][0][cs])
        vec[:, RV["lnx_g"]:RV["lnx_g"] + 2] = fcols(inp["c_lnx_g"][0][cs])
        vec[:, RV["lnx_b"]:RV["lnx_b"] + 2] = fcols(inp["c_lnx_b"][0][cs])
        vec[:, RV["r_k"]:RV["r_k"] + 2] = fcols(inp["c_r_k"][0].reshape(-1)[cs])
        m["rvecs"] = vec
        m["bc"] = np.ascontiguousarray(np.stack([np.tile(inp[n][0][cs][None, :], (128, 1)) for n in ("c_w0", "c_a0", "c_k_k", "c_k_a")], axis=1).astype(np.float32))
        wl2 = np.zeros((128, 768), np.float32)
        wl2[:64, 0:256] = inp["c_w2"][0][:, cs]; wl2[:64, 256:512] = inp["c_a2"][0][:, cs]; wl2[:, 512:768] = inp["c_g2"][0][:, cs]
        m["wl2"] = wl2
        m["wrkv"] = np.ascontiguousarray(np.concatenate([Wr[0][:, cs], Wr[1][:, cs], Wr[2][:, cs]], axis=1))
        m["wl1"] = wl1
        m["blk"] = blk
        m["rc"] = rwkv_consts()
        in_maps.append(m)
    res = run_bass_kernel_spmd(nc, in_maps, core_ids=list(range(8)))
    out = np.empty_like(x)
    for c in range(8):
        b, q = c // 4, c % 4
        out[b, q * NT:(q + 1) * NT] = res.results[c]["yT"].T
    return np.ascontiguousarray(out, dtype=np.float32)
```
